# Optimizing a Trainium2 kernel written in Bass

```python
import math
import jax, jax.numpy as jnp
from jax import lax
import numpy as np

D_MODEL = 1024
BATCH = 32
SEQ = 2048
DEPTH = 1
DEC_BATCH = 32
DEC_SEQ = 32
PAST_LEN = 4096

CHUNK = 64
A_HEADS = 16
A_KV_HEADS = 2
A_HEAD_DIM = 64
A_GROUP = A_HEADS // A_KV_HEADS
WINDOW = 128
WIN_CHUNKS = WINDOW // CHUNK
A_Q_W = A_HEADS * A_HEAD_DIM
A_KV_W = A_KV_HEADS * A_HEAD_DIM
NUM_BUCKETS = 32
MAX_DISTANCE = 128
B_HEADS = 8
B_KEY_DIM = 128
B_VAL_DIM = D_MODEL // B_HEADS
B_KEY_W = B_HEADS * B_KEY_DIM
B_VAL_W = B_HEADS * B_VAL_DIM
D_FF = 2816
CONV_W = 3
PLE_DIM = 256
EPS = 1e-6
NEG_INF = -1e30

_SPLIT_SIZES = (A_Q_W, A_KV_W, A_KV_W, B_KEY_W, B_KEY_W, B_VAL_W, B_VAL_W, D_MODEL, D_MODEL)
IN_COLS = sum(_SPLIT_SIZES)
SPLIT_POINTS = tuple(int(s) for s in np.cumsum(_SPLIT_SIZES)[:-1])

kernel_name = "hybrid_swa_hgrn2_convffn_stream_step"


def _rms_norm(x, g):
    xf = x.astype(jnp.float32)
    y = xf * lax.rsqrt(jnp.mean(xf * xf, axis=-1, keepdims=True) + EPS)
    return (y * g.astype(jnp.float32)).astype(x.dtype)


def _t5_bucket(rel):
    nb = NUM_BUCKETS // 2
    ret = jnp.where(rel > 0, nb, 0)
    n = jnp.abs(rel)
    max_exact = nb // 2
    large = max_exact + (jnp.log(jnp.maximum(n, max_exact).astype(jnp.float32) / max_exact)
                         / math.log(MAX_DISTANCE / max_exact) * (nb - max_exact)).astype(jnp.int32)
    large = jnp.minimum(large, nb - 1)
    return ret + jnp.where(n < max_exact, n, large)


def _rel_bias(table, q_pos, k_pos):
    buckets = _t5_bucket(k_pos[None, :] - q_pos[:, None])
    b = jnp.transpose(table[buckets], (2, 0, 1))
    return b.reshape(A_KV_HEADS, A_GROUP, q_pos.shape[0], k_pos.shape[0])


def _sink_attention(q, k, v, bias, mask, sinks):
    s = jnp.einsum('bnqkgd,bnskd->bnkgqs', q, k).astype(jnp.float32) * (A_HEAD_DIM ** -0.5)
    s = s + bias.astype(jnp.float32)[None, None]
    s = jnp.where(mask[None, :, None, None], s, NEG_INF)
    sink = sinks.astype(jnp.float32)[None, None, :, :, None, None]
    m = jnp.maximum(jnp.max(s, axis=-1, keepdims=True), sink)
    p = jnp.exp(s - m)
    denom = jnp.sum(p, axis=-1, keepdims=True) + jnp.exp(sink - m)
    return jnp.einsum('bnkgqs,bnskd->bnqkgd', (p / denom).astype(v.dtype), v)


def _hgrn2(q, f_logit, i_in, s0, lb, block):
    f32 = jnp.float32
    f = lb + (1.0 - lb) * jax.nn.sigmoid(f_logit.astype(f32))
    logf = jnp.log(f)
    k = 1.0 - f
    q = jax.nn.silu(q.astype(f32))
    v = i_in.astype(f32)
    bsz, t, h, dk = q.shape
    dv = v.shape[-1]
    nb = t // block

    def to_blocks(a):
        return a.reshape(bsz, nb, block, h, a.shape[-1]).transpose(1, 0, 3, 2, 4)

    qc, kc, vc = to_blocks(q), to_blocks(k), to_blocks(v)
    cum = jnp.cumsum(to_blocks(logf), axis=3)
    causal = jnp.tril(jnp.ones((block, block), dtype=bool))
    mid = block // 2

    def step(state, blk):
        q_, k_, v_, b_ = blk
        b_last = b_[:, :, -1:, :]
        b_mid = b_[:, :, mid:mid + 1, :]
        o = jnp.einsum('bhld,bhdv->bhlv', q_ * jnp.exp(b_), state)
        a = jnp.einsum('bhtd,bhsd->bhts', q_ * jnp.exp(b_ - b_mid), k_ * jnp.exp(b_mid - b_))
        o = o + jnp.einsum('bhts,bhsv->bhtv', jnp.where(causal, a, 0.0), v_)
        state = (jnp.exp(b_last)[:, :, 0, :, None] * state
                 + jnp.einsum('bhsd,bhsv->bhdv', k_ * jnp.exp(b_last - b_), v_))
        return state, o

    s_fin, o = lax.scan(step, s0.astype(f32), (qc, kc, vc, cum))
    o = o.transpose(1, 0, 3, 2, 4).reshape(bsz, t, h, dv)
    return o, s_fin


def _layer(x, pe, k_prev, v_prev, s_prev, conv_prev, rel_table, lb,
           g_pre_mix, w_in, sinks, g_hgrn_out, w_br_a, w_br_b, w_out, g_post_mix,
           g_pre_ffn, w_up, w_conv, b_conv, w_down, g_post_ffn, w_ple, w_ple_gate):
    prompt = k_prev is None
    bsz, t, _ = x.shape
    h = _rms_norm(x, g_pre_mix)
    z = h @ w_in
    qa, ka, va, qb, fb, ib, ob, ga, gb = jnp.split(z, SPLIT_POINTS, axis=-1)
    qa = qa.reshape(bsz, t, A_KV_HEADS, A_GROUP, A_HEAD_DIM)
    ka = ka.reshape(bsz, t, A_KV_HEADS, A_HEAD_DIM)
    va = va.reshape(bsz, t, A_KV_HEADS, A_HEAD_DIM)

    if prompt:
        nc = t // CHUNK
        lk = (WIN_CHUNKS + 1) * CHUNK
        qblk = qa.reshape(bsz, nc, CHUNK, A_KV_HEADS, A_GROUP, A_HEAD_DIM)
        pad = ((0, 0), (WIN_CHUNKS * CHUNK, 0), (0, 0), (0, 0))
        kp = jnp.pad(ka, pad).reshape(bsz, nc + WIN_CHUNKS, CHUNK, A_KV_HEADS, A_HEAD_DIM)
        vp = jnp.pad(va, pad).reshape(bsz, nc + WIN_CHUNKS, CHUNK, A_KV_HEADS, A_HEAD_DIM)
        kblk = jnp.concatenate([kp[:, j:j + nc] for j in range(WIN_CHUNKS + 1)], axis=2)
        vblk = jnp.concatenate([vp[:, j:j + nc] for j in range(WIN_CHUNKS + 1)], axis=2)
        q_pos = jnp.arange(CHUNK) + WIN_CHUNKS * CHUNK
        k_pos = jnp.arange(lk)
        valid = (jnp.arange(nc)[:, None] - WIN_CHUNKS + (k_pos // CHUNK)[None, :]) >= 0
        mask = jnp.broadcast_to(valid[:, None, :], (nc, CHUNK, lk))
        keep = min(WINDOW, t)
        new_k, new_v = ka[:, t - keep:], va[:, t - keep:]
    else:
        lc = k_prev.shape[1]
        qblk = qa[:, None]
        kblk = jnp.concatenate([k_prev.astype(ka.dtype), ka], axis=1)[:, None]
        vblk = jnp.concatenate([v_prev.astype(va.dtype), va], axis=1)[:, None]
        q_pos = jnp.arange(t) + lc
        k_pos = jnp.arange(lc + t)
        mask = jnp.ones((1, t, lc + t), dtype=bool)
        new_k, new_v = ka, va
    bias = _rel_bias(rel_table, q_pos, k_pos)
    ya = _sink_attention(qblk, kblk, vblk, bias, mask, sinks.reshape(A_KV_HEADS, A_GROUP))
    ya = ya.reshape(bsz, t, A_Q_W)

    s0 = jnp.zeros((bsz, B_HEADS, B_KEY_DIM, B_VAL_DIM), jnp.float32) if prompt else s_prev
    block = CHUNK if prompt else t
    yb, s_fin = _hgrn2(qb.reshape(bsz, t, B_HEADS, B_KEY_DIM), fb.reshape(bsz, t, B_HEADS, B_KEY_DIM),
                       ib.reshape(bsz, t, B_HEADS, B_VAL_DIM), s0, lb, block)
    yb = _rms_norm(yb.astype(x.dtype), g_hgrn_out) * jax.nn.silu(ob.reshape(bsz, t, B_HEADS, B_VAL_DIM))
    yb = yb.reshape(bsz, t, B_VAL_W)

    mix = jax.nn.sigmoid(ga) * (ya @ w_br_a) + jax.nn.sigmoid(gb) * (yb @ w_br_b)
    x = x + _rms_norm(mix @ w_out, g_post_mix)

    hf = _rms_norm(x, g_pre_ffn)
    a, u = jnp.split(hf @ w_up, 2, axis=-1)
    prev = jnp.zeros((bsz, CONV_W - 1, D_FF), a.dtype) if prompt else conv_prev.astype(a.dtype)
    ap = jnp.concatenate([prev, a], axis=1)
    ac = b_conv
    for j in range(CONV_W):
        ac = ac + ap[:, j:j + t] * w_conv[j]
    ffn = (jax.nn.gelu(ac, approximate=True) * u) @ w_down
    x = x + _rms_norm(ffn, g_post_ffn)
    conv_tail = ap[:, ap.shape[1] - (CONV_W - 1):]

    x = x + (pe @ w_ple) * jax.nn.sigmoid(x @ w_ple_gate)
    return x, new_k, new_v, s_fin.astype(x.dtype), conv_tail


def setup_inputs(seed: int = 0) -> dict:
    key = jax.random.key(seed)
    ks = jax.random.split(key, 32)
    f32 = jnp.float32

    def nrm(k, shape, scale):
        return jax.random.normal(k, shape, f32) * scale

    def gain(k, shape):
        return 1.0 + 0.05 * jax.random.normal(k, shape, f32)

    w_cache = min(WINDOW, PAST_LEN)
    return {
        "x_prompt": nrm(ks[0], (BATCH, SEQ, D_MODEL), 1.0),
        "x_sample": nrm(ks[1], (DEC_BATCH, DEC_SEQ, D_MODEL), 1.0),
        "cache_win_k": nrm(ks[2], (DEPTH, DEC_BATCH, w_cache, A_KV_HEADS, A_HEAD_DIM), 1.0),
        "cache_win_v": nrm(ks[3], (DEPTH, DEC_BATCH, w_cache, A_KV_HEADS, A_HEAD_DIM), 1.0),
        "state_hgrn": nrm(ks[4], (DEPTH, DEC_BATCH, B_HEADS, B_KEY_DIM, B_VAL_DIM), 0.5),
        "cache_ffn_conv": nrm(ks[5], (DEPTH, DEC_BATCH, CONV_W - 1, D_FF), 1.0),
        "p_prompt": nrm(ks[6], (DEPTH, BATCH, SEQ, PLE_DIM), 1.0),
        "p_sample": nrm(ks[7], (DEPTH, DEC_BATCH, DEC_SEQ, PLE_DIM), 1.0),
        "rel_bias_table": nrm(ks[8], (NUM_BUCKETS, A_HEADS), 0.5),
        "lb_logits": nrm(ks[9], (DEPTH + 1, B_KEY_W), 0.5),
        "g_pre_mix": gain(ks[10], (DEPTH, D_MODEL)),
        "w_in": nrm(ks[11], (DEPTH, D_MODEL, IN_COLS), D_MODEL ** -0.5),
        "attn_sinks": nrm(ks[12], (DEPTH, A_HEADS), 0.5),
        "g_hgrn_out": gain(ks[13], (DEPTH, B_VAL_DIM)),
        "w_br_a": nrm(ks[14], (DEPTH, A_Q_W, D_MODEL), A_Q_W ** -0.5),
        "w_br_b": nrm(ks[15], (DEPTH, B_VAL_W, D_MODEL), B_VAL_W ** -0.5),
        "w_out": nrm(ks[16], (DEPTH, D_MODEL, D_MODEL), D_MODEL ** -0.5),
        "g_post_mix": gain(ks[17], (DEPTH, D_MODEL)),
        "g_pre_ffn": gain(ks[18], (DEPTH, D_MODEL)),
        "w_up": nrm(ks[19], (DEPTH, D_MODEL, 2 * D_FF), D_MODEL ** -0.5),
        "w_conv": nrm(ks[20], (DEPTH, CONV_W, D_FF), CONV_W ** -0.5),
        "b_conv": nrm(ks[21], (DEPTH, D_FF), 0.02),
        "w_down": nrm(ks[22], (DEPTH, D_FF, D_MODEL), D_FF ** -0.5),
        "g_post_ffn": gain(ks[23], (DEPTH, D_MODEL)),
        "w_ple": nrm(ks[24], (DEPTH, PLE_DIM, D_MODEL), PLE_DIM ** -0.5),
        "w_ple_gate": nrm(ks[25], (DEPTH, D_MODEL, D_MODEL), D_MODEL ** -0.5),
    }


def reference(x_prompt, x_sample, cache_win_k, cache_win_v, state_hgrn, cache_ffn_conv,
              p_prompt, p_sample, rel_bias_table, lb_logits, g_pre_mix, w_in, attn_sinks,
              g_hgrn_out, w_br_a, w_br_b, w_out, g_post_mix, g_pre_ffn, w_up, w_conv, b_conv,
              w_down, g_post_ffn, w_ple, w_ple_gate):
    lbs = jnp.cumsum(jax.nn.softmax(lb_logits.astype(jnp.float32), axis=0), axis=0)
    yp, ys = x_prompt, x_sample
    outs_p, outs_s = [], []
    for i in range(DEPTH):
        lw = (g_pre_mix[i], w_in[i], attn_sinks[i], g_hgrn_out[i], w_br_a[i], w_br_b[i], w_out[i],
              g_post_mix[i], g_pre_ffn[i], w_up[i], w_conv[i], b_conv[i], w_down[i], g_post_ffn[i],
              w_ple[i], w_ple_gate[i])
        lb = lbs[i].reshape(B_HEADS, B_KEY_DIM)
        yp, pk, pv, ps, pc = _layer(yp, p_prompt[i], None, None, None, None, rel_bias_table, lb, *lw)
        ys, sk, sv, ss, sc = _layer(ys, p_sample[i], cache_win_k[i], cache_win_v[i], state_hgrn[i],
                                    cache_ffn_conv[i], rel_bias_table, lb, *lw)
        outs_p.append((pk, pv, ps, pc))
        outs_s.append((sk, sv, ss, sc))
    prompt_win_k = jnp.stack([o[0] for o in outs_p])
    prompt_win_v = jnp.stack([o[1] for o in outs_p])
    prompt_hgrn_state = jnp.stack([o[2] for o in outs_p])
    prompt_ffn_conv = jnp.stack([o[3] for o in outs_p])
    sample_win_k = jnp.stack([o[0] for o in outs_s])
    sample_win_v = jnp.stack([o[1] for o in outs_s])
    sample_hgrn_state = jnp.stack([o[2] for o in outs_s])
    sample_ffn_conv = jnp.stack([o[3] for o in outs_s])
    return (yp, ys, prompt_win_k, prompt_win_v, prompt_hgrn_state, prompt_ffn_conv,
            sample_win_k, sample_win_v, sample_hgrn_state, sample_ffn_conv)
```

```python
import math
from contextlib import ExitStack

import numpy as np
import concourse.bass as bass
import concourse.mybir as mybir
from concourse.bass_utils import run_bass_kernel_spmd

F32 = mybir.dt.float32
BF16 = mybir.dt.bfloat16
AF = mybir.ActivationFunctionType
ALU = mybir.AluOpType

ENGS = ("pe", "act", "dve", "pool", "sp")


class SemRef:
    def __init__(self, name):
        self.name = name
        self.handle = None
        self.count = 0


class Buf:
    def __init__(self, name, t=None, dsem=None):
        self.name = name
        self.t = t
        self.last_w = None
        self.readers = {}
        self.overlaps = []
        self.dsem = dsem

    def __getitem__(self, idx):
        return self.t[idx]


class Rec:
    def __init__(self):
        self.q = {e: [] for e in ENGS}
        self.esem = {e: SemRef("s_" + e) for e in ENGS}
        self.waited = {e: {} for e in ENGS}
        self.dsems = []
        self.out_marks = {}
        self.n_instr = 0

    def new_dsem(self, name):
        s = SemRef(name)
        self.dsems.append(s)
        return s

    def _deps(self, reads, writes):
        deps = {}

        def add(d):
            if d is None:
                return
            s, v = d
            if deps.get(s, -1) < v:
                deps[s] = v

        for b in reads:
            for bb in [b] + b.overlaps:
                add(bb.last_w)
        for b in writes:
            for bb in [b] + b.overlaps:
                add(bb.last_w)
                for s, v in bb.readers.items():
                    add((s, v))
        return deps

    def _waits(self, eng, deps, skip_self=False):
        ws = []
        wd = self.waited[eng]
        for s, v in deps.items():
            if skip_self and s is self.esem[eng]:
                continue
            if wd.get(s, 0) >= v:
                continue
            wd[s] = v
            ws.append((s, v))
        return ws

    def op(self, eng, fn, reads=(), writes=(), inc=True, skip_self=False):
        reads = list(reads)
        writes = list(writes)
        deps = self._deps(reads, writes)
        ws = self._waits(eng, deps, skip_self)
        sem = self.esem[eng]
        if inc:
            sem.count += 1
        mark = (sem, sem.count if inc else sem.count + 1)
        self.q[eng].append((ws, fn, sem if inc else None, 1))
        for b in reads:
            if b.readers.get(sem, 0) < mark[1]:
                b.readers[sem] = mark[1]
        for b in writes:
            b.last_w = mark
            b.readers = {}
        self.n_instr += 1
        return mark

    def dma(self, eng, out_ap, in_ap, reads=(), writes=(), dsem=None, is_output=False, nodeps=False):
        reads = list(reads)
        writes = list(writes)
        if dsem is None:
            for b in writes + reads:
                if b.dsem is not None:
                    dsem = b.dsem
                    break
        assert dsem is not None
        deps = {} if nodeps else self._deps(reads, writes)
        ws = self._waits(eng, deps)
        dsem.count += 16
        mark = (dsem, dsem.count)

        def fn(e, out_ap=out_ap, in_ap=in_ap):
            return e.dma_start(out=out_ap, in_=in_ap)

        self.q[eng].append((ws, fn, dsem, 16))
        for b in reads:
            b.readers[dsem] = dsem.count
        for b in writes:
            b.last_w = mark
            b.readers = {}
        if is_output:
            self.out_marks[dsem] = dsem.count
        self.n_instr += 1
        return mark

    def barrier(self):
        sems = [s for s in list(self.esem.values()) + self.dsems if s.count > 0]
        for e in ENGS:
            ws = self._waits(e, {s: s.count for s in sems})
            if ws:
                self.q[e].append((ws, None, None, 0))

    def finish(self, eng="sp"):
        ws = [(s, v) for s, v in self.out_marks.items()]
        ws += [(s, s.count) for s in self.esem.values() if s.count > 0]
        self.q[eng].append((ws, None, None, 0))

    def replay(self, nc, stack):
        for s in list(self.esem.values()) + self.dsems:
            s.handle = stack.enter_context(nc.semaphore(s.name))
        block = stack.enter_context(nc.Block())
        decos = {"pe": block.tensor, "act": block.scalar, "dve": block.vector,
                 "pool": block.gpsimd, "sp": block.sync}
        for en in ENGS:
            lst = self.q[en]

            def body(e, lst=lst):
                for ws, fn, sem, inc in lst:
                    for s, v in ws:
                        e.wait_ge(s.handle, v)
                    if fn is None:
                        continue
                    ins = fn(e)
                    if sem is not None:
                        ins.then_inc(sem.handle, inc)

            decos[en](body)


D = 1024
NCORE = 8
PB = 4
SEQ = 2048
T = 256
NTILE = SEQ // T
DSEQ = 32
DFF = 2816
NJ = 22
INC = 7424
PLE = 256
EPS = 1e-6
NEG = -30000.0
C_QA, C_KA, C_VA, C_QB, C_FB, C_IB, C_OB, C_GA, C_GB = 0, 1024, 1152, 1280, 2304, 3328, 4352, 5376, 6400

WSHAPES = {"w_in": (D, INC), "w_br_a": (D, D), "w_br_b": (D, D), "w_out": (D, D), "w_up": (D, 2 * DFF),
           "w_down": (DFF, D), "w_ple": (PLE, D), "w_ple_gate": (D, D)}

O_ID, O_OH, O_MA, O_MAS, O_HMP, O_HMS, O_SCP, O_SCS, O_SEG, O_EPS, O_ONELH, O_ONE = (
    0, 128, 512, 768, 896, 1024, 1152, 1408, 1536, 1542, 1543, 1799)
O_J = 1927
CST_W = 2055
P_G1, P_G2, P_L0, P_L1, P_GH, P_WC, P_BC, P_SK = 0, 8, 16, 24, 32, 33, 99, 121
PV_W = 129


def _t5_bucket(rel):
    nb = 16
    ret = np.where(rel > 0, nb, 0)
    n = np.abs(rel)
    max_exact = nb // 2
    large = max_exact + (np.log(np.maximum(n, max_exact).astype(np.float32) / max_exact)
                         / math.log(128 / max_exact) * (nb - max_exact)).astype(np.int32)
    large = np.minimum(large, nb - 1)
    return ret + np.where(n < max_exact, n, large)


def _make_cst():
    c = np.zeros((128, CST_W), np.float32)
    c[:, O_ID:O_ID + 128] = np.eye(128, dtype=np.float32)
    j = np.arange(384)
    bk = _t5_bucket(127 - j)
    c[bk, O_OH + j] = 1.0
    p = np.arange(128)[:, None]
    col = np.arange(256)[None, :]
    hh = p // 64
    dd = col // 64
    valid = ((dd - hh) >= 0) & ((dd - hh) <= 2)
    c[:, O_MA:O_MA + 256] = np.where(valid, 0.0, NEG)
    col1 = np.arange(128)[None, :]
    c[:, O_MAS:O_MAS + 128] = np.where((p // 32) == (col1 // 32), 0.0, NEG)
    c[:, O_HMP:O_HMP + 128] = ((p // 64) == (col1 // 64)) & (p <= col1)
    c[:, O_HMS:O_HMS + 128] = ((p // 32) == (col1 // 32)) & (p <= col1)
    c[:, O_SCP:O_SCP + 256] = (np.arange(256) % 64 != 0)[None, :]
    c[:, O_SCS:O_SCS + 128] = (np.arange(128) % 32 != 0)[None, :]
    c[:, O_SEG + 0] = (p[:, 0] < 64)
    c[:, O_SEG + 1] = (p[:, 0] >= 64)
    for s in range(4):
        c[:, O_SEG + 2 + s] = (p[:, 0] // 32 == s)
    c[:, O_EPS] = EPS
    c[:, O_ONELH:O_ONELH + 64] = 1.0
    c[:, O_ONELH + 128 + 64:O_ONELH + 256] = 1.0
    c[:, O_ONE:O_ONE + 128] = 1.0
    c[np.arange(128), O_J + 127 - np.arange(128)] = 1.0
    return c


DBG = {"tiles": None, "stage": 99, "setup": 99, "sub": 99}


def build_program():
    nc = bass.Bass("TRN2", target_bir_lowering=False)
    R = Rec()

    def din(name, shape, dt=F32):
        return nc.dram_tensor(name, list(shape), dt, kind="ExternalInput")

    def dout(name, shape, dt=F32):
        return nc.dram_tensor(name, list(shape), dt, kind="ExternalOutput")

    xp = din("x_prompt", [PB, SEQ, D]).ap()
    xs = din("x_sample", [PB * DSEQ, D]).ap()
    ck = din("cache_k", [PB, 128, 128]).ap()
    cv = din("cache_v", [PB, 128, 128]).ap()
    sh = din("state_hgrn", [PB, 8, 128, 128]).ap()
    cfc = din("cache_conv", [128, NJ, PB, 2]).ap()
    pp = din("p_prompt", [PB, SEQ, PLE]).ap()
    psm = din("p_sample", [PB * DSEQ, PLE]).ap()
    tabp = din("table_pad", [128, 128]).ap()
    cst_d = din("cst", [128, CST_W]).ap()
    pv_d = din("pv", [128, PV_W]).ap()
    gbc_d = din("gbc", [128, 2, D]).ap()
    wsrc = {k: din(k, list(v)).ap() for k, v in WSHAPES.items()}

    o_yp = dout("y_prompt", [PB, SEQ, D]).ap()
    o_ys = dout("y_sample", [PB * DSEQ, D]).ap()
    o_pk = dout("o_pk", [PB, 128, 128]).ap()
    o_pvv = dout("o_pv", [PB, 128, 128]).ap()
    o_ps = dout("o_ps", [PB, 8, 128, 128]).ap()
    o_pc = dout("o_pc", [128, NJ, PB, 2]).ap()
    o_sk = dout("o_sk", [PB * DSEQ, 128]).ap()
    o_sv = dout("o_sv", [PB * DSEQ, 128]).ap()
    o_ss = dout("o_ss", [PB, 8, 128, 128]).ap()
    o_sc = dout("o_sc", [128, NJ, PB, 2]).ap()

    wb_t = {k: nc.dram_tensor(k + "_bf", list(v), BF16, kind="Internal") for k, v in WSHAPES.items()}
    trev_t = nc.dram_tensor("trev", [16, 384], F32, kind="Internal")

    with ExitStack() as st:
        sbtot = [0]

        def sb(name, shape, dt):
            n_ = 1
            for d_ in shape[1:]:
                n_ *= d_
            sbtot[0] += n_ * (4 if dt == F32 else 2)
            return st.enter_context(nc.sbuf_tensor("sb_" + name, list(shape), dt))

        def pst(name, shape, dt):
            return st.enter_context(nc.psum_tensor("ps_" + name, list(shape), dt))

        def B(name, shape, dt, dma=False):
            return Buf(name, sb(name, shape, dt), R.new_dsem("d_" + name) if dma else None)

        xtbufs = [B("xt%d" % i, [128, 2, D], F32, dma=True) for i in range(2)]
        xt = xtbufs[0]
        hn = B("hn", [128, 2, D], F32)
        hT = B("hT", [128, 8, T], BF16)
        qT = B("qT", [128, 8, T], BF16)
        kz_t = sb("kz", [128, 4, 128 + T], BF16)
        kz_c = Buf("kz_c", kz_t)
        kz_n = Buf("kz_n", kz_t)
        vz_t = sb("vz", [128, 3, 4, 128], BF16)
        vz_c = Buf("vz_c", vz_t)
        vz_n = Buf("vz_n", vz_t)
        pe_tok = B("pe_tok", [128, 2, PLE], F32, dma=True)
        peT = B("peT", [128, 2, T], BF16)
        ya = B("ya", [128, 8, T], BF16)
        yb = B("yb", [128, 8, T], BF16)
        mixT = B("mixT", [128, 8, T], BF16)
        sga = B("sga", [128, 8, T], BF16)
        sgb = B("sgb", [128, 8, T], BF16)
        tA = B("tA", [128, 8, T], BF16)
        hm = B("hm", [128, NJ, T], BF16)
        vb = B("vb", [128, 2, D], BF16)
        S32_t = sb("S32", [128, 8, 128], F32)
        Sbf_t = sb("Sbf", [128, 8, 128], BF16)
        S32 = [Buf("S32_%d" % i, S32_t, R.new_dsem("d_S32_%d" % i)) for i in range(8)]
        Sbf = [Buf("Sbf_%d" % i, Sbf_t) for i in range(8)]
        carry = B("carry", [128, NJ, 4, 2], F32, dma=True)
        BT = B("BT", [128, 16, 256], BF16)
        BTs = B("BTs", [128, 16, 128], BF16)
        gbc = B("gbc", [128, 2, D], F32, dma=True)
        cst = B("cst", [128, CST_W], F32, dma=True)
        cbf = B("cbf", [128, 512], BF16)
        pv = B("pv", [128, PV_W], F32, dma=True)
        pv2 = B("pv2", [128, 56], F32)
        tab = B("tab", [128, 128], F32, dma=True)
        trs = B("trs", [128, 384], F32, dma=True)
        wkz = B("wkz", [128, 4, 8, 128], BF16, dma=True)
        kvout = [B("kvout%d" % i, [128, 256], F32, dma=True) for i in range(2)]
        wslot = [B("wslot%d" % i, [128, 4096], BF16, dma=True) for i in range(4)]
        ss = B("ss", [128, 8], F32)
        rstd = B("rstd", [128, 8], F32)
        f32t = [B("f32t%d" % i, [128, 256], F32) for i in range(6)]
        hs_f = [[B("hsf%d_%d" % (s_, i), [128, 256], F32) for i in range(8)] for s_ in range(2)]
        hs_b = [[B("hsb%d_%d" % (s_, i), [128, 256], BF16) for i in range(6)] for s_ in range(2)]
        hs_md = [B("hsmd%d" % s_, [128, 8], F32) for s_ in range(2)]
        hs_st = [B("hsst%d" % s_, [128, 3, 128], BF16) for s_ in range(2)]
        f32w = [B("f32w%d" % i, [128, 512], F32, dma=True) for i in range(3)]
        bf16t = [B("bf16t%d" % i, [128, 256], BF16) for i in range(6)]
        aext = [B("aext%d" % i, [128, T + 8], F32) for i in range(3)]
        kstT = [B("kstT%d" % i, [128, 4, 128], BF16) for i in range(2)]
        dec = [B("dec%d" % i, [128, 8], F32) for i in range(2)]
        rot = {"f": 0, "b": 0, "a": 0, "k": 0, "d": 0, "kv": 0, "w": 0}
        junk_ap = hn.t[:, 0, :]
        kcz_v = mixT.t[:, :, :].rearrange("p a (b n) -> p (a b) n", n=128).rearrange("p (i v) n -> p i v n", v=4)
        vcz_v = sga.t[:, :, :].rearrange("p a (b n) -> p (a b) n", n=128).rearrange("p (i v) n -> p i v n", v=4)

        def tw():
            rot["w"] = (rot["w"] + 1) % len(f32w)
            return f32w[rot["w"]]

        def tf():
            rot["f"] = (rot["f"] + 1) % len(f32t)
            return f32t[rot["f"]]

        def tb():
            rot["b"] = (rot["b"] + 1) % len(bf16t)
            return bf16t[rot["b"]]

        pbank = [Buf("pb%d" % i, pst("pb%d" % i, [128, 512], F32)) for i in range(8)]
        PA, PBk, PS0, PS1, PY, PD, PT0, PT1 = pbank
        pT = [PT0, PT1]
        rotp = {"i": 0, "set": list(pbank)}

        def set_rot(banks):
            rotp["set"] = list(banks)

        def pany():
            rotp["i"] = (rotp["i"] + 1) % len(rotp["set"])
            return rotp["set"][rotp["i"]]

        pmm = pany
        pscore = pany

        def ptr():
            bk = pany()
            return bk, bk.t[:, 0:256]

        def run_zip(gens):
            active = list(gens)
            while active:
                for g_ in list(active):
                    try:
                        next(g_)
                    except StopIteration:
                        active.remove(g_)

        ident = cbf.t[:, 0:128]
        ones_lo = cbf.t[:, 128:256]
        ones_hi = cbf.t[:, 256:384]
        ones_bf = cbf.t[:, 384:512]
        ident_f = cst.t[:, O_ID:O_ID + 128]
        eps_col = cst.t[:, O_EPS:O_EPS + 1]

        R.dma("sp", cst[:, :], cst_d, writes=[cst])
        R.dma("sp", pv[:, :], pv_d, writes=[pv])
        R.dma("sp", gbc[:, :, :], gbc_d, writes=[gbc])
        R.dma("sp", tab[:, :], tabp, writes=[tab])
        R.op("dve", lambda e: e.tensor_copy(out=cbf[:, 0:128], in_=cst[:, O_ID:O_ID + 128]), reads=[cst], writes=[cbf])
        R.op("dve", lambda e: e.tensor_copy(out=cbf[:, 128:384], in_=cst[:, O_ONELH:O_ONELH + 256]), reads=[cst], writes=[cbf])
        R.op("dve", lambda e: e.tensor_copy(out=cbf[:, 384:512], in_=cst[:, O_ONE:O_ONE + 128]), reads=[cst], writes=[cbf])
        t0 = tf()
        R.op("dve", lambda e: e.tensor_tensor(out=t0[:, 0:8], in0=pv[:, P_L0:P_L0 + 8], in1=pv[:, P_L1:P_L1 + 8], op=ALU.subtract),
             reads=[pv], writes=[t0])
        R.op("act", lambda e: e.activation(out=pv2[:, 0:8], in_=t0[:, 0:8], func=AF.Sigmoid), reads=[t0], writes=[pv2])
        R.op("act", lambda e: e.activation(out=pv2[:, 8:16], in_=t0[:, 0:8], func=AF.Sigmoid, scale=-1.0), reads=[t0], writes=[pv2])
        R.op("act", lambda e: e.activation(out=pv2[:, 16:24], in_=pv[:, P_SK:P_SK + 8], func=AF.Exp), reads=[pv], writes=[pv2])
        R.op("dve", lambda e: e.tensor_scalar(out=pv2[:, 24:32], in0=pv2[:, 8:16], scalar1=0.5, scalar2=None, op0=ALU.mult), reads=[pv2], writes=[pv2])
        R.op("dve", lambda e: e.tensor_scalar(out=pv2[:, 32:40], in0=pv2[:, 8:16], scalar1=-0.5, scalar2=None, op0=ALU.mult), reads=[pv2], writes=[pv2])
        R.op("dve", lambda e: e.tensor_tensor(out=pv2[:, 40:48], in0=pv2[:, 24:32], in1=pv2[:, 0:8], op=ALU.add), reads=[pv2], writes=[pv2])
        R.op("dve", lambda e: e.tensor_scalar(out=pv2[:, 48:49], in0=pv[:, P_GH:P_GH + 1], scalar1=0.5, scalar2=None, op0=ALU.mult), reads=[pv], writes=[pv2])
        R.op("dve", lambda e: e.tensor_scalar(out=pv2[:, 49:50], in0=cst[:, O_EPS:O_EPS + 1], scalar1=4.0, scalar2=None, op0=ALU.mult), reads=[cst], writes=[pv2])
        R.op("pe", lambda e: e.matmul(PA[:, 0:384], lhsT=tab[:, :], rhs=cst[:, O_OH:O_OH + 384], start=True, stop=True),
             reads=[tab, cst], writes=[PA], skip_self=True)
        R.op("dve", lambda e: e.tensor_copy(out=trs[:, :], in_=PA[:, 0:384]), reads=[PA], writes=[trs])
        d_trev = Buf("trev", trev_t, R.new_dsem("d_trev"))
        R.dma("sp", trev_t.ap()[:, :], trs[0:16, :], reads=[trs], writes=[d_trev])
        btfs = [xb_.t[:, :, :].rearrange("p a (h c) -> p (a h) c", c=256) for xb_ in xtbufs]
        for half in range(2):
            src = bass.AP(trev_t, half * 8 * 384, [[1, 128], [384, 8], [1, 256]])
            R.dma("sp", btfs[half][:, :, :], src, reads=[d_trev], writes=[xtbufs[half]])

        def emit_bias_tiles():
            for half in range(2):
                for h8 in range(8):
                    h = half * 8 + h8
                    pb_ = pany()
                    R.op("pe", lambda e, pb_=pb_, h8=h8, half=half: e.matmul(pb_[:, 0:256], lhsT=cst[:, O_J:O_J + 128], rhs=btfs[half][:, h8, :],
                                                                             start=True, stop=True),
                         reads=[cst, xtbufs[half]], writes=[pb_], skip_self=True)
                    R.op("dve", lambda e, h=h, pb_=pb_: e.scalar_tensor_tensor(out=BT[:, h, :], in0=pb_[:, 0:256], scalar=8.0,
                                                                              in1=cst[:, O_MA:O_MA + 256], op0=ALU.mult, op1=ALU.add),
                         reads=[pb_, cst], writes=[BT])
                    R.op("dve", lambda e, h=h, pb_=pb_: e.scalar_tensor_tensor(out=BTs[:, h, :], in0=pb_[:, 0:128], scalar=8.0,
                                                                              in1=cst[:, O_MAS:O_MAS + 128], op0=ALU.mult, op1=ALU.add),
                         reads=[pb_, cst], writes=[BTs])

        NSTG = 7
        stg = [Buf("stg%d" % i, None, R.new_dsem("d_stg%d" % i)) for i in range(NSTG)]
        cvb = [Buf("cvb%d" % i, None, R.new_dsem("d_cvb%d" % i)) for i in range(NSTG)]
        wbuf = {k: Buf("wb_" + k, wb_t[k]) for k in WSHAPES}
        xt4 = hn.t[:, :, :].rearrange("p a (b c) -> p (a b) c", c=512)
        stg_ap = [xt4[:, i, :] for i in range(4)] + [f32w[i].t[:, 0:512] for i in range(3)]
        cvb_ap = [wslot[i % 4].t[:, (i // 4) * 512:(i // 4) * 512 + 512] for i in range(NSTG)]
        ceng = ("dve", "pool")
        plist = []
        for wn, (K, N) in WSHAPES.items():
            for kc in range(K // 128):
                for c0 in range(0, N, 512):
                    plist.append((wn, kc, c0, min(512, N - c0)))
        pend = []
        for ci, (wn, kc, c0, cw) in enumerate(plist):
            s_ = ci % NSTG
            R.dma("sp", stg_ap[s_][:, 0:cw], wsrc[wn][kc * 128:(kc + 1) * 128, c0:c0 + cw], writes=[stg[s_]])
            dst = cvb_ap[s_][:, 0:cw]
            en = ceng[ci % 2]
            R.op(en, lambda e, s_=s_, cw=cw, dst=dst: e.tensor_copy(out=dst, in_=stg_ap[s_][:, 0:cw]), reads=[stg[s_]], writes=[cvb[s_]])
            pend.append((wn, kc, c0, cw, s_, dst))
            if len(pend) > 4 or ci == len(plist) - 1:
                todo = pend if ci == len(plist) - 1 else [pend.pop(0)]
                for (wn2, kc2, c02, cw2, s2, dst2) in todo:
                    R.dma("act", wb_t[wn2].ap()[kc2 * 128:(kc2 + 1) * 128, c02:c02 + cw2], dst2, reads=[cvb[s2]], dsem=cvb[s2].dsem)
        emit_bias_tiles()
        R.barrier()
        R.op("pool", lambda e: e.memset(wkz[:, :, :, :], 0.0), writes=[wkz])
        wbin = wb_t["w_in"].ap()
        for kv in range(2):
            for hi in range(2):
                v = kv * 2 + hi
                R.dma("sp", wkz[:, v, :, hi * 64:hi * 64 + 64],
                      wbin[:, C_KA + kv * 64:C_KA + kv * 64 + 64].rearrange("(k p) n -> p k n", p=128),
                      reads=[wbuf["w_in"]], writes=[wkz])
        R.op("pool", lambda e: e.memset(vz_t[:, :, :, :], 0.0), writes=[vz_c, vz_n])

        panels = []

        def P_(wn, kc0, nkc, c0, ncols):
            panels.append((wn, kc0, nkc, c0, ncols))
            return len(panels) - 1

        PI = {}
        PI["qa"] = [P_("w_in", 0, 8, C_QA + 512 * i, 512) for i in range(2)]
        PI["kv"] = P_("w_in", 0, 8, C_KA, 256)
        PI["ib"] = [P_("w_in", 0, 8, C_IB + 512 * i, 512) for i in range(2)]
        PI["hg"] = []
        for hg in range(2):
            PI["hg"].append((P_("w_in", 0, 8, C_QB + 512 * hg, 512), P_("w_in", 0, 8, C_FB + 512 * hg, 512),
                             P_("w_in", 0, 8, C_OB + 512 * hg, 512)))
        PI["ga"] = [P_("w_in", 0, 8, C_GA + 512 * i, 512) for i in range(2)]
        PI["gb"] = [P_("w_in", 0, 8, C_GB + 512 * i, 512) for i in range(2)]
        PI["bra"] = [P_("w_br_a", 0, 8, 512 * i, 512) for i in range(2)]
        PI["brb"] = [P_("w_br_b", 0, 8, 512 * i, 512) for i in range(2)]
        PI["out"] = [P_("w_out", 0, 8, 512 * i, 512) for i in range(2)]
        PI["up"] = []
        for jg in range(6):
            ncl = 512 if jg < 5 else 256
            PI["up"].append((P_("w_up", 0, 8, 512 * jg, ncl), P_("w_up", 0, 8, DFF + 512 * jg, ncl), ncl // 128))
        PI["down"] = [[P_("w_down", k0, nk, 512 * i, 512) for (k0, nk) in ((0, 8), (8, 8), (16, 6))] for i in range(2)]
        PI["ple"] = P_("w_ple", 0, 2, 0, 1024)
        PI["pg"] = [P_("w_ple_gate", 0, 8, 512 * i, 512) for i in range(2)]
        NPAN = len(panels)
        wstate = {"issued": 0}
        NT_TILES = (PB * NTILE + 1) if DBG["tiles"] is None else len(DBG["tiles"])
        TOTAL_PAN = NPAN * NT_TILES

        def issue_panel(g):
            wn, kc0, nkc, c0, ncols = panels[g % NPAN]
            s = wslot[g % 4]
            src = wb_t[wn].ap()[kc0 * 128:(kc0 + nkc) * 128, c0:c0 + ncols].rearrange("(k p) n -> p k n", p=128)
            dst = s.t[:, 0:nkc * ncols].rearrange("p (k n) -> p k n", n=ncols)
            R.dma("sp", dst, src, reads=[wbuf[wn]], writes=[s])

        def getp(tile_idx, pidx, lo=None):
            g = tile_idx * NPAN + pidx
            glo = g if lo is None else tile_idx * NPAN + lo
            while wstate["issued"] <= min(glo + 3, TOTAL_PAN - 1):
                issue_panel(wstate["issued"])
                wstate["issued"] += 1
            wn, kc0, nkc, c0, ncols = panels[pidx]
            s = wslot[g % 4]
            return s, s.t[:, 0:nkc * ncols].rearrange("p (k n) -> p k n", n=ncols)

        def evac_copy(out_ap, obuf, pbuf, in_ap, k):
            if k % 2 == 0:
                R.op("act", lambda e: e.copy(out=out_ap, in_=in_ap), reads=[pbuf], writes=[obuf])
            else:
                R.op("dve", lambda e: e.tensor_copy(out=out_ap, in_=in_ap), reads=[pbuf], writes=[obuf])

        def mm_group(pbuf, out_ap, pairs, extra_reads, start=True, stop=True):
            n = len(pairs)
            for i, (l, r) in enumerate(pairs):
                R.op("pe", lambda e, l=l, r=r, i=i: e.matmul(out_ap, lhsT=l, rhs=r, start=(start and i == 0),
                                                              stop=(stop and i == n - 1)),
                     reads=extra_reads, writes=[pbuf], inc=(i == n - 1), skip_self=True)

        def bcast_last(ap2, n):
            prs = [list(x) for x in ap2.ap]
            return bass.AP(ap2.tensor, ap2.offset, prs + [[0, n]])

        def bcast_mid(ap2, n):
            prs = [list(x) for x in ap2.ap]
            return bass.AP(ap2.tensor, ap2.offset, [prs[0], [0, n]] + prs[1:])

        def sq_accum(in_ap, srcbuf, col):
            R.op("act", lambda e: e.activation(out=junk_ap[:, 0:in_ap.shape[-1]], in_=in_ap, func=AF.Square,
                                               accum_out=ss[:, col:col + 1]), reads=[srcbuf], writes=[hn, ss])

        def rstd_from_ss(nb, stride):
            for blk in range(nb):
                R.op("act", lambda e, blk=blk: e.activation(out=rstd[:, blk:blk + 1], in_=ss[:, blk * stride:blk * stride + 1],
                                                            func=AF.Ln, scale=1.0 / D, bias=eps_col), reads=[ss, cst], writes=[rstd])
            R.op("act", lambda e: e.activation(out=rstd[:, 0:nb], in_=rstd[:, 0:nb], func=AF.Exp, scale=-0.5), reads=[rstd], writes=[rstd])

        def to_feature_major(xt, nb, gcol, dstbuf, with_norm):
            NT = nb * 128
            if with_norm:
                for blk in range(nb):
                    sq_accum(xt[:, blk, :], xt, blk)
                rstd_from_ss(nb, 1)
                for blk in range(nb):
                    R.op("dve", lambda e, blk=blk: e.tensor_scalar(out=hn[:, blk, :], in0=xt[:, blk, :], scalar1=rstd[:, blk:blk + 1],
                                                                    scalar2=None, op0=ALU.mult), reads=[xt, rstd], writes=[hn])
            srcb = hn if with_norm else xt
            for c in range(8):
                pb_, pap = ptr()
                for blk in range(nb):
                    R.op("pe", lambda e, c=c, blk=blk, pap=pap: e.transpose(out=pap[:, blk * 128:(blk + 1) * 128],
                                                                           in_=srcb[:, blk, c * 128:(c + 1) * 128], identity=ident_f),
                         reads=[srcb, cst], writes=[pb_], inc=(blk == nb - 1), skip_self=True)
                if with_norm:
                    R.op("dve", lambda e, c=c, pap=pap: e.tensor_scalar(out=dstbuf[:, c, 0:NT], in0=pap[:, 0:NT],
                                                                        scalar1=pv[:, gcol + c:gcol + c + 1], scalar2=None, op0=ALU.mult),
                         reads=[pb_, pv], writes=[dstbuf])
                else:
                    evac_copy(dstbuf[:, c, 0:NT], dstbuf, pb_, pap[:, 0:NT], c)

        def fm_proj(slotbuf, slot3, ncc, rhsbuf, NT, consume):
            for cc in range(ncc):
                pb_ = pmm()
                mm_group(pb_, pb_[:, 0:NT], [(slot3[:, kc, cc * 128:(cc + 1) * 128], rhsbuf[:, kc, 0:NT]) for kc in range(8)],
                         [slotbuf, rhsbuf])
                consume(cc, pb_, pb_[:, 0:NT])

        def load_x(xt, kind, b, ti):
            if kind == "p":
                R.dma("sp", xt[:, 0:2, :], xp[b, ti * T:ti * T + T, :].rearrange("(n p) d -> p n d", p=128), writes=[xt])
            else:
                R.dma("sp", xt[:, 0, :], xs, writes=[xt])

        def tile_body(ti_glob, kind, b, ti, nxt):
            xt = xtbufs[ti_glob % 2]
            if ti_glob == 0:
                load_x(xt, kind, b, ti)
            if nxt is not None:
                load_x(xtbufs[(ti_glob + 1) % 2], *nxt)
            nb = 2 if kind == "p" else 1
            NT = nb * 128
            first_t = (kind == "p" and ti == 0)
            last_t = (kind == "p" and ti == NTILE - 1)
            L = 64 if kind == "p" else 32
            nch = NT // L
            nseg = 1 if kind == "p" else 4
            Ls = NT // nseg

            if kind == "p":
                t0_ = ti * T
                R.dma("sp", pe_tok[:, 0:nb, :], pp[b, t0_:t0_ + NT, :].rearrange("(n p) d -> p n d", p=128), writes=[pe_tok])
            else:
                R.dma("sp", pe_tok[:, 0, :], psm, writes=[pe_tok])
                kctok, vctok = tw(), tw()
                kc3 = kctok.t[:, 0:512].rearrange("p (i f) -> p i f", f=128)
                vc3 = vctok.t[:, 0:512].rearrange("p (i f) -> p i f", f=128)
                R.dma("sp", kc3, ck.rearrange("b s f -> s b f"), writes=[kctok])
                R.dma("sp", vc3, cv.rearrange("b s f -> s b f"), writes=[vctok])
                R.dma("sp", carry[:, :, :, :], cfc, writes=[carry])
                R.op("pool", lambda e: e.memset(kcz_v, 0.0), writes=[mixT])
                R.op("pool", lambda e: e.memset(vcz_v, 0.0), writes=[sga])
                kcsw = tw()
                ks3 = kcsw.t[:, 0:512].rearrange("p (i f) -> p i f", f=128)
                R.op("dve", lambda e: e.tensor_copy(out=ks3[:, :, 0:64], in_=kc3[:, :, 64:128]), reads=[kctok], writes=[kcsw])
                R.op("dve", lambda e: e.tensor_copy(out=ks3[:, :, 64:128], in_=kc3[:, :, 0:64]), reads=[kctok], writes=[kcsw])
                for i in range(4):
                    for kv in range(2):
                        for hi in range(2):
                            v = kv * 2 + hi
                            R.op("pool", lambda e, i=i, kv=kv, hi=hi, v=v: e.tensor_copy(
                                out=vcz_v[:, i, v, hi * 64:hi * 64 + 64], in_=vc3[:, i, kv * 64:kv * 64 + 64]),
                                reads=[vctok], writes=[sga])
                for i in range(4):
                    pb_, pap = ptr()
                    R.op("pe", lambda e, i=i, pap=pap: e.transpose(out=pap[:, 0:128], in_=kc3[:, i, :], identity=ident_f),
                         reads=[kctok, cst], writes=[pb_], inc=False, skip_self=True)
                    R.op("pe", lambda e, i=i, pap=pap: e.transpose(out=pap[:, 128:256], in_=ks3[:, i, :], identity=ident_f),
                         reads=[kcsw, cst], writes=[pb_], inc=True, skip_self=True)
                    R.op("dve", lambda e, i=i, pap=pap: e.tensor_copy(out=kcz_v[0:64, i, 0, :], in_=pap[0:64, 0:128]), reads=[pb_], writes=[mixT])
                    R.op("dve", lambda e, i=i, pap=pap: e.tensor_copy(out=kcz_v[64:128, i, 3, :], in_=pap[64:128, 0:128]), reads=[pb_], writes=[mixT])
                    R.op("act", lambda e, i=i, pap=pap: e.copy(out=kcz_v[0:64, i, 2, :], in_=pap[0:64, 128:256]), reads=[pb_], writes=[mixT])
                    R.op("act", lambda e, i=i, pap=pap: e.copy(out=kcz_v[64:128, i, 1, :], in_=pap[64:128, 128:256]), reads=[pb_], writes=[mixT])
            if first_t:
                R.op("pool", lambda e: e.memset(S32_t[:, :, :], 0.0), writes=S32)
                R.op("pool", lambda e: e.memset(Sbf_t[:, :, :], 0.0), writes=Sbf)
                R.op("pool", lambda e: e.memset(carry[:, :, :, :], 0.0), writes=[carry])

            if DBG["stage"] <= 1:
                return
            to_feature_major(xt, nb, P_G1, hT, True)

            if DBG["stage"] <= 2:
                return
            for p in range(2):
                sbuf_, s3 = getp(ti_glob, PI["qa"][p])
                fm_proj(sbuf_, s3, 4, hT, NT,
                        lambda cc, pb_, pap, p=p: evac_copy(qT[:, 4 * p + cc, 0:NT], qT, pb_, pap, cc))
            sbuf_, s3 = getp(ti_glob, PI["kv"])
            for blk in range(nb):
                PX = pscore()
                mm_group(PX, PX[:, 0:256], [(hT[:, kc, blk * 128:(blk + 1) * 128], s3[:, kc, 0:256]) for kc in range(8)], [sbuf_, hT])
                for kv in range(2):
                    for hi in range(2):
                        v = kv * 2 + hi
                        eng = "dve"
                        if eng == "dve":
                            R.op("dve", lambda e, blk=blk, kv=kv, hi=hi, v=v, PX=PX: e.tensor_copy(
                                out=vz_t[:, 1 + blk, v, hi * 64:hi * 64 + 64], in_=PX[:, 128 + kv * 64:128 + kv * 64 + 64]),
                                reads=[PX], writes=[vz_n])
                        else:
                            R.op("act", lambda e, blk=blk, kv=kv, hi=hi, v=v: e.copy(
                                out=vz_t[:, 1 + blk, v, hi * 64:hi * 64 + 64], in_=PX[:, 128 + kv * 64:128 + kv * 64 + 64]),
                                reads=[PX], writes=[vz_n])
                want = (kind == "s") or (last_t and blk == nb - 1)
                if want:
                    rot["kv"] ^= 1
                    ko = kvout[rot["kv"]]
                    R.op("dve", lambda e, ko=ko, PX=PX: e.tensor_copy(out=ko[:, :], in_=PX[:, 0:256]), reads=[PX], writes=[ko])
                    if kind == "s":
                        R.dma("sp", o_sk, ko[:, 0:128], reads=[ko], is_output=True)
                        R.dma("sp", o_sv, ko[:, 128:256], reads=[ko], is_output=True)
                    else:
                        R.dma("sp", o_pk[b], ko[:, 0:128], reads=[ko], is_output=True)
                        R.dma("sp", o_pvv[b], ko[:, 128:256], reads=[ko], is_output=True)
            for v in range(4):
                pb_ = pmm()
                mm_group(pb_, pb_[:, 0:NT], [(wkz[:, v, kc, :], hT[:, kc, 0:NT]) for kc in range(8)], [wkz, hT])
                evac_copy(kz_t[:, v, 128:128 + NT], kz_n, pb_, pb_[:, 0:NT], v)

            if DBG["stage"] <= 3:
                return
            set_rot([PA, PBk, PS0, PS1])

            def attn_stream(pairs, YB, DB, sbanks=None):
                arot = [0]
                items = []
                for c in pairs:
                    kvh = c // 4
                    for hi in range(2):
                        h = 2 * c + hi
                        v = kvh * 2 + hi
                        if kind == "p":
                            blks = []
                            if not first_t:
                                blks.append((kz_c, kz_t[:, v, 0:128], 0, 128, BT, BT[:, h, 128:256], vz_c, vz_t[:, 0, v, :]))
                            blks.append((kz_n, kz_t[:, v, 128:256], 0, 256, BT, BT[:, h, 0:256], vz_n, vz_t[:, 1, v, :]))
                            blks.append((kz_n, kz_t[:, v, 256:384], 128, 256, BT, BT[:, h, 0:128], vz_n, vz_t[:, 2, v, :]))
                        else:
                            blks = [(kz_n, kz_t[:, v, 128:256], 0, 128, BTs, BTs[:, h, :], vz_n, vz_t[:, 1, v, :])]
                            for i in range(4):
                                blks.append((mixT, kcz_v[:, i, v, :], 32 * i, 32 * i + 32, BT, BT[:, h, 128:160], sga, vcz_v[:, i, v, :]))
                        for bi, blk_ in enumerate(blks):
                            items.append((c, hi, bi == 0 and hi == 0, (hi == 1 and bi == len(blks) - 1), blk_))

                def emit_scores(it):
                    c, hi, _, _, (kb, kap, q0, q1, bb, bap, _, _) = it
                    arot[0] ^= 1
                    ps_ = (PS0, PS1)[arot[0]] if sbanks is None else sbanks[arot[0]]
                    nq = q1 - q0
                    mm_group(ps_, ps_[:, 0:nq], [(kap, qT[:, c, q0:q1]), (ident, bap)], [kb, qT, bb, cbf])
                    return ps_

                written = {}

                def emit_rest(it, ps_):
                    c, hi, firstpair, lastpair, (kb, kap, q0, q1, bb, bap, vbuf, vap) = it
                    nq = q1 - q0
                    ptb = tb()
                    R.op("act", lambda e: e.activation(out=ptb[:, 0:nq], in_=ps_[:, 0:nq], func=AF.Exp, scale=0.125),
                         reads=[ps_], writes=[ptb])
                    if firstpair:
                        written.clear()
                    ol = ones_lo if hi == 0 else ones_hi
                    for s0 in range(q0, q1, 128):
                        s1 = min(s0 + 128, q1)
                        st_ = (len(written) == 0)
                        written[s0] = True
                        R.op("pe", lambda e, s0=s0, s1=s1, st_=st_: e.matmul(YB[:, s0:s1], lhsT=vap, rhs=ptb[:, s0 - q0:s1 - q0],
                                                                             start=st_, stop=False, skip_group_check=True),
                             reads=[vbuf, ptb], writes=[YB], inc=False, skip_self=True)
                        R.op("pe", lambda e, s0=s0, s1=s1, st_=st_: e.matmul(DB[:, s0:s1], lhsT=ol, rhs=ptb[:, s0 - q0:s1 - q0],
                                                                             start=st_, stop=False, skip_group_check=True),
                             reads=[cbf, ptb], writes=[DB], inc=True, skip_self=True)
                    if lastpair:
                        t1, t1y = tf(), tf()
                        R.op("dve", lambda e: e.tensor_scalar(out=t1[:, 0:NT], in0=DB[:, 0:NT], scalar1=pv2[:, 16 + c:17 + c], scalar2=None,
                                                              op0=ALU.add), reads=[DB, pv2], writes=[t1])
                        R.op("act", lambda e: e.copy(out=t1y[:, 0:NT], in_=YB[:, 0:NT]), reads=[YB], writes=[t1y])
                        R.op("dve", lambda e: e.reciprocal(out=t1[:, 0:NT], in_=t1[:, 0:NT]), reads=[t1], writes=[t1])
                        R.op("pool", lambda e: e.tensor_tensor(out=ya[:, c, 0:NT], in0=t1y[:, 0:NT], in1=t1[:, 0:NT], op=ALU.mult),
                             reads=[t1y, t1], writes=[ya])

                ps_cur = emit_scores(items[0])
                for i, it in enumerate(items):
                    ps_next = emit_scores(items[i + 1]) if i + 1 < len(items) else None
                    emit_rest(it, ps_cur)
                    ps_cur = ps_next
                    yield

            if DBG["stage"] <= 4:
                run_zip([attn_stream(range(0, 8, 2), PY, PD, (PA, PBk)), attn_stream(range(1, 8, 2), PT0, PT1, (PS0, PS1))])
                set_rot(pbank)
                return
            set_rot(pbank)
            for p in range(2):
                sbuf_, s3 = getp(ti_glob, PI["ib"][p])
                for blk in range(nb):
                    pb_ = pany()
                    mm_group(pb_, pb_[:, 0:512], [(hT[:, kc, blk * 128:(blk + 1) * 128], s3[:, kc, 0:512]) for kc in range(8)], [sbuf_, hT])
                    evac_copy(vb[:, blk, p * 512:(p + 1) * 512], vb, pb_, pb_[:, 0:512], blk)
            set_rot([PA, PBk])
            scm = cst.t[:, O_SCP:O_SCP + NT] if kind == "p" else cst.t[:, O_SCS:O_SCS + NT]
            hmk = cst.t[:, O_HMP:O_HMP + 128] if kind == "p" else cst.t[:, O_HMS:O_HMS + 128]
            mid = L // 2
            nsb = 128 // L

            def hgrn_head(h, hh, slot, YB, sq_b, sq3, sf_b, sf3, so_b, so3):
                fA, fB, fC, fD, fE, fF, fG, fH = hs_f[slot]
                qe, qst, ke, atm0, atm1, o2 = hs_b[slot]
                dc, kT = dec[slot], kstT[slot]
                wcols = slice(hh * 128, hh * 128 + 128)
                c3 = lambda ap_: ap_.rearrange("p (c l) -> p c l", l=L)
                pf = pany()
                mm_group(pf, pf[:, 0:NT], [(sf3[:, kc, wcols], hT[:, kc, 0:NT]) for kc in range(8)], [sf_b, hT])
                R.op("act", lambda e: e.activation(out=fB[:, 0:NT], in_=pf[:, 0:NT], func=AF.Tanh, scale=0.5), reads=[pf], writes=[fB])
                yield
                pq = pany()
                mm_group(pq, pq[:, 0:NT], [(sq3[:, kc, wcols], hT[:, kc, 0:NT]) for kc in range(8)], [sq_b, hT])
                R.op("act", lambda e: e.activation(out=fF[:, 0:NT], in_=pq[:, 0:NT], func=AF.Tanh, scale=0.5), reads=[pq], writes=[fF])
                R.op("dve", lambda e: e.scalar_tensor_tensor(out=fF[:, 0:NT], in0=fF[:, 0:NT], scalar=1.0, in1=pq[:, 0:NT],
                                                             op0=ALU.add, op1=ALU.mult), reads=[fF, pq], writes=[fF])
                yield
                po = pany()
                mm_group(po, po[:, 0:NT], [(so3[:, kc, wcols], hT[:, kc, 0:NT]) for kc in range(8)], [so_b, hT])
                R.op("act", lambda e: e.activation(out=fH[:, 0:NT], in_=po[:, 0:NT], func=AF.Tanh, scale=0.5), reads=[po], writes=[fH])
                R.op("dve", lambda e: e.scalar_tensor_tensor(out=fH[:, 0:NT], in0=fH[:, 0:NT], scalar=1.0, in1=po[:, 0:NT],
                                                             op0=ALU.add, op1=ALU.mult), reads=[fH, po], writes=[fH])
                yield
                R.op("act", lambda e: e.activation(out=fA[:, 0:NT], in_=fB[:, 0:NT], func=AF.Ln, scale=pv2[:, 24 + h:25 + h],
                                                   bias=pv2[:, 40 + h:41 + h]), reads=[fB, pv2], writes=[fA])
                yield
                R.op("dve", lambda e: e.tensor_tensor_scan(out=fC[:, 0:NT], data0=scm, data1=fA[:, 0:NT], initial=0.0,
                                                           op0=ALU.mult, op1=ALU.add), reads=[cst, fA], writes=[fC])
                yield
                R.op("act", lambda e: e.activation(out=fD[:, 0:NT], in_=fC[:, 0:NT], func=AF.Exp), reads=[fC], writes=[fD])
                R.op("dve", lambda e: e.tensor_tensor(out=c3(fG[:, 0:NT]), in0=c3(fC[:, 0:NT]), in1=bcast_last(c3(fC[:, 0:NT])[:, :, mid], L),
                                                      op=ALU.subtract), reads=[fC], writes=[fG])
                yield
                R.op("dve", lambda e: e.tensor_copy(out=dc[:, 0:nch], in_=c3(fD[:, 0:NT])[:, :, L - 1]), reads=[fD], writes=[dc])
                R.op("act", lambda e: e.activation(out=fE[:, 0:NT], in_=fG[:, 0:NT], func=AF.Exp), reads=[fG], writes=[fE])
                R.op("act", lambda e: e.activation(out=fC[:, 0:NT], in_=fG[:, 0:NT], func=AF.Exp, scale=-1.0), reads=[fG], writes=[fC])
                R.op("dve", lambda e: e.tensor_scalar(out=fA[:, 0:NT], in0=fB[:, 0:NT], scalar1=pv2[:, 32 + h:33 + h], scalar2=pv2[:, 24 + h:25 + h],
                                                      op0=ALU.mult, op1=ALU.add), reads=[fB, pv2], writes=[fA])
                yield
                R.op("dve", lambda e: e.tensor_tensor(out=qe[:, 0:NT], in0=fF[:, 0:NT], in1=fE[:, 0:NT], op=ALU.mult), reads=[fF, fE], writes=[qe])
                R.op("pool", lambda e: e.tensor_tensor(out=qst[:, 0:NT], in0=fF[:, 0:NT], in1=fD[:, 0:NT], op=ALU.mult), reads=[fF, fD], writes=[qst])
                R.op("dve", lambda e: e.tensor_tensor(out=ke[:, 0:NT], in0=fA[:, 0:NT], in1=fC[:, 0:NT], op=ALU.mult), reads=[fA, fC], writes=[ke])
                R.op("dve", lambda e: e.tensor_tensor(out=c3(fG[:, 0:NT]), in0=c3(ke[:, 0:NT]), in1=bcast_last(c3(fE[:, 0:NT])[:, :, L - 1], L),
                                                      op=ALU.mult), reads=[ke, fE], writes=[fG])
                yield
                pb_, pap = ptr()
                for blk in range(nb):
                    R.op("pe", lambda e, blk=blk: e.transpose(out=pap[:, blk * 128:(blk + 1) * 128], in_=fG[:, blk * 128:(blk + 1) * 128],
                                                              identity=ident_f), reads=[fG, cst], writes=[pb_], inc=(blk == nb - 1), skip_self=True)
                sg0 = O_SEG + (0 if nsb == 2 else 2)
                for blk in range(nb):
                    R.op("dve", lambda e, blk=blk: e.tensor_tensor(
                        out=kT[:, blk * nsb:(blk + 1) * nsb, :], in0=bcast_mid(pap[:, blk * 128:(blk + 1) * 128], nsb),
                        in1=bcast_last(cst[:, sg0:sg0 + nsb], 128), op=ALU.mult), reads=[pb_, cst], writes=[kT])
                yield
                atms = [atm0, atm1]
                for blk in range(nb):
                    ps_ = pany()
                    mm_group(ps_, ps_[:, 0:128], [(ke[:, blk * 128:(blk + 1) * 128], qe[:, blk * 128:(blk + 1) * 128])], [ke, qe])
                    atm = atms[blk]
                    R.op("dve", lambda e, ps_=ps_, atm=atm: e.tensor_tensor(out=atm[:, 0:128], in0=ps_[:, 0:128], in1=hmk, op=ALU.mult),
                         reads=[ps_, cst], writes=[atm])
                    yield
                for blk in range(nb):
                    atm = atms[blk]
                    R.op("pe", lambda e, blk=blk, atm=atm: e.matmul(YB[:, blk * 128:(blk + 1) * 128], lhsT=vb[:, blk, h * 128:(h + 1) * 128],
                                                                    rhs=atm[:, 0:128], start=(blk == 0), stop=False, skip_group_check=True),
                         reads=[vb, atm], writes=[YB], inc=True, skip_self=True)
                yield
                if kind == "p":
                    stt = hs_st[slot]
                    R.op("pe", lambda e: e.matmul(YB[:, 0:L], lhsT=Sbf_t[:, h, :], rhs=qst[:, 0:L],
                                                  start=False, stop=True, skip_group_check=True),
                         reads=[Sbf[h], qst], writes=[YB], inc=False, skip_self=True)
                    PU = pany()
                    for cix in range(nch):
                        blk = (cix * L) // 128
                        s_ = (cix * L % 128) // L
                        R.op("pe", lambda e, cix=cix, blk=blk, s_=s_: e.matmul(PU[:, cix * 128:(cix + 1) * 128], lhsT=kT[:, blk * nsb + s_, :],
                                                                               rhs=vb[:, blk, h * 128:(h + 1) * 128], start=True, stop=True),
                             reads=[kT, vb], writes=[PU], inc=(cix == nch - 1), skip_self=True)
                    scr = [(fB, fB[:, 0:128]), (fC, fC[:, 0:128]), (fE, fE[:, 0:128]), (S32[h], S32_t[:, h, :])]
                    prev_b, prev_ap = S32[h], S32_t[:, h, :]
                    for cix in range(nch):
                        last = (cix == nch - 1)
                        ob, oap = scr[3] if last else scr[cix]
                        R.op("dve", lambda e, cix=cix, oap=oap, prev_ap=prev_ap: e.scalar_tensor_tensor(
                            out=oap, in0=prev_ap, scalar=dc[:, cix:cix + 1], in1=PU[:, cix * 128:(cix + 1) * 128],
                            op0=ALU.mult, op1=ALU.add), reads=[prev_b, dc, PU], writes=[ob])
                        dstb = Sbf[h] if last else stt
                        dst_ap = Sbf_t[:, h, :] if last else stt[:, cix, :]
                        R.op("act", lambda e, oap=oap, dst_ap=dst_ap: e.copy(out=dst_ap, in_=oap), reads=[ob], writes=[dstb])
                        prev_b, prev_ap = ob, oap
                    yield
                    for cix in range(1, nch):
                        cols = slice(cix * L, (cix + 1) * L)
                        R.op("pe", lambda e, cix=cix, cols=cols: e.matmul(YB[:, cols], lhsT=stt[:, cix - 1, :], rhs=qst[:, cols],
                                                                          start=False, stop=True, skip_group_check=True),
                             reads=[stt, qst], writes=[YB], inc=(cix == nch - 1), skip_self=True)
                    yield
                else:
                    for cix in range(nch):
                        blk = (cix * L) // 128
                        s_ = (cix * L % 128) // L
                        cols = slice(cix * L, (cix + 1) * L)
                        R.dma("sp", S32_t[:, h, :], sh[cix, h], writes=[S32[h]])
                        R.op("act", lambda e: e.copy(out=Sbf_t[:, h, :], in_=S32_t[:, h, :]), reads=[S32[h]], writes=[Sbf[h]])
                        R.op("pe", lambda e, cols=cols: e.matmul(YB[:, cols], lhsT=Sbf_t[:, h, :], rhs=qst[:, cols],
                                                                 start=False, stop=True, skip_group_check=True),
                             reads=[Sbf[h], qst], writes=[YB], inc=True, skip_self=True)
                        PX = pany()
                        mm_group(PX, PX[:, 0:128], [(kT[:, blk * nsb + s_, :], vb[:, blk, h * 128:(h + 1) * 128])], [kT, vb])
                        R.op("dve", lambda e, cix=cix, PX=PX: e.scalar_tensor_tensor(out=S32_t[:, h, :], in0=S32_t[:, h, :],
                                                                                    scalar=dc[:, cix:cix + 1], in1=PX[:, 0:128],
                                                                                    op0=ALU.mult, op1=ALU.add),
                             reads=[S32[h], dc, PX], writes=[S32[h]])
                        R.dma("sp", o_ss[cix, h], S32_t[:, h, :], reads=[S32[h]], is_output=True)
                        yield
                R.op("act", lambda e: e.activation(out=o2[:, 0:NT], in_=YB[:, 0:NT], func=AF.Square), reads=[YB], writes=[o2])
                pd_ = pany()
                mm_group(pd_, pd_[:, 0:NT], [(ones_bf, o2[:, 0:NT])], [cbf, o2])
                R.op("act", lambda e: e.activation(out=fA[:, 0:NT], in_=pd_[:, 0:NT], func=AF.Ln, scale=1.0 / 128, bias=pv2[:, 49:50]),
                     reads=[pd_, pv2], writes=[fA])
                R.op("act", lambda e: e.activation(out=fA[:, 0:NT], in_=fA[:, 0:NT], func=AF.Exp, scale=-0.5), reads=[fA], writes=[fA])
                yield
                R.op("dve", lambda e: e.scalar_tensor_tensor(out=fE[:, 0:NT], in0=YB[:, 0:NT], scalar=pv2[:, 48:49], in1=fA[:, 0:NT],
                                                             op0=ALU.mult, op1=ALU.mult), reads=[YB, pv2, fA], writes=[fE])
                R.op("dve", lambda e: e.tensor_tensor(out=yb[:, h, 0:NT], in0=fE[:, 0:NT], in1=fH[:, 0:NT], op=ALU.mult),
                     reads=[fE, fH], writes=[yb])
                yield

            def zip_gen(gens):
                active = list(gens)
                while active:
                    for g_ in list(active):
                        try:
                            next(g_)
                        except StopIteration:
                            active.remove(g_)
                    yield

            def hgrn_all():
                for hg in range(2):
                    q_i, f_i, o_i = PI["hg"][hg]
                    sq_b, sq3 = getp(ti_glob, q_i, lo=q_i)
                    sf_b, sf3 = getp(ti_glob, f_i, lo=q_i)
                    so_b, so3 = getp(ti_glob, o_i, lo=q_i)
                    for pi_ in range(2):
                        yield from zip_gen([hgrn_head(4 * hg + 2 * pi_ + s_, 2 * pi_ + s_, s_, (PY, PD)[s_], sq_b, sq3, sf_b, sf3, so_b, so3)
                                            for s_ in range(2)])

            run_zip([attn_stream(range(8), PT0, PT1), hgrn_all()])
            if kind == "p" and not last_t:
                R.op("pool", lambda e: e.tensor_copy(out=kz_t[:, :, 0:128], in_=kz_t[:, :, T:T + 128]), reads=[kz_n], writes=[kz_c])
                R.op("pool", lambda e: e.tensor_copy(out=vz_t[:, 0, :, :], in_=vz_t[:, 2, :, :]), reads=[vz_n], writes=[vz_c])
            if last_t:
                R.dma("sp", o_ps[b].rearrange("h d v -> d h v"), S32_t[:, :, :], reads=S32, dsem=S32[0].dsem, is_output=True)
            set_rot(pbank)


            if DBG["stage"] <= 5:
                return
            for nm, dst in (("ga", sga), ("gb", sgb)):
                for p in range(2):
                    sbuf_, s3 = getp(ti_glob, PI[nm][p])
                    fm_proj(sbuf_, s3, 4, hT, NT, lambda cc, pb_, pap, p=p, dst=dst: R.op(
                        "act", lambda e: e.activation(out=dst[:, 4 * p + cc, 0:NT], in_=pap, func=AF.Sigmoid), reads=[pb_], writes=[dst]))
            for p in range(2):
                sbuf_, s3 = getp(ti_glob, PI["bra"][p])
                fm_proj(sbuf_, s3, 4, ya, NT, lambda cc, pb_, pap, p=p: R.op(
                    "dve", lambda e: e.tensor_tensor(out=tA[:, 4 * p + cc, 0:NT], in0=pap, in1=sga[:, 4 * p + cc, 0:NT], op=ALU.mult),
                    reads=[pb_, sga], writes=[tA]))
            for p in range(2):
                sbuf_, s3 = getp(ti_glob, PI["brb"][p])

                def cons(cc, pb_, pap, p=p):
                    t3 = tf()
                    R.op("dve", lambda e: e.tensor_tensor(out=t3[:, 0:NT], in0=pap, in1=sgb[:, 4 * p + cc, 0:NT], op=ALU.mult),
                         reads=[pb_, sgb], writes=[t3])
                    R.op("pool", lambda e: e.tensor_tensor(out=mixT[:, 4 * p + cc, 0:NT], in0=t3[:, 0:NT], in1=tA[:, 4 * p + cc, 0:NT], op=ALU.add),
                         reads=[t3, tA], writes=[mixT])
                fm_proj(sbuf_, s3, 4, yb, NT, cons)

            def tok_proj_norm(pkey, lhs_buf, kgroups, gidx):
                banks = {(0, 0): PA, (0, 1): PBk, (1, 0): PS0, (1, 1): PS1}
                for p in range(2):
                    for gi, (k0, nk) in enumerate(kgroups):
                        sbuf_, s3 = getp(ti_glob, PI[pkey][p][gi] if isinstance(PI[pkey][p], list) else PI[pkey][p])
                        for blk in range(nb):
                            bk = banks[(blk, p)]
                            mm_group(bk, bk[:, 0:512], [(lhs_buf[:, k0 + kc, blk * 128:(blk + 1) * 128], s3[:, kc, 0:512]) for kc in range(nk)],
                                     [sbuf_, lhs_buf], start=(gi == 0), stop=(gi == len(kgroups) - 1))
                    for blk in range(nb):
                        sq_accum(banks[(blk, p)][:, 0:512], banks[(blk, p)], blk * 2 + p)
                for blk in range(nb):
                    R.op("dve", lambda e, blk=blk: e.tensor_tensor(out=ss[:, blk * 2:blk * 2 + 1], in0=ss[:, blk * 2:blk * 2 + 1],
                                                                    in1=ss[:, blk * 2 + 1:blk * 2 + 2], op=ALU.add),
                         reads=[ss], writes=[ss])
                rstd_from_ss(nb, 2)
                for blk in range(nb):
                    for p in range(2):
                        bk = banks[(blk, p)]
                        t4 = tw()
                        R.op("dve", lambda e, bk=bk, blk=blk, p=p, t4=t4: e.scalar_tensor_tensor(
                            out=t4[:, 0:512], in0=bk[:, 0:512], scalar=rstd[:, blk:blk + 1], in1=gbc[:, gidx, p * 512:(p + 1) * 512],
                            op0=ALU.mult, op1=ALU.mult), reads=[bk, rstd, gbc], writes=[t4])
                        eng = "pool" if p == 0 else "dve"
                        R.op(eng, lambda e, blk=blk, p=p, t4=t4: e.tensor_tensor(out=xt[:, blk, p * 512:(p + 1) * 512],
                                                                                in0=xt[:, blk, p * 512:(p + 1) * 512], in1=t4[:, 0:512], op=ALU.add),
                             reads=[xt, t4], writes=[xt])

            tok_proj_norm("out", mixT, [(0, 8)], 0)

            if DBG["stage"] <= 6:
                return
            to_feature_major(xt, nb, P_G2, hT, True)
            ffn_pend = []
            for jg in range(6):
                pa_i, pu_i, ncc = PI["up"][jg]
                sa_b, sa3 = getp(ti_glob, pa_i, lo=pa_i)
                su_b, su3 = getp(ti_glob, pu_i, lo=pa_i)
                for cc in range(ncc):
                    j = 4 * jg + cc
                    wc = slice(cc * 128, cc * 128 + 128)
                    pa_ = (PA, PS0, PY, PT0)[j % 4]
                    pu_ = (PBk, PS1, PD, PT1)[j % 4]
                    mm_group(pa_, pa_[:, 0:NT], [(sa3[:, kc, wc], hT[:, kc, 0:NT]) for kc in range(8)], [sa_b, hT])
                    mm_group(pu_, pu_[:, 0:NT], [(su3[:, kc, wc], hT[:, kc, 0:NT]) for kc in range(8)], [su_b, hT])
                    rot["a"] = (rot["a"] + 1) % len(aext)
                    ax = aext[rot["a"]]
                    ax3 = ax.t[:, 0:nseg * (Ls + 2)].rearrange("p (s l) -> p s l", l=Ls + 2)
                    pa3 = pa_.t[:, 0:NT].rearrange("p (s l) -> p s l", l=Ls)
                    R.op("pool", lambda e, ax3=ax3, j=j: e.tensor_copy(out=ax3[:, :, 0:2], in_=carry[:, j, 0:nseg, :]), reads=[carry], writes=[ax])
                    R.op("act", lambda e, ax3=ax3, pa3=pa3: e.copy(out=ax3[:, :, 2:Ls + 2], in_=pa3), reads=[pa_], writes=[ax])
                    R.op("pool", lambda e, ax3=ax3, j=j: e.tensor_copy(out=carry[:, j, 0:nseg, :], in_=ax3[:, :, Ls:Ls + 2]), reads=[ax], writes=[carry])
                    t5 = tf()
                    wcol = P_WC + 3 * j
                    t53 = t5.t[:, 0:NT].rearrange("p (s l) -> p s l", l=Ls)
                    R.op("pool", lambda e, ax3=ax3, t53=t53, wcol=wcol, j=j: e.tensor_scalar(
                        out=t53, in0=ax3[:, :, 2:Ls + 2], scalar1=pv[:, wcol + 2:wcol + 3], scalar2=pv[:, P_BC + j:P_BC + j + 1],
                        op0=ALU.mult, op1=ALU.add), reads=[ax, pv], writes=[t5])
                    R.op("dve", lambda e, ax3=ax3, t53=t53, wcol=wcol: e.scalar_tensor_tensor(
                        out=t53, in0=ax3[:, :, 1:Ls + 1], scalar=pv[:, wcol + 1:wcol + 2], in1=t53, op0=ALU.mult, op1=ALU.add),
                        reads=[ax, pv, t5], writes=[t5])
                    R.op("dve", lambda e, ax3=ax3, t53=t53, wcol=wcol: e.scalar_tensor_tensor(
                        out=t53, in0=ax3[:, :, 0:Ls], scalar=pv[:, wcol:wcol + 1], in1=t53, op0=ALU.mult, op1=ALU.add),
                        reads=[ax, pv, t5], writes=[t5])
                    if ffn_pend:
                        ffn_pend.pop(0)()

                    def tail(t5=t5, pu_=pu_, j=j):
                        R.op("act", lambda e: e.activation(out=t5[:, 0:NT], in_=t5[:, 0:NT], func=AF.Gelu_apprx_tanh), reads=[t5], writes=[t5])
                        R.op("dve", lambda e: e.tensor_tensor(out=hm[:, j, 0:NT], in0=pu_[:, 0:NT], in1=t5[:, 0:NT], op=ALU.mult),
                             reads=[pu_, t5], writes=[hm])
                    ffn_pend.append(tail)
            while ffn_pend:
                ffn_pend.pop(0)()
            if kind == "s":
                R.dma("sp", o_sc, carry[:, :, :, :], reads=[carry], is_output=True)
            elif last_t:
                R.dma("sp", o_pc[:, :, b, :], carry[:, :, 0, :], reads=[carry], is_output=True)
            tok_proj_norm("down", hm, [(0, 8), (8, 8), (16, 6)], 1)

            if DBG["stage"] <= 7:
                return
            to_feature_major(xt, nb, 0, hT, False)
            for c2 in range(2):
                pb_, pap = ptr()
                for blk in range(nb):
                    R.op("pe", lambda e, c2=c2, blk=blk, pap=pap: e.transpose(out=pap[:, blk * 128:(blk + 1) * 128],
                                                                             in_=pe_tok[:, blk, c2 * 128:(c2 + 1) * 128], identity=ident_f),
                         reads=[pe_tok, cst], writes=[pb_], inc=(blk == nb - 1), skip_self=True)
                evac_copy(peT[:, c2, 0:NT], peT, pb_, pap[:, 0:NT], c2)
            sple_b, sple3 = getp(ti_glob, PI["ple"])
            for p in range(2):
                spg_b, spg3 = getp(ti_glob, PI["pg"][p], lo=PI["ple"])
                for blk in range(nb):
                    bg = PA if blk == 0 else PS0
                    bp = PBk if blk == 0 else PS1
                    mm_group(bg, bg[:, 0:512], [(hT[:, kc, blk * 128:(blk + 1) * 128], spg3[:, kc, 0:512]) for kc in range(8)], [spg_b, hT])
                    mm_group(bp, bp[:, 0:512], [(peT[:, k2, blk * 128:(blk + 1) * 128], sple3[:, k2, p * 512:(p + 1) * 512]) for k2 in range(2)],
                             [sple_b, peT])
                    t8, t9 = tw(), tw()
                    R.op("act", lambda e, bg=bg, t8=t8: e.activation(out=t8[:, 0:512], in_=bg[:, 0:512], func=AF.Sigmoid), reads=[bg], writes=[t8])
                    R.op("dve", lambda e, bp=bp, t8=t8, t9=t9: e.tensor_tensor(out=t9[:, 0:512], in0=bp[:, 0:512], in1=t8[:, 0:512], op=ALU.mult),
                         reads=[bp, t8], writes=[t9])
                    R.op("pool", lambda e, blk=blk, p=p, t9=t9: e.tensor_tensor(out=xt[:, blk, p * 512:(p + 1) * 512],
                                                                               in0=xt[:, blk, p * 512:(p + 1) * 512], in1=t9[:, 0:512], op=ALU.add),
                         reads=[xt, t9], writes=[xt])
            if kind == "p":
                t0_ = ti * T
                R.dma("sp", o_yp[b, t0_:t0_ + NT, :].rearrange("(n p) d -> p n d", p=128), xt[:, 0:nb, :], reads=[xt], is_output=True)
            else:
                R.dma("sp", o_ys, xt[:, 0, :], reads=[xt], is_output=True)

        tg = 0
        tl = [("p", b, ti) for b in range(PB) for ti in range(NTILE)] + [("s", 0, 0)]
        if DBG["tiles"] is not None:
            tl = DBG["tiles"]
        for ix, (kd, b, ti) in enumerate(tl):
            tile_body(tg, kd, b, ti, tl[ix + 1] if ix + 1 < len(tl) else None)
            tg += 1

        R.finish("sp")
        R.replay(nc, st)
        build_program.stats = {e: len(R.q[e]) for e in ENGS}
        build_program.sbuf_bytes = sbtot[0]
    return nc


_CACHE = {}


def kernel(x_prompt, x_sample, cache_win_k, cache_win_v, state_hgrn, cache_ffn_conv, p_prompt, p_sample,
           rel_bias_table, lb_logits, g_pre_mix, w_in, attn_sinks, g_hgrn_out, w_br_a, w_br_b, w_out, g_post_mix,
           g_pre_ffn, w_up, w_conv, b_conv, w_down, g_post_ffn, w_ple, w_ple_gate):
    f = lambda a: np.ascontiguousarray(np.asarray(a, dtype=np.float32))
    if "nc" not in _CACHE:
        _CACHE["nc"] = build_program()
    nc = _CACHE["nc"]
    ncores = _CACHE.get("ncores", NCORE)
    cst = _make_cst()
    tabp = np.zeros((128, 128), np.float32)
    tabp[0:32, 0:16] = f(rel_bias_table)
    pv = np.zeros((128, PV_W), np.float32)
    pv[:, P_G1:P_G1 + 8] = f(g_pre_mix)[0].reshape(8, 128).T
    pv[:, P_G2:P_G2 + 8] = f(g_pre_ffn)[0].reshape(8, 128).T
    pv[:, P_L0:P_L0 + 8] = f(lb_logits)[0].reshape(8, 128).T
    pv[:, P_L1:P_L1 + 8] = f(lb_logits)[1].reshape(8, 128).T
    pv[:, P_GH] = f(g_hgrn_out)[0]
    pv[:, P_WC:P_WC + 66] = f(w_conv)[0].reshape(3, NJ, 128).transpose(2, 1, 0).reshape(128, 66)
    pv[:, P_BC:P_BC + NJ] = f(b_conv)[0].reshape(NJ, 128).T
    sk = f(attn_sinks)[0]
    for c in range(8):
        pv[0:64, P_SK + c] = sk[2 * c]
        pv[64:128, P_SK + c] = sk[2 * c + 1]
    gbc = np.ascontiguousarray(np.broadcast_to(np.stack([f(g_post_mix)[0], f(g_post_ffn)[0]])[None], (128, 2, D)))
    shared = {"table_pad": tabp, "cst": cst, "pv": pv, "gbc": gbc, "w_in": f(w_in)[0], "w_br_a": f(w_br_a)[0],
              "w_br_b": f(w_br_b)[0], "w_out": f(w_out)[0], "w_up": f(w_up)[0], "w_down": f(w_down)[0],
              "w_ple": f(w_ple)[0], "w_ple_gate": f(w_ple_gate)[0]}
    xpf, xsf = f(x_prompt), f(x_sample)
    ckf, cvf = f(cache_win_k)[0].reshape(32, 128, 128), f(cache_win_v)[0].reshape(32, 128, 128)
    shf, cff = f(state_hgrn)[0], f(cache_ffn_conv)[0]
    ppf, psf = f(p_prompt)[0], f(p_sample)[0]
    in_maps = []
    for i in range(ncores):
        sl = slice(PB * i, PB * i + PB)
        m = dict(shared)
        m["x_prompt"] = xpf[sl]
        m["x_sample"] = np.ascontiguousarray(xsf[sl].reshape(PB * DSEQ, D))
        m["cache_k"] = ckf[sl]
        m["cache_v"] = cvf[sl]
        m["state_hgrn"] = shf[sl]
        m["cache_conv"] = np.ascontiguousarray(cff[sl].reshape(PB, 2, NJ, 128).transpose(3, 2, 0, 1))
        m["p_prompt"] = ppf[sl]
        m["p_sample"] = np.ascontiguousarray(psf[sl].reshape(PB * DSEQ, PLE))
        in_maps.append(m)
    res = run_bass_kernel_spmd(nc, in_maps, core_ids=list(range(ncores)))
    rs = res.results
    g = lambda k: [np.asarray(r[k], dtype=np.float32) for r in rs]
    y_p = np.concatenate(g("y_prompt"), 0)
    y_s = np.concatenate([a.reshape(PB, DSEQ, D) for a in g("y_sample")], 0)
    pk = np.concatenate(g("o_pk"), 0).reshape(1, -1, 128, 2, 64)
    pvv = np.concatenate(g("o_pv"), 0).reshape(1, -1, 128, 2, 64)
    ps = np.concatenate(g("o_ps"), 0)[None]
    cvt = lambda a: a.transpose(2, 3, 1, 0).reshape(PB, 2, DFF)
    pc = np.concatenate([cvt(a) for a in g("o_pc")], 0)[None]
    skk = np.concatenate([a.reshape(PB, DSEQ, 2, 64) for a in g("o_sk")], 0)[None]
    svv = np.concatenate([a.reshape(PB, DSEQ, 2, 64) for a in g("o_sv")], 0)[None]
    sss = np.concatenate(g("o_ss"), 0)[None]
    sc = np.concatenate([cvt(a) for a in g("o_sc")], 0)[None]
    return (y_p, y_s, pk, pvv, ps, pc, skk, svv, sss, sc)
```

```python
import math
from contextlib import ExitStack

import numpy as np
import concourse.bass as bass
import concourse.mybir as mybir
from concourse.bass_utils import run_bass_kernel_spmd

F32 = mybir.dt.float32
BF16 = mybir.dt.bfloat16
AF = mybir.ActivationFunctionType
ALU = mybir.AluOpType

ENGS = ("pe", "act", "dve", "pool", "sp")


class SemRef:
    def __init__(self, name):
        self.name = name
        self.handle = None
        self.count = 0


class Buf:
    def __init__(self, name, t=None, dsem=None):
        self.name = name
        self.t = t
        self.last_w = None
        self.readers = {}
        self.overlaps = []
        self.dsem = dsem

    def __getitem__(self, idx):
        return self.t[idx]


class Rec:
    def __init__(self):
        self.q = {e: [] for e in ENGS}
        self.esem = {e: SemRef("s_" + e) for e in ENGS}
        self.waited = {e: {} for e in ENGS}
        self.dsems = []
        self.out_marks = {}
        self.n_instr = 0

    def new_dsem(self, name):
        s = SemRef(name)
        self.dsems.append(s)
        return s

    def _deps(self, reads, writes):
        deps = {}

        def add(d):
            if d is None:
                return
            s, v = d
            if deps.get(s, -1) < v:
                deps[s] = v

        for b in reads:
            for bb in [b] + b.overlaps:
                add(bb.last_w)
        for b in writes:
            for bb in [b] + b.overlaps:
                add(bb.last_w)
                for s, v in bb.readers.items():
                    add((s, v))
        return deps

    def _waits(self, eng, deps, skip_self=False):
        ws = []
        wd = self.waited[eng]
        for s, v in deps.items():
            if skip_self and s is self.esem[eng]:
                continue
            if wd.get(s, 0) >= v:
                continue
            wd[s] = v
            ws.append((s, v))
        return ws

    def op(self, eng, fn, reads=(), writes=(), inc=True, skip_self=False):
        reads = list(reads)
        writes = list(writes)
        deps = self._deps(reads, writes)
        ws = self._waits(eng, deps, skip_self)
        sem = self.esem[eng]
        if inc:
            sem.count += 1
        mark = (sem, sem.count if inc else sem.count + 1)
        self.q[eng].append((ws, fn, sem if inc else None, 1))
        for b in reads:
            if b.readers.get(sem, 0) < mark[1]:
                b.readers[sem] = mark[1]
        for b in writes:
            b.last_w = mark
            b.readers = {}
        self.n_instr += 1
        return mark

    def dma(self, eng, out_ap, in_ap, reads=(), writes=(), dsem=None, is_output=False, nodeps=False):
        reads = list(reads)
        writes = list(writes)
        if dsem is None:
            for b in writes + reads:
                if b.dsem is not None:
                    dsem = b.dsem
                    break
        assert dsem is not None
        deps = {} if nodeps else self._deps(reads, writes)
        ws = self._waits(eng, deps)
        dsem.count += 16
        mark = (dsem, dsem.count)

        def fn(e, out_ap=out_ap, in_ap=in_ap):
            return e.dma_start(out=out_ap, in_=in_ap)

        self.q[eng].append((ws, fn, dsem, 16))
        for b in reads:
            b.readers[dsem] = dsem.count
        for b in writes:
            b.last_w = mark
            b.readers = {}
        if is_output:
            self.out_marks[dsem] = dsem.count
        self.n_instr += 1
        return mark

    def barrier(self):
        sems = [s for s in list(self.esem.values()) + self.dsems if s.count > 0]
        for e in ENGS:
            ws = self._waits(e, {s: s.count for s in sems})
            if ws:
                self.q[e].append((ws, None, None, 0))

    def finish(self, eng="sp"):
        ws = [(s, v) for s, v in self.out_marks.items()]
        ws += [(s, s.count) for s in self.esem.values() if s.count > 0]
        self.q[eng].append((ws, None, None, 0))

    def replay(self, nc, stack):
        for s in list(self.esem.values()) + self.dsems:
            s.handle = stack.enter_context(nc.semaphore(s.name))
        block = stack.enter_context(nc.Block())
        decos = {"pe": block.tensor, "act": block.scalar, "dve": block.vector,
                 "pool": block.gpsimd, "sp": block.sync}
        for en in ENGS:
            lst = self.q[en]

            def body(e, lst=lst):
                for ws, fn, sem, inc in lst:
                    for s, v in ws:
                        e.wait_ge(s.handle, v)
                    if fn is None:
                        continue
                    ins = fn(e)
                    if sem is not None:
                        ins.then_inc(sem.handle, inc)

            decos[en](body)


D = 1024
NCORE = 8
PB = 4
SEQ = 2048
T = 256
NTILE = SEQ // T
DSEQ = 32
DFF = 2816
NJ = 22
INC = 7424
PLE = 256
EPS = 1e-6
NEG = -30000.0
C_QA, C_KA, C_VA, C_QB, C_FB, C_IB, C_OB, C_GA, C_GB = 0, 1024, 1152, 1280, 2304, 3328, 4352, 5376, 6400

WSHAPES = {"w_in": (D, INC), "w_br_a": (D, D), "w_br_b": (D, D), "w_out": (D, D), "w_up": (D, 2 * DFF),
           "w_down": (DFF, D), "w_ple": (PLE, D), "w_ple_gate": (D, D)}

O_ID, O_OH, O_MA, O_MAS, O_HMP, O_HMS, O_SCP, O_SCS, O_SEG, O_EPS, O_ONELH, O_ONE = (
    0, 128, 512, 768, 896, 1024, 1152, 1408, 1536, 1542, 1543, 1799)
O_J = 1927
CST_W = 2055
P_G1, P_G2, P_L0, P_L1, P_GH, P_WC, P_BC, P_SK = 0, 8, 16, 24, 32, 33, 99, 121
PV_W = 129


def _t5_bucket(rel):
    nb = 16
    ret = np.where(rel > 0, nb, 0)
    n = np.abs(rel)
    max_exact = nb // 2
    large = max_exact + (np.log(np.maximum(n, max_exact).astype(np.float32) / max_exact)
                         / math.log(128 / max_exact) * (nb - max_exact)).astype(np.int32)
    large = np.minimum(large, nb - 1)
    return ret + np.where(n < max_exact, n, large)


def _make_cst():
    c = np.zeros((128, CST_W), np.float32)
    c[:, O_ID:O_ID + 128] = np.eye(128, dtype=np.float32)
    j = np.arange(384)
    bk = _t5_bucket(127 - j)
    c[bk, O_OH + j] = 1.0
    p = np.arange(128)[:, None]
    col = np.arange(256)[None, :]
    hh = p // 64
    dd = col // 64
    valid = ((dd - hh) >= 0) & ((dd - hh) <= 2)
    c[:, O_MA:O_MA + 256] = np.where(valid, 0.0, NEG)
    col1 = np.arange(128)[None, :]
    c[:, O_MAS:O_MAS + 128] = np.where((p // 32) == (col1 // 32), 0.0, NEG)
    c[:, O_HMP:O_HMP + 128] = ((p // 64) == (col1 // 64)) & (p <= col1)
    c[:, O_HMS:O_HMS + 128] = ((p // 32) == (col1 // 32)) & (p <= col1)
    c[:, O_SCP:O_SCP + 256] = (np.arange(256) % 64 != 0)[None, :]
    c[:, O_SCS:O_SCS + 128] = (np.arange(128) % 32 != 0)[None, :]
    c[:, O_SEG + 0] = (p[:, 0] < 64)
    c[:, O_SEG + 1] = (p[:, 0] >= 64)
    for s in range(4):
        c[:, O_SEG + 2 + s] = (p[:, 0] // 32 == s)
    c[:, O_EPS] = EPS
    c[:, O_ONELH:O_ONELH + 64] = 1.0
    c[:, O_ONELH + 128 + 64:O_ONELH + 256] = 1.0
    c[:, O_ONE:O_ONE + 128] = 1.0
    c[np.arange(128), O_J + 127 - np.arange(128)] = 1.0
    return c


DBG = {"tiles": None, "stage": 99, "setup": 99, "sub": 99}


def build_program():
    nc = bass.Bass("TRN2", target_bir_lowering=False)
    R = Rec()

    def din(name, shape, dt=F32):
        return nc.dram_tensor(name, list(shape), dt, kind="ExternalInput")

    def dout(name, shape, dt=F32):
        return nc.dram_tensor(name, list(shape), dt, kind="ExternalOutput")

    xp = din("x_prompt", [PB, SEQ, D]).ap()
    xs = din("x_sample", [PB * DSEQ, D]).ap()
    ck = din("cache_k", [PB, 128, 128]).ap()
    cv = din("cache_v", [PB, 128, 128]).ap()
    sh = din("state_hgrn", [PB, 8, 128, 128]).ap()
    cfc = din("cache_conv", [128, NJ, PB, 2]).ap()
    pp = din("p_prompt", [PB, SEQ, PLE]).ap()
    psm = din("p_sample", [PB * DSEQ, PLE]).ap()
    tabp = din("table_pad", [128, 128]).ap()
    cst_d = din("cst", [128, CST_W]).ap()
    pv_d = din("pv", [128, PV_W]).ap()
    gbc_d = din("gbc", [128, 2, D]).ap()
    wsrc = {k: din(k, list(v)).ap() for k, v in WSHAPES.items()}

    o_yp = dout("y_prompt", [PB, SEQ, D]).ap()
    o_ys = dout("y_sample", [PB * DSEQ, D]).ap()
    o_pk = dout("o_pk", [PB, 128, 128]).ap()
    o_pvv = dout("o_pv", [PB, 128, 128]).ap()
    o_ps = dout("o_ps", [PB, 8, 128, 128]).ap()
    o_pc = dout("o_pc", [128, NJ, PB, 2]).ap()
    o_sk = dout("o_sk", [PB * DSEQ, 128]).ap()
    o_sv = dout("o_sv", [PB * DSEQ, 128]).ap()
    o_ss = dout("o_ss", [PB, 8, 128, 128]).ap()
    o_sc = dout("o_sc", [128, NJ, PB, 2]).ap()

    wb_t = {k: nc.dram_tensor(k + "_bf", list(v), BF16, kind="Internal") for k, v in WSHAPES.items()}
    trev_t = nc.dram_tensor("trev", [16, 384], F32, kind="Internal")

    with ExitStack() as st:
        sbtot = [0]

        def sb(name, shape, dt):
            n_ = 1
            for d_ in shape[1:]:
                n_ *= d_
            sbtot[0] += n_ * (4 if dt == F32 else 2)
            return st.enter_context(nc.sbuf_tensor("sb_" + name, list(shape), dt))

        def pst(name, shape, dt):
            return st.enter_context(nc.psum_tensor("ps_" + name, list(shape), dt))

        def B(name, shape, dt, dma=False):
            return Buf(name, sb(name, shape, dt), R.new_dsem("d_" + name) if dma else None)

        xtbufs = [B("xt%d" % i, [128, 2, D], F32, dma=True) for i in range(2)]
        xt = xtbufs[0]
        hn = B("hn", [128, 2, D], F32)
        hT = B("hT", [128, 8, T], BF16)
        qT = B("qT", [128, 8, T], BF16)
        kz_t = sb("kz", [128, 4, 128 + T], BF16)
        kz_c = Buf("kz_c", kz_t)
        kz_n = Buf("kz_n", kz_t)
        vz_t = sb("vz", [128, 3, 4, 128], BF16)
        vz_c = Buf("vz_c", vz_t)
        vz_n = Buf("vz_n", vz_t)
        pe_tok = B("pe_tok", [128, 2, PLE], F32, dma=True)
        peT = B("peT", [128, 2, T], BF16)
        ya = B("ya", [128, 8, T], BF16)
        yb = B("yb", [128, 8, T], BF16)
        mixT = B("mixT", [128, 8, T], BF16)
        sga = B("sga", [128, 8, T], BF16)
        sgb = B("sgb", [128, 8, T], BF16)
        tA = B("tA", [128, 8, T], BF16)
        hm = B("hm", [128, NJ, T], BF16)
        vb = B("vb", [128, 2, D], BF16)
        S32_t = sb("S32", [128, 8, 128], F32)
        Sbf_t = sb("Sbf", [128, 8, 128], BF16)
        S32 = [Buf("S32_%d" % i, S32_t, R.new_dsem("d_S32_%d" % i)) for i in range(8)]
        Sbf = [Buf("Sbf_%d" % i, Sbf_t) for i in range(8)]
        carry = B("carry", [128, NJ, 4, 2], F32, dma=True)
        BT = B("BT", [128, 16, 256], BF16)
        BTs = B("BTs", [128, 16, 128], BF16)
        gbc = B("gbc", [128, 2, D], F32, dma=True)
        cst = B("cst", [128, CST_W], F32, dma=True)
        cbf = B("cbf", [128, 512], BF16)
        pv = B("pv", [128, PV_W], F32, dma=True)
        pv2 = B("pv2", [128, 56], F32)
        tab = B("tab", [128, 128], F32, dma=True)
        trs = B("trs", [128, 384], F32, dma=True)
        wkz = B("wkz", [128, 4, 8, 128], BF16, dma=True)
        kvout = [B("kvout%d" % i, [128, 256], F32, dma=True) for i in range(2)]
        wslot = [B("wslot%d" % i, [128, 4096], BF16, dma=True) for i in range(4)]
        ss = B("ss", [128, 8], F32)
        rstd = B("rstd", [128, 8], F32)
        f32t = [B("f32t%d" % i, [128, 256], F32) for i in range(6)]
        hs_f = [[B("hsf%d_%d" % (s_, i), [128, 256], F32) for i in range(8)] for s_ in range(2)]
        hs_b = [[B("hsb%d_%d" % (s_, i), [128, 256], BF16) for i in range(6)] for s_ in range(2)]
        hs_md = [B("hsmd%d" % s_, [128, 8], F32) for s_ in range(2)]
        hs_st = [B("hsst%d" % s_, [128, 3, 128], BF16) for s_ in range(2)]
        f32w = [B("f32w%d" % i, [128, 512], F32, dma=True) for i in range(3)]
        bf16t = [B("bf16t%d" % i, [128, 256], BF16) for i in range(6)]
        aext = [B("aext%d" % i, [128, T + 8], F32) for i in range(3)]
        kstT = [B("kstT%d" % i, [128, 4, 128], BF16) for i in range(2)]
        dec = [B("dec%d" % i, [128, 8], F32) for i in range(2)]
        rot = {"f": 0, "b": 0, "a": 0, "k": 0, "d": 0, "kv": 0, "w": 0}
        junk_ap = hn.t[:, 0, :]
        kcz_v = mixT.t[:, :, :].rearrange("p a (b n) -> p (a b) n", n=128).rearrange("p (i v) n -> p i v n", v=4)
        vcz_v = sga.t[:, :, :].rearrange("p a (b n) -> p (a b) n", n=128).rearrange("p (i v) n -> p i v n", v=4)

        def tw():
            rot["w"] = (rot["w"] + 1) % len(f32w)
            return f32w[rot["w"]]

        def tf():
            rot["f"] = (rot["f"] + 1) % len(f32t)
            return f32t[rot["f"]]

        def tb():
            rot["b"] = (rot["b"] + 1) % len(bf16t)
            return bf16t[rot["b"]]

        pbank = [Buf("pb%d" % i, pst("pb%d" % i, [128, 512], F32)) for i in range(8)]
        PA, PBk, PS0, PS1, PY, PD, PT0, PT1 = pbank
        pT = [PT0, PT1]
        rotp = {"i": 0, "set": list(pbank)}

        def set_rot(banks):
            rotp["set"] = list(banks)

        def pany():
            rotp["i"] = (rotp["i"] + 1) % len(rotp["set"])
            return rotp["set"][rotp["i"]]

        pmm = pany
        pscore = pany

        def ptr():
            bk = pany()
            return bk, bk.t[:, 0:256]

        def run_zip(gens):
            active = list(gens)
            while active:
                for g_ in list(active):
                    try:
                        next(g_)
                    except StopIteration:
                        active.remove(g_)

        ident = cbf.t[:, 0:128]
        ones_lo = cbf.t[:, 128:256]
        ones_hi = cbf.t[:, 256:384]
        ones_bf = cbf.t[:, 384:512]
        ident_f = cst.t[:, O_ID:O_ID + 128]
        eps_col = cst.t[:, O_EPS:O_EPS + 1]

        R.dma("sp", cst[:, :], cst_d, writes=[cst])
        R.dma("sp", pv[:, :], pv_d, writes=[pv])
        R.dma("sp", gbc[:, :, :], gbc_d, writes=[gbc])
        R.dma("sp", tab[:, :], tabp, writes=[tab])
        R.op("dve", lambda e: e.tensor_copy(out=cbf[:, 0:128], in_=cst[:, O_ID:O_ID + 128]), reads=[cst], writes=[cbf])
        R.op("dve", lambda e: e.tensor_copy(out=cbf[:, 128:384], in_=cst[:, O_ONELH:O_ONELH + 256]), reads=[cst], writes=[cbf])
        R.op("dve", lambda e: e.tensor_copy(out=cbf[:, 384:512], in_=cst[:, O_ONE:O_ONE + 128]), reads=[cst], writes=[cbf])
        t0 = tf()
        R.op("dve", lambda e: e.tensor_tensor(out=t0[:, 0:8], in0=pv[:, P_L0:P_L0 + 8], in1=pv[:, P_L1:P_L1 + 8], op=ALU.subtract),
             reads=[pv], writes=[t0])
        R.op("act", lambda e: e.activation(out=pv2[:, 0:8], in_=t0[:, 0:8], func=AF.Sigmoid), reads=[t0], writes=[pv2])
        R.op("act", lambda e: e.activation(out=pv2[:, 8:16], in_=t0[:, 0:8], func=AF.Sigmoid, scale=-1.0), reads=[t0], writes=[pv2])
        R.op("act", lambda e: e.activation(out=pv2[:, 16:24], in_=pv[:, P_SK:P_SK + 8], func=AF.Exp), reads=[pv], writes=[pv2])
        R.op("dve", lambda e: e.tensor_scalar(out=pv2[:, 24:32], in0=pv2[:, 8:16], scalar1=0.5, scalar2=None, op0=ALU.mult), reads=[pv2], writes=[pv2])
        R.op("dve", lambda e: e.tensor_scalar(out=pv2[:, 32:40], in0=pv2[:, 8:16], scalar1=-0.5, scalar2=None, op0=ALU.mult), reads=[pv2], writes=[pv2])
        R.op("dve", lambda e: e.tensor_tensor(out=pv2[:, 40:48], in0=pv2[:, 24:32], in1=pv2[:, 0:8], op=ALU.add), reads=[pv2], writes=[pv2])
        R.op("dve", lambda e: e.tensor_scalar(out=pv2[:, 48:49], in0=pv[:, P_GH:P_GH + 1], scalar1=0.5, scalar2=None, op0=ALU.mult), reads=[pv], writes=[pv2])
        R.op("dve", lambda e: e.tensor_scalar(out=pv2[:, 49:50], in0=cst[:, O_EPS:O_EPS + 1], scalar1=4.0, scalar2=None, op0=ALU.mult), reads=[cst], writes=[pv2])
        R.op("pe", lambda e: e.matmul(PA[:, 0:384], lhsT=tab[:, :], rhs=cst[:, O_OH:O_OH + 384], start=True, stop=True),
             reads=[tab, cst], writes=[PA], skip_self=True)
        R.op("dve", lambda e: e.tensor_copy(out=trs[:, :], in_=PA[:, 0:384]), reads=[PA], writes=[trs])
        d_trev = Buf("trev", trev_t, R.new_dsem("d_trev"))
        R.dma("sp", trev_t.ap()[:, :], trs[0:16, :], reads=[trs], writes=[d_trev])
        btfs = [xb_.t[:, :, :].rearrange("p a (h c) -> p (a h) c", c=256) for xb_ in xtbufs]
        for half in range(2):
            src = bass.AP(trev_t, half * 8 * 384, [[1, 128], [384, 8], [1, 256]])
            R.dma("sp", btfs[half][:, :, :], src, reads=[d_trev], writes=[xtbufs[half]])

        def emit_bias_tiles():
            for half in range(2):
                for h8 in range(8):
                    h = half * 8 + h8
                    pb_ = pany()
                    R.op("pe", lambda e, pb_=pb_, h8=h8, half=half: e.matmul(pb_[:, 0:256], lhsT=cst[:, O_J:O_J + 128], rhs=btfs[half][:, h8, :],
                                                                             start=True, stop=True),
                         reads=[cst, xtbufs[half]], writes=[pb_], skip_self=True)
                    R.op("dve", lambda e, h=h, pb_=pb_: e.scalar_tensor_tensor(out=BT[:, h, :], in0=pb_[:, 0:256], scalar=8.0,
                                                                              in1=cst[:, O_MA:O_MA + 256], op0=ALU.mult, op1=ALU.add),
                         reads=[pb_, cst], writes=[BT])
                    R.op("dve", lambda e, h=h, pb_=pb_: e.scalar_tensor_tensor(out=BTs[:, h, :], in0=pb_[:, 0:128], scalar=8.0,
                                                                              in1=cst[:, O_MAS:O_MAS + 128], op0=ALU.mult, op1=ALU.add),
                         reads=[pb_, cst], writes=[BTs])

        NSTG = 7
        stg = [Buf("stg%d" % i, None, R.new_dsem("d_stg%d" % i)) for i in range(NSTG)]
        cvb = [Buf("cvb%d" % i, None, R.new_dsem("d_cvb%d" % i)) for i in range(NSTG)]
        wbuf = {k: Buf("wb_" + k, wb_t[k]) for k in WSHAPES}
        xt4 = hn.t[:, :, :].rearrange("p a (b c) -> p (a b) c", c=512)
        stg_ap = [xt4[:, i, :] for i in range(4)] + [f32w[i].t[:, 0:512] for i in range(3)]
        cvb_ap = [wslot[i % 4].t[:, (i // 4) * 512:(i // 4) * 512 + 512] for i in range(NSTG)]
        ceng = ("dve", "pool")
        plist = []
        for wn, (K, N) in WSHAPES.items():
            for kc in range(K // 128):
                for c0 in range(0, N, 512):
                    plist.append((wn, kc, c0, min(512, N - c0)))
        pend = []
        for ci, (wn, kc, c0, cw) in enumerate(plist):
            s_ = ci % NSTG
            R.dma("sp", stg_ap[s_][:, 0:cw], wsrc[wn][kc * 128:(kc + 1) * 128, c0:c0 + cw], writes=[stg[s_]])
            dst = cvb_ap[s_][:, 0:cw]
            en = ceng[ci % 2]
            R.op(en, lambda e, s_=s_, cw=cw, dst=dst: e.tensor_copy(out=dst, in_=stg_ap[s_][:, 0:cw]), reads=[stg[s_]], writes=[cvb[s_]])
            pend.append((wn, kc, c0, cw, s_, dst))
            if len(pend) > 4 or ci == len(plist) - 1:
                todo = pend if ci == len(plist) - 1 else [pend.pop(0)]
                for (wn2, kc2, c02, cw2, s2, dst2) in todo:
                    R.dma("act", wb_t[wn2].ap()[kc2 * 128:(kc2 + 1) * 128, c02:c02 + cw2], dst2, reads=[cvb[s2]], dsem=cvb[s2].dsem)
        emit_bias_tiles()
        R.barrier()
        R.op("pool", lambda e: e.memset(wkz[:, :, :, :], 0.0), writes=[wkz])
        wbin = wb_t["w_in"].ap()
        for kv in range(2):
            for hi in range(2):
                v = kv * 2 + hi
                R.dma("sp", wkz[:, v, :, hi * 64:hi * 64 + 64],
                      wbin[:, C_KA + kv * 64:C_KA + kv * 64 + 64].rearrange("(k p) n -> p k n", p=128),
                      reads=[wbuf["w_in"]], writes=[wkz])
        R.op("pool", lambda e: e.memset(vz_t[:, :, :, :], 0.0), writes=[vz_c, vz_n])

        panels = []

        def P_(wn, kc0, nkc, c0, ncols):
            panels.append((wn, kc0, nkc, c0, ncols))
            return len(panels) - 1

        PI = {}
        PI["qa"] = [P_("w_in", 0, 8, C_QA + 512 * i, 512) for i in range(2)]
        PI["kv"] = P_("w_in", 0, 8, C_KA, 256)
        PI["ib"] = [P_("w_in", 0, 8, C_IB + 512 * i, 512) for i in range(2)]
        PI["hg"] = []
        for hg in range(2):
            PI["hg"].append((P_("w_in", 0, 8, C_QB + 512 * hg, 512), P_("w_in", 0, 8, C_FB + 512 * hg, 512),
                             P_("w_in", 0, 8, C_OB + 512 * hg, 512)))
        PI["ga"] = [P_("w_in", 0, 8, C_GA + 512 * i, 512) for i in range(2)]
        PI["gb"] = [P_("w_in", 0, 8, C_GB + 512 * i, 512) for i in range(2)]
        PI["bra"] = [P_("w_br_a", 0, 8, 512 * i, 512) for i in range(2)]
        PI["brb"] = [P_("w_br_b", 0, 8, 512 * i, 512) for i in range(2)]
        PI["out"] = [P_("w_out", 0, 8, 512 * i, 512) for i in range(2)]
        PI["up"] = []
        for jg in range(6):
            ncl = 512 if jg < 5 else 256
            PI["up"].append((P_("w_up", 0, 8, 512 * jg, ncl), P_("w_up", 0, 8, DFF + 512 * jg, ncl), ncl // 128))
        PI["down"] = [[P_("w_down", k0, nk, 512 * i, 512) for (k0, nk) in ((0, 8), (8, 8), (16, 6))] for i in range(2)]
        PI["ple"] = P_("w_ple", 0, 2, 0, 1024)
        PI["pg"] = [P_("w_ple_gate", 0, 8, 512 * i, 512) for i in range(2)]
        NPAN = len(panels)
        wstate = {"issued": 0}
        NT_TILES = (PB * NTILE + 1) if DBG["tiles"] is None else len(DBG["tiles"])
        TOTAL_PAN = NPAN * NT_TILES

        def issue_panel(g):
            wn, kc0, nkc, c0, ncols = panels[g % NPAN]
            s = wslot[g % 4]
            src = wb_t[wn].ap()[kc0 * 128:(kc0 + nkc) * 128, c0:c0 + ncols].rearrange("(k p) n -> p k n", p=128)
            dst = s.t[:, 0:nkc * ncols].rearrange("p (k n) -> p k n", n=ncols)
            R.dma("sp", dst, src, reads=[wbuf[wn]], writes=[s])

        def getp(tile_idx, pidx, lo=None):
            g = tile_idx * NPAN + pidx
            glo = g if lo is None else tile_idx * NPAN + lo
            while wstate["issued"] <= min(glo + 3, TOTAL_PAN - 1):
                issue_panel(wstate["issued"])
                wstate["issued"] += 1
            wn, kc0, nkc, c0, ncols = panels[pidx]
            s = wslot[g % 4]
            return s, s.t[:, 0:nkc * ncols].rearrange("p (k n) -> p k n", n=ncols)

        def evac_copy(out_ap, obuf, pbuf, in_ap, k):
            if k % 2 == 0:
                R.op("act", lambda e: e.copy(out=out_ap, in_=in_ap), reads=[pbuf], writes=[obuf])
            else:
                R.op("dve", lambda e: e.tensor_copy(out=out_ap, in_=in_ap), reads=[pbuf], writes=[obuf])

        def mm_group(pbuf, out_ap, pairs, extra_reads, start=True, stop=True):
            n = len(pairs)
            for i, (l, r) in enumerate(pairs):
                R.op("pe", lambda e, l=l, r=r, i=i: e.matmul(out_ap, lhsT=l, rhs=r, start=(start and i == 0),
                                                              stop=(stop and i == n - 1)),
                     reads=extra_reads, writes=[pbuf], inc=(i == n - 1), skip_self=True)

        def bcast_last(ap2, n):
            prs = [list(x) for x in ap2.ap]
            return bass.AP(ap2.tensor, ap2.offset, prs + [[0, n]])

        def bcast_mid(ap2, n):
            prs = [list(x) for x in ap2.ap]
            return bass.AP(ap2.tensor, ap2.offset, [prs[0], [0, n]] + prs[1:])

        def sq_accum(in_ap, srcbuf, col):
            R.op("act", lambda e: e.activation(out=junk_ap[:, 0:in_ap.shape[-1]], in_=in_ap, func=AF.Square,
                                               accum_out=ss[:, col:col + 1]), reads=[srcbuf], writes=[hn, ss])

        def rstd_from_ss(nb, stride):
            for blk in range(nb):
                R.op("act", lambda e, blk=blk: e.activation(out=rstd[:, blk:blk + 1], in_=ss[:, blk * stride:blk * stride + 1],
                                                            func=AF.Ln, scale=1.0 / D, bias=eps_col), reads=[ss, cst], writes=[rstd])
            R.op("act", lambda e: e.activation(out=rstd[:, 0:nb], in_=rstd[:, 0:nb], func=AF.Exp, scale=-0.5), reads=[rstd], writes=[rstd])

        def to_feature_major(xt, nb, gcol, dstbuf, with_norm):
            NT = nb * 128
            if with_norm:
                for blk in range(nb):
                    sq_accum(xt[:, blk, :], xt, blk)
                rstd_from_ss(nb, 1)
                for blk in range(nb):
                    R.op("dve", lambda e, blk=blk: e.tensor_scalar(out=hn[:, blk, :], in0=xt[:, blk, :], scalar1=rstd[:, blk:blk + 1],
                                                                    scalar2=None, op0=ALU.mult), reads=[xt, rstd], writes=[hn])
            srcb = hn if with_norm else xt
            for c in range(8):
                pb_, pap = ptr()
                for blk in range(nb):
                    R.op("pe", lambda e, c=c, blk=blk, pap=pap: e.transpose(out=pap[:, blk * 128:(blk + 1) * 128],
                                                                           in_=srcb[:, blk, c * 128:(c + 1) * 128], identity=ident_f),
                         reads=[srcb, cst], writes=[pb_], inc=(blk == nb - 1), skip_self=True)
                if with_norm:
                    R.op("dve", lambda e, c=c, pap=pap: e.tensor_scalar(out=dstbuf[:, c, 0:NT], in0=pap[:, 0:NT],
                                                                        scalar1=pv[:, gcol + c:gcol + c + 1], scalar2=None, op0=ALU.mult),
                         reads=[pb_, pv], writes=[dstbuf])
                else:
                    evac_copy(dstbuf[:, c, 0:NT], dstbuf, pb_, pap[:, 0:NT], c)

        def fm_proj(slotbuf, slot3, ncc, rhsbuf, NT, consume):
            for cc in range(ncc):
                pb_ = pmm()
                mm_group(pb_, pb_[:, 0:NT], [(slot3[:, kc, cc * 128:(cc + 1) * 128], rhsbuf[:, kc, 0:NT]) for kc in range(8)],
                         [slotbuf, rhsbuf])
                consume(cc, pb_, pb_[:, 0:NT])

        def load_x(xt, kind, b, ti):
            if kind == "p":
                R.dma("sp", xt[:, 0:2, :], xp[b, ti * T:ti * T + T, :].rearrange("(n p) d -> p n d", p=128), writes=[xt])
            else:
                R.dma("sp", xt[:, 0, :], xs, writes=[xt])

        def tile_body(ti_glob, kind, b, ti, nxt):
            xt = xtbufs[ti_glob % 2]
            if ti_glob == 0:
                load_x(xt, kind, b, ti)
            if nxt is not None:
                load_x(xtbufs[(ti_glob + 1) % 2], *nxt)
            nb = 2 if kind == "p" else 1
            NT = nb * 128
            first_t = (kind == "p" and ti == 0)
            last_t = (kind == "p" and ti == NTILE - 1)
            L = 64 if kind == "p" else 32
            nch = NT // L
            nseg = 1 if kind == "p" else 4
            Ls = NT // nseg

            if kind == "p":
                t0_ = ti * T
                R.dma("sp", pe_tok[:, 0:nb, :], pp[b, t0_:t0_ + NT, :].rearrange("(n p) d -> p n d", p=128), writes=[pe_tok])
            else:
                R.dma("sp", pe_tok[:, 0, :], psm, writes=[pe_tok])
                kctok, vctok = tw(), tw()
                kc3 = kctok.t[:, 0:512].rearrange("p (i f) -> p i f", f=128)
                vc3 = vctok.t[:, 0:512].rearrange("p (i f) -> p i f", f=128)
                R.dma("sp", kc3, ck.rearrange("b s f -> s b f"), writes=[kctok])
                R.dma("sp", vc3, cv.rearrange("b s f -> s b f"), writes=[vctok])
                R.dma("sp", carry[:, :, :, :], cfc, writes=[carry])
                R.op("pool", lambda e: e.memset(kcz_v, 0.0), writes=[mixT])
                R.op("pool", lambda e: e.memset(vcz_v, 0.0), writes=[sga])
                kcsw = tw()
                ks3 = kcsw.t[:, 0:512].rearrange("p (i f) -> p i f", f=128)
                R.op("dve", lambda e: e.tensor_copy(out=ks3[:, :, 0:64], in_=kc3[:, :, 64:128]), reads=[kctok], writes=[kcsw])
                R.op("dve", lambda e: e.tensor_copy(out=ks3[:, :, 64:128], in_=kc3[:, :, 0:64]), reads=[kctok], writes=[kcsw])
                for i in range(4):
                    for kv in range(2):
                        for hi in range(2):
                            v = kv * 2 + hi
                            R.op("pool", lambda e, i=i, kv=kv, hi=hi, v=v: e.tensor_copy(
                                out=vcz_v[:, i, v, hi * 64:hi * 64 + 64], in_=vc3[:, i, kv * 64:kv * 64 + 64]),
                                reads=[vctok], writes=[sga])
                for i in range(4):
                    pb_, pap = ptr()
                    R.op("pe", lambda e, i=i, pap=pap: e.transpose(out=pap[:, 0:128], in_=kc3[:, i, :], identity=ident_f),
                         reads=[kctok, cst], writes=[pb_], inc=False, skip_self=True)
                    R.op("pe", lambda e, i=i, pap=pap: e.transpose(out=pap[:, 128:256], in_=ks3[:, i, :], identity=ident_f),
                         reads=[kcsw, cst], writes=[pb_], inc=True, skip_self=True)
                    R.op("dve", lambda e, i=i, pap=pap: e.tensor_copy(out=kcz_v[0:64, i, 0, :], in_=pap[0:64, 0:128]), reads=[pb_], writes=[mixT])
                    R.op("dve", lambda e, i=i, pap=pap: e.tensor_copy(out=kcz_v[64:128, i, 3, :], in_=pap[64:128, 0:128]), reads=[pb_], writes=[mixT])
                    R.op("act", lambda e, i=i, pap=pap: e.copy(out=kcz_v[0:64, i, 2, :], in_=pap[0:64, 128:256]), reads=[pb_], writes=[mixT])
                    R.op("act", lambda e, i=i, pap=pap: e.copy(out=kcz_v[64:128, i, 1, :], in_=pap[64:128, 128:256]), reads=[pb_], writes=[mixT])
            if first_t:
                R.op("pool", lambda e: e.memset(S32_t[:, :, :], 0.0), writes=S32)
                R.op("pool", lambda e: e.memset(Sbf_t[:, :, :], 0.0), writes=Sbf)
                R.op("pool", lambda e: e.memset(carry[:, :, :, :], 0.0), writes=[carry])

            if DBG["stage"] <= 1:
                return
            to_feature_major(xt, nb, P_G1, hT, True)

            if DBG["stage"] <= 2:
                return
            for p in range(2):
                sbuf_, s3 = getp(ti_glob, PI["qa"][p])
                fm_proj(sbuf_, s3, 4, hT, NT,
                        lambda cc, pb_, pap, p=p: evac_copy(qT[:, 4 * p + cc, 0:NT], qT, pb_, pap, cc))
            sbuf_, s3 = getp(ti_glob, PI["kv"])
            for blk in range(nb):
                PX = pscore()
                mm_group(PX, PX[:, 0:256], [(hT[:, kc, blk * 128:(blk + 1) * 128], s3[:, kc, 0:256]) for kc in range(8)], [sbuf_, hT])
                for kv in range(2):
                    for hi in range(2):
                        v = kv * 2 + hi
                        eng = "dve"
                        if eng == "dve":
                            R.op("dve", lambda e, blk=blk, kv=kv, hi=hi, v=v, PX=PX: e.tensor_copy(
                                out=vz_t[:, 1 + blk, v, hi * 64:hi * 64 + 64], in_=PX[:, 128 + kv * 64:128 + kv * 64 + 64]),
                                reads=[PX], writes=[vz_n])
                        else:
                            R.op("act", lambda e, blk=blk, kv=kv, hi=hi, v=v: e.copy(
                                out=vz_t[:, 1 + blk, v, hi * 64:hi * 64 + 64], in_=PX[:, 128 + kv * 64:128 + kv * 64 + 64]),
                                reads=[PX], writes=[vz_n])
                want = (kind == "s") or (last_t and blk == nb - 1)
                if want:
                    rot["kv"] ^= 1
                    ko = kvout[rot["kv"]]
                    R.op("dve", lambda e, ko=ko, PX=PX: e.tensor_copy(out=ko[:, :], in_=PX[:, 0:256]), reads=[PX], writes=[ko])
                    if kind == "s":
                        R.dma("sp", o_sk, ko[:, 0:128], reads=[ko], is_output=True)
                        R.dma("sp", o_sv, ko[:, 128:256], reads=[ko], is_output=True)
                    else:
                        R.dma("sp", o_pk[b], ko[:, 0:128], reads=[ko], is_output=True)
                        R.dma("sp", o_pvv[b], ko[:, 128:256], reads=[ko], is_output=True)
            for v in range(4):
                pb_ = pmm()
                mm_group(pb_, pb_[:, 0:NT], [(wkz[:, v, kc, :], hT[:, kc, 0:NT]) for kc in range(8)], [wkz, hT])
                evac_copy(kz_t[:, v, 128:128 + NT], kz_n, pb_, pb_[:, 0:NT], v)

            if DBG["stage"] <= 3:
                return
            set_rot([PA, PBk, PS0, PS1])

            def attn_stream(pairs, YB, DB, sbanks=None):
                arot = [0]
                items = []
                for c in pairs:
                    kvh = c // 4
                    for hi in range(2):
                        h = 2 * c + hi
                        v = kvh * 2 + hi
                        if kind == "p":
                            blks = []
                            if not first_t:
                                blks.append((kz_c, kz_t[:, v, 0:128], 0, 128, BT, BT[:, h, 128:256], vz_c, vz_t[:, 0, v, :]))
                            blks.append((kz_n, kz_t[:, v, 128:256], 0, 256, BT, BT[:, h, 0:256], vz_n, vz_t[:, 1, v, :]))
                            blks.append((kz_n, kz_t[:, v, 256:384], 128, 256, BT, BT[:, h, 0:128], vz_n, vz_t[:, 2, v, :]))
                        else:
                            blks = [(kz_n, kz_t[:, v, 128:256], 0, 128, BTs, BTs[:, h, :], vz_n, vz_t[:, 1, v, :])]
                            for i in range(4):
                                blks.append((mixT, kcz_v[:, i, v, :], 32 * i, 32 * i + 32, BT, BT[:, h, 128:160], sga, vcz_v[:, i, v, :]))
                        for bi, blk_ in enumerate(blks):
                            items.append((c, hi, bi == 0 and hi == 0, (hi == 1 and bi == len(blks) - 1), blk_))

                def emit_scores(it):
                    c, hi, _, _, (kb, kap, q0, q1, bb, bap, _, _) = it
                    arot[0] ^= 1
                    ps_ = (PS0, PS1)[arot[0]] if sbanks is None else sbanks[arot[0]]
                    nq = q1 - q0
                    mm_group(ps_, ps_[:, 0:nq], [(kap, qT[:, c, q0:q1]), (ident, bap)], [kb, qT, bb, cbf])
                    return ps_

                written = {}

                def emit_rest(it, ps_):
                    c, hi, firstpair, lastpair, (kb, kap, q0, q1, bb, bap, vbuf, vap) = it
                    nq = q1 - q0
                    ptb = tb()
                    R.op("act", lambda e: e.activation(out=ptb[:, 0:nq], in_=ps_[:, 0:nq], func=AF.Exp, scale=0.125),
                         reads=[ps_], writes=[ptb])
                    if firstpair:
                        written.clear()
                    ol = ones_lo if hi == 0 else ones_hi
                    for s0 in range(q0, q1, 128):
                        s1 = min(s0 + 128, q1)
                        st_ = (len(written) == 0)
                        written[s0] = True
                        R.op("pe", lambda e, s0=s0, s1=s1, st_=st_: e.matmul(YB[:, s0:s1], lhsT=vap, rhs=ptb[:, s0 - q0:s1 - q0],
                                                                             start=st_, stop=False, skip_group_check=True),
                             reads=[vbuf, ptb], writes=[YB], inc=False, skip_self=True)
                        R.op("pe", lambda e, s0=s0, s1=s1, st_=st_: e.matmul(DB[:, s0:s1], lhsT=ol, rhs=ptb[:, s0 - q0:s1 - q0],
                                                                             start=st_, stop=False, skip_group_check=True),
                             reads=[cbf, ptb], writes=[DB], inc=True, skip_self=True)
                    if lastpair:
                        t1, t1y = tf(), tf()
                        R.op("dve", lambda e: e.tensor_scalar(out=t1[:, 0:NT], in0=DB[:, 0:NT], scalar1=pv2[:, 16 + c:17 + c], scalar2=None,
                                                              op0=ALU.add), reads=[DB, pv2], writes=[t1])
                        R.op("act", lambda e: e.copy(out=t1y[:, 0:NT], in_=YB[:, 0:NT]), reads=[YB], writes=[t1y])
                        R.op("dve", lambda e: e.reciprocal(out=t1[:, 0:NT], in_=t1[:, 0:NT]), reads=[t1], writes=[t1])
                        R.op("pool", lambda e: e.tensor_tensor(out=ya[:, c, 0:NT], in0=t1y[:, 0:NT], in1=t1[:, 0:NT], op=ALU.mult),
                             reads=[t1y, t1], writes=[ya])

                ps_cur = emit_scores(items[0])
                for i, it in enumerate(items):
                    ps_next = emit_scores(items[i + 1]) if i + 1 < len(items) else None
                    emit_rest(it, ps_cur)
                    ps_cur = ps_next
                    yield

            if DBG["stage"] <= 4:
                run_zip([attn_stream(range(0, 8, 2), PY, PD, (PA, PBk)), attn_stream(range(1, 8, 2), PT0, PT1, (PS0, PS1))])
                set_rot(pbank)
                return
            set_rot(pbank)
            for p in range(2):
                sbuf_, s3 = getp(ti_glob, PI["ib"][p])
                for blk in range(nb):
                    pb_ = pany()
                    mm_group(pb_, pb_[:, 0:512], [(hT[:, kc, blk * 128:(blk + 1) * 128], s3[:, kc, 0:512]) for kc in range(8)], [sbuf_, hT])
                    evac_copy(vb[:, blk, p * 512:(p + 1) * 512], vb, pb_, pb_[:, 0:512], blk)
            set_rot([PA, PBk])
            scm = cst.t[:, O_SCP:O_SCP + NT] if kind == "p" else cst.t[:, O_SCS:O_SCS + NT]
            hmk = cst.t[:, O_HMP:O_HMP + 128] if kind == "p" else cst.t[:, O_HMS:O_HMS + 128]
            mid = L // 2
            nsb = 128 // L

            def hgrn_head(h, hh, slot, YB, sq_b, sq3, sf_b, sf3, so_b, so3):
                fA, fB, fC, fD, fE, fF, fG, fH = hs_f[slot]
                qe, qst, ke, atm0, atm1, o2 = hs_b[slot]
                dc, kT = dec[slot], kstT[slot]
                wcols = slice(hh * 128, hh * 128 + 128)
                c3 = lambda ap_: ap_.rearrange("p (c l) -> p c l", l=L)
                pf = pany()
                mm_group(pf, pf[:, 0:NT], [(sf3[:, kc, wcols], hT[:, kc, 0:NT]) for kc in range(8)], [sf_b, hT])
                R.op("act", lambda e: e.activation(out=fB[:, 0:NT], in_=pf[:, 0:NT], func=AF.Tanh, scale=0.5), reads=[pf], writes=[fB])
                yield
                pq = pany()
                mm_group(pq, pq[:, 0:NT], [(sq3[:, kc, wcols], hT[:, kc, 0:NT]) for kc in range(8)], [sq_b, hT])
                R.op("act", lambda e: e.activation(out=fF[:, 0:NT], in_=pq[:, 0:NT], func=AF.Tanh, scale=0.5), reads=[pq], writes=[fF])
                R.op("dve", lambda e: e.scalar_tensor_tensor(out=fF[:, 0:NT], in0=fF[:, 0:NT], scalar=1.0, in1=pq[:, 0:NT],
                                                             op0=ALU.add, op1=ALU.mult), reads=[fF, pq], writes=[fF])
                yield
                po = pany()
                mm_group(po, po[:, 0:NT], [(so3[:, kc, wcols], hT[:, kc, 0:NT]) for kc in range(8)], [so_b, hT])
                R.op("act", lambda e: e.activation(out=fH[:, 0:NT], in_=po[:, 0:NT], func=AF.Tanh, scale=0.5), reads=[po], writes=[fH])
                R.op("dve", lambda e: e.scalar_tensor_tensor(out=fH[:, 0:NT], in0=fH[:, 0:NT], scalar=1.0, in1=po[:, 0:NT],
                                                             op0=ALU.add, op1=ALU.mult), reads=[fH, po], writes=[fH])
                yield
                R.op("act", lambda e: e.activation(out=fA[:, 0:NT], in_=fB[:, 0:NT], func=AF.Ln, scale=pv2[:, 24 + h:25 + h],
                                                   bias=pv2[:, 40 + h:41 + h]), reads=[fB, pv2], writes=[fA])
                yield
                R.op("dve", lambda e: e.tensor_tensor_scan(out=fC[:, 0:NT], data0=scm, data1=fA[:, 0:NT], initial=0.0,
                                                           op0=ALU.mult, op1=ALU.add), reads=[cst, fA], writes=[fC])
                yield
                R.op("act", lambda e: e.activation(out=fD[:, 0:NT], in_=fC[:, 0:NT], func=AF.Exp), reads=[fC], writes=[fD])
                R.op("dve", lambda e: e.tensor_tensor(out=c3(fG[:, 0:NT]), in0=c3(fC[:, 0:NT]), in1=bcast_last(c3(fC[:, 0:NT])[:, :, mid], L),
                                                      op=ALU.subtract), reads=[fC], writes=[fG])
                yield
                R.op("act", lambda e: e.activation(out=fE[:, 0:NT], in_=fG[:, 0:NT], func=AF.Exp), reads=[fG], writes=[fE])
                R.op("act", lambda e: e.activation(out=fC[:, 0:NT], in_=fG[:, 0:NT], func=AF.Exp, scale=-1.0), reads=[fG], writes=[fC])
                R.op("act", lambda e: e.activation(out=fA[:, 0:NT], in_=fB[:, 0:NT], func=AF.Identity, scale=pv2[:, 32 + h:33 + h],
                                                   bias=pv2[:, 24 + h:25 + h]), reads=[fB, pv2], writes=[fA])
                yield
                R.op("dve", lambda e: e.tensor_tensor(out=qe[:, 0:NT], in0=fF[:, 0:NT], in1=fE[:, 0:NT], op=ALU.mult), reads=[fF, fE], writes=[qe])
                R.op("pool", lambda e: e.tensor_tensor(out=qst[:, 0:NT], in0=fF[:, 0:NT], in1=fD[:, 0:NT], op=ALU.mult), reads=[fF, fD], writes=[qst])
                R.op("dve", lambda e: e.tensor_tensor(out=ke[:, 0:NT], in0=fA[:, 0:NT], in1=fC[:, 0:NT], op=ALU.mult), reads=[fA, fC], writes=[ke])
                R.op("dve", lambda e: e.tensor_tensor(out=c3(fG[:, 0:NT]), in0=c3(ke[:, 0:NT]), in1=bcast_last(c3(fE[:, 0:NT])[:, :, L - 1], L),
                                                      op=ALU.mult), reads=[ke, fE], writes=[fG])
                yield
                pb_, pap = ptr()
                for blk in range(nb):
                    R.op("pe", lambda e, blk=blk: e.transpose(out=pap[:, blk * 128:(blk + 1) * 128], in_=fG[:, blk * 128:(blk + 1) * 128],
                                                              identity=ident_f), reads=[fG, cst], writes=[pb_], inc=(blk == nb - 1), skip_self=True)
                sg0 = O_SEG + (0 if nsb == 2 else 2)
                for blk in range(nb):
                    R.op("dve", lambda e, blk=blk: e.tensor_tensor(
                        out=kT[:, blk * nsb:(blk + 1) * nsb, :], in0=bcast_mid(pap[:, blk * 128:(blk + 1) * 128], nsb),
                        in1=bcast_last(cst[:, sg0:sg0 + nsb], 128), op=ALU.mult), reads=[pb_, cst], writes=[kT])
                yield
                atms = [atm0, atm1]
                for blk in range(nb):
                    ps_ = pany()
                    mm_group(ps_, ps_[:, 0:128], [(ke[:, blk * 128:(blk + 1) * 128], qe[:, blk * 128:(blk + 1) * 128])], [ke, qe])
                    atm = atms[blk]
                    R.op("dve", lambda e, ps_=ps_, atm=atm: e.tensor_tensor(out=atm[:, 0:128], in0=ps_[:, 0:128], in1=hmk, op=ALU.mult),
                         reads=[ps_, cst], writes=[atm])
                    yield
                for blk in range(nb):
                    atm = atms[blk]
                    R.op("pe", lambda e, blk=blk, atm=atm: e.matmul(YB[:, blk * 128:(blk + 1) * 128], lhsT=vb[:, blk, h * 128:(h + 1) * 128],
                                                                    rhs=atm[:, 0:128], start=(blk == 0), stop=False, skip_group_check=True),
                         reads=[vb, atm], writes=[YB], inc=True, skip_self=True)
                yield
                if kind == "p":
                    stt = hs_st[slot]
                    R.op("pe", lambda e: e.matmul(YB[:, 0:L], lhsT=Sbf_t[:, h, :], rhs=qst[:, 0:L],
                                                  start=False, stop=True, skip_group_check=True),
                         reads=[Sbf[h], qst], writes=[YB], inc=False, skip_self=True)
                    PU = pany()
                    for cix in range(nch):
                        blk = (cix * L) // 128
                        s_ = (cix * L % 128) // L
                        R.op("pe", lambda e, cix=cix, blk=blk, s_=s_: e.matmul(PU[:, cix * 128:(cix + 1) * 128], lhsT=kT[:, blk * nsb + s_, :],
                                                                               rhs=vb[:, blk, h * 128:(h + 1) * 128], start=True, stop=True),
                             reads=[kT, vb], writes=[PU], inc=(cix == nch - 1), skip_self=True)
                    scr = [(fB, fB[:, 0:128]), (fC, fC[:, 0:128]), (fE, fE[:, 0:128]), (S32[h], S32_t[:, h, :])]
                    prev_b, prev_ap = S32[h], S32_t[:, h, :]
                    for cix in range(nch):
                        last = (cix == nch - 1)
                        ob, oap = scr[3] if last else scr[cix]
                        R.op("dve", lambda e, cix=cix, oap=oap, prev_ap=prev_ap: e.scalar_tensor_tensor(
                            out=oap, in0=prev_ap, scalar=fD[:, cix * L + L - 1:cix * L + L], in1=PU[:, cix * 128:(cix + 1) * 128],
                            op0=ALU.mult, op1=ALU.add), reads=[prev_b, fD, PU], writes=[ob])
                        dstb = Sbf[h] if last else stt
                        dst_ap = Sbf_t[:, h, :] if last else stt[:, cix, :]
                        R.op("act", lambda e, oap=oap, dst_ap=dst_ap: e.copy(out=dst_ap, in_=oap), reads=[ob], writes=[dstb])
                        prev_b, prev_ap = ob, oap
                    yield
                    for cix in range(1, nch):
                        cols = slice(cix * L, (cix + 1) * L)
                        R.op("pe", lambda e, cix=cix, cols=cols: e.matmul(YB[:, cols], lhsT=stt[:, cix - 1, :], rhs=qst[:, cols],
                                                                          start=False, stop=True, skip_group_check=True),
                             reads=[stt, qst], writes=[YB], inc=(cix == nch - 1), skip_self=True)
                    yield
                else:
                    for cix in range(nch):
                        blk = (cix * L) // 128
                        s_ = (cix * L % 128) // L
                        cols = slice(cix * L, (cix + 1) * L)
                        R.dma("sp", S32_t[:, h, :], sh[cix, h], writes=[S32[h]])
                        R.op("act", lambda e: e.copy(out=Sbf_t[:, h, :], in_=S32_t[:, h, :]), reads=[S32[h]], writes=[Sbf[h]])
                        R.op("pe", lambda e, cols=cols: e.matmul(YB[:, cols], lhsT=Sbf_t[:, h, :], rhs=qst[:, cols],
                                                                 start=False, stop=True, skip_group_check=True),
                             reads=[Sbf[h], qst], writes=[YB], inc=True, skip_self=True)
                        PX = pany()
                        mm_group(PX, PX[:, 0:128], [(kT[:, blk * nsb + s_, :], vb[:, blk, h * 128:(h + 1) * 128])], [kT, vb])
                        R.op("dve", lambda e, cix=cix, PX=PX: e.scalar_tensor_tensor(out=S32_t[:, h, :], in0=S32_t[:, h, :],
                                                                                    scalar=fD[:, cix * L + L - 1:cix * L + L], in1=PX[:, 0:128],
                                                                                    op0=ALU.mult, op1=ALU.add),
                             reads=[S32[h], fD, PX], writes=[S32[h]])
                        R.dma("sp", o_ss[cix, h], S32_t[:, h, :], reads=[S32[h]], is_output=True)
                        yield
                R.op("act", lambda e: e.activation(out=o2[:, 0:NT], in_=YB[:, 0:NT], func=AF.Square), reads=[YB], writes=[o2])
                pd_ = pany()
                mm_group(pd_, pd_[:, 0:NT], [(ones_bf, o2[:, 0:NT])], [cbf, o2])
                R.op("act", lambda e: e.activation(out=fA[:, 0:NT], in_=pd_[:, 0:NT], func=AF.Ln, scale=1.0 / 128, bias=pv2[:, 49:50]),
                     reads=[pd_, pv2], writes=[fA])
                R.op("act", lambda e: e.activation(out=fA[:, 0:NT], in_=fA[:, 0:NT], func=AF.Exp, scale=-0.5), reads=[fA], writes=[fA])
                yield
                R.op("dve", lambda e: e.scalar_tensor_tensor(out=fE[:, 0:NT], in0=YB[:, 0:NT], scalar=pv2[:, 48:49], in1=fA[:, 0:NT],
                                                             op0=ALU.mult, op1=ALU.mult), reads=[YB, pv2, fA], writes=[fE])
                R.op("dve", lambda e: e.tensor_tensor(out=yb[:, h, 0:NT], in0=fE[:, 0:NT], in1=fH[:, 0:NT], op=ALU.mult),
                     reads=[fE, fH], writes=[yb])
                yield

            def zip_gen(gens):
                active = list(gens)
                while active:
                    for g_ in list(active):
                        try:
                            next(g_)
                        except StopIteration:
                            active.remove(g_)
                    yield

            def hgrn_all():
                for hg in range(2):
                    q_i, f_i, o_i = PI["hg"][hg]
                    sq_b, sq3 = getp(ti_glob, q_i, lo=q_i)
                    sf_b, sf3 = getp(ti_glob, f_i, lo=q_i)
                    so_b, so3 = getp(ti_glob, o_i, lo=q_i)
                    for pi_ in range(2):
                        yield from zip_gen([hgrn_head(4 * hg + 2 * pi_ + s_, 2 * pi_ + s_, s_, (PY, PD)[s_], sq_b, sq3, sf_b, sf3, so_b, so3)
                                            for s_ in range(2)])

            run_zip([attn_stream(range(8), PT0, PT1), hgrn_all()])
            if kind == "p" and not last_t:
                R.op("pool", lambda e: e.tensor_copy(out=kz_t[:, :, 0:128], in_=kz_t[:, :, T:T + 128]), reads=[kz_n], writes=[kz_c])
                R.op("pool", lambda e: e.tensor_copy(out=vz_t[:, 0, :, :], in_=vz_t[:, 2, :, :]), reads=[vz_n], writes=[vz_c])
            if last_t:
                R.dma("sp", o_ps[b].rearrange("h d v -> d h v"), S32_t[:, :, :], reads=S32, dsem=S32[0].dsem, is_output=True)
            set_rot(pbank)


            if DBG["stage"] <= 5:
                return
            for nm, dst in (("ga", sga), ("gb", sgb)):
                for p in range(2):
                    sbuf_, s3 = getp(ti_glob, PI[nm][p])
                    fm_proj(sbuf_, s3, 4, hT, NT, lambda cc, pb_, pap, p=p, dst=dst: R.op(
                        "act", lambda e: e.activation(out=dst[:, 4 * p + cc, 0:NT], in_=pap, func=AF.Sigmoid), reads=[pb_], writes=[dst]))
            for p in range(2):
                sbuf_, s3 = getp(ti_glob, PI["bra"][p])
                fm_proj(sbuf_, s3, 4, ya, NT, lambda cc, pb_, pap, p=p: R.op(
                    "dve", lambda e: e.tensor_tensor(out=tA[:, 4 * p + cc, 0:NT], in0=pap, in1=sga[:, 4 * p + cc, 0:NT], op=ALU.mult),
                    reads=[pb_, sga], writes=[tA]))
            for p in range(2):
                sbuf_, s3 = getp(ti_glob, PI["brb"][p])

                def cons(cc, pb_, pap, p=p):
                    t3 = tf()
                    R.op("dve", lambda e: e.tensor_tensor(out=t3[:, 0:NT], in0=pap, in1=sgb[:, 4 * p + cc, 0:NT], op=ALU.mult),
                         reads=[pb_, sgb], writes=[t3])
                    R.op("pool", lambda e: e.tensor_tensor(out=mixT[:, 4 * p + cc, 0:NT], in0=t3[:, 0:NT], in1=tA[:, 4 * p + cc, 0:NT], op=ALU.add),
                         reads=[t3, tA], writes=[mixT])
                fm_proj(sbuf_, s3, 4, yb, NT, cons)

            def tok_proj_norm(pkey, lhs_buf, kgroups, gidx):
                banks = {(0, 0): PA, (0, 1): PBk, (1, 0): PS0, (1, 1): PS1}
                for p in range(2):
                    for gi, (k0, nk) in enumerate(kgroups):
                        sbuf_, s3 = getp(ti_glob, PI[pkey][p][gi] if isinstance(PI[pkey][p], list) else PI[pkey][p])
                        for blk in range(nb):
                            bk = banks[(blk, p)]
                            mm_group(bk, bk[:, 0:512], [(lhs_buf[:, k0 + kc, blk * 128:(blk + 1) * 128], s3[:, kc, 0:512]) for kc in range(nk)],
                                     [sbuf_, lhs_buf], start=(gi == 0), stop=(gi == len(kgroups) - 1))
                    for blk in range(nb):
                        sq_accum(banks[(blk, p)][:, 0:512], banks[(blk, p)], blk * 2 + p)
                for blk in range(nb):
                    R.op("dve", lambda e, blk=blk: e.tensor_tensor(out=ss[:, blk * 2:blk * 2 + 1], in0=ss[:, blk * 2:blk * 2 + 1],
                                                                    in1=ss[:, blk * 2 + 1:blk * 2 + 2], op=ALU.add),
                         reads=[ss], writes=[ss])
                rstd_from_ss(nb, 2)
                for blk in range(nb):
                    for p in range(2):
                        bk = banks[(blk, p)]
                        t4 = tw()
                        R.op("dve", lambda e, bk=bk, blk=blk, p=p, t4=t4: e.scalar_tensor_tensor(
                            out=t4[:, 0:512], in0=bk[:, 0:512], scalar=rstd[:, blk:blk + 1], in1=gbc[:, gidx, p * 512:(p + 1) * 512],
                            op0=ALU.mult, op1=ALU.mult), reads=[bk, rstd, gbc], writes=[t4])
                        eng = "pool" if p == 0 else "dve"
                        R.op(eng, lambda e, blk=blk, p=p, t4=t4: e.tensor_tensor(out=xt[:, blk, p * 512:(p + 1) * 512],
                                                                                in0=xt[:, blk, p * 512:(p + 1) * 512], in1=t4[:, 0:512], op=ALU.add),
                             reads=[xt, t4], writes=[xt])

            tok_proj_norm("out", mixT, [(0, 8)], 0)

            if DBG["stage"] <= 6:
                return
            to_feature_major(xt, nb, P_G2, hT, True)
            ffn_pend = []
            for jg in range(6):
                pa_i, pu_i, ncc = PI["up"][jg]
                sa_b, sa3 = getp(ti_glob, pa_i, lo=pa_i)
                su_b, su3 = getp(ti_glob, pu_i, lo=pa_i)
                for cc in range(ncc):
                    j = 4 * jg + cc
                    wc = slice(cc * 128, cc * 128 + 128)
                    pa_ = (PA, PS0, PY, PT0)[j % 4]
                    pu_ = (PBk, PS1, PD, PT1)[j % 4]
                    mm_group(pa_, pa_[:, 0:NT], [(sa3[:, kc, wc], hT[:, kc, 0:NT]) for kc in range(8)], [sa_b, hT])
                    mm_group(pu_, pu_[:, 0:NT], [(su3[:, kc, wc], hT[:, kc, 0:NT]) for kc in range(8)], [su_b, hT])
                    rot["a"] = (rot["a"] + 1) % len(aext)
                    ax = aext[rot["a"]]
                    ax3 = ax.t[:, 0:nseg * (Ls + 2)].rearrange("p (s l) -> p s l", l=Ls + 2)
                    pa3 = pa_.t[:, 0:NT].rearrange("p (s l) -> p s l", l=Ls)
                    R.op("pool", lambda e, ax3=ax3, j=j: e.tensor_copy(out=ax3[:, :, 0:2], in_=carry[:, j, 0:nseg, :]), reads=[carry], writes=[ax])
                    R.op("act", lambda e, ax3=ax3, pa3=pa3: e.copy(out=ax3[:, :, 2:Ls + 2], in_=pa3), reads=[pa_], writes=[ax])
                    R.op("pool", lambda e, ax3=ax3, j=j: e.tensor_copy(out=carry[:, j, 0:nseg, :], in_=ax3[:, :, Ls:Ls + 2]), reads=[ax], writes=[carry])
                    t5 = tf()
                    wcol = P_WC + 3 * j
                    t53 = t5.t[:, 0:NT].rearrange("p (s l) -> p s l", l=Ls)
                    R.op("pool", lambda e, ax3=ax3, t53=t53, wcol=wcol, j=j: e.tensor_scalar(
                        out=t53, in0=ax3[:, :, 2:Ls + 2], scalar1=pv[:, wcol + 2:wcol + 3], scalar2=pv[:, P_BC + j:P_BC + j + 1],
                        op0=ALU.mult, op1=ALU.add), reads=[ax, pv], writes=[t5])
                    R.op("dve", lambda e, ax3=ax3, t53=t53, wcol=wcol: e.scalar_tensor_tensor(
                        out=t53, in0=ax3[:, :, 1:Ls + 1], scalar=pv[:, wcol + 1:wcol + 2], in1=t53, op0=ALU.mult, op1=ALU.add),
                        reads=[ax, pv, t5], writes=[t5])
                    R.op("dve", lambda e, ax3=ax3, t53=t53, wcol=wcol: e.scalar_tensor_tensor(
                        out=t53, in0=ax3[:, :, 0:Ls], scalar=pv[:, wcol:wcol + 1], in1=t53, op0=ALU.mult, op1=ALU.add),
                        reads=[ax, pv, t5], writes=[t5])
                    if ffn_pend:
                        ffn_pend.pop(0)()

                    def tail(t5=t5, pu_=pu_, j=j):
                        R.op("act", lambda e: e.activation(out=t5[:, 0:NT], in_=t5[:, 0:NT], func=AF.Gelu_apprx_tanh), reads=[t5], writes=[t5])
                        R.op("dve", lambda e: e.tensor_tensor(out=hm[:, j, 0:NT], in0=pu_[:, 0:NT], in1=t5[:, 0:NT], op=ALU.mult),
                             reads=[pu_, t5], writes=[hm])
                    ffn_pend.append(tail)
            while ffn_pend:
                ffn_pend.pop(0)()
            if kind == "s":
                R.dma("sp", o_sc, carry[:, :, :, :], reads=[carry], is_output=True)
            elif last_t:
                R.dma("sp", o_pc[:, :, b, :], carry[:, :, 0, :], reads=[carry], is_output=True)
            tok_proj_norm("down", hm, [(0, 8), (8, 8), (16, 6)], 1)

            if DBG["stage"] <= 7:
                return
            to_feature_major(xt, nb, 0, hT, False)
            for c2 in range(2):
                pb_, pap = ptr()
                for blk in range(nb):
                    R.op("pe", lambda e, c2=c2, blk=blk, pap=pap: e.transpose(out=pap[:, blk * 128:(blk + 1) * 128],
                                                                             in_=pe_tok[:, blk, c2 * 128:(c2 + 1) * 128], identity=ident_f),
                         reads=[pe_tok, cst], writes=[pb_], inc=(blk == nb - 1), skip_self=True)
                evac_copy(peT[:, c2, 0:NT], peT, pb_, pap[:, 0:NT], c2)
            sple_b, sple3 = getp(ti_glob, PI["ple"])
            for p in range(2):
                spg_b, spg3 = getp(ti_glob, PI["pg"][p], lo=PI["ple"])
                for blk in range(nb):
                    bg = PA if blk == 0 else PS0
                    bp = PBk if blk == 0 else PS1
                    mm_group(bg, bg[:, 0:512], [(hT[:, kc, blk * 128:(blk + 1) * 128], spg3[:, kc, 0:512]) for kc in range(8)], [spg_b, hT])
                    mm_group(bp, bp[:, 0:512], [(peT[:, k2, blk * 128:(blk + 1) * 128], sple3[:, k2, p * 512:(p + 1) * 512]) for k2 in range(2)],
                             [sple_b, peT])
                    t8, t9 = tw(), tw()
                    R.op("act", lambda e, bg=bg, t8=t8: e.activation(out=t8[:, 0:512], in_=bg[:, 0:512], func=AF.Sigmoid), reads=[bg], writes=[t8])
                    R.op("dve", lambda e, bp=bp, t8=t8, t9=t9: e.tensor_tensor(out=t9[:, 0:512], in0=bp[:, 0:512], in1=t8[:, 0:512], op=ALU.mult),
                         reads=[bp, t8], writes=[t9])
                    R.op("pool", lambda e, blk=blk, p=p, t9=t9: e.tensor_tensor(out=xt[:, blk, p * 512:(p + 1) * 512],
                                                                               in0=xt[:, blk, p * 512:(p + 1) * 512], in1=t9[:, 0:512], op=ALU.add),
                         reads=[xt, t9], writes=[xt])
            if kind == "p":
                t0_ = ti * T
                R.dma("sp", o_yp[b, t0_:t0_ + NT, :].rearrange("(n p) d -> p n d", p=128), xt[:, 0:nb, :], reads=[xt], is_output=True)
            else:
                R.dma("sp", o_ys, xt[:, 0, :], reads=[xt], is_output=True)

        tg = 0
        tl = [("p", b, ti) for b in range(PB) for ti in range(NTILE)] + [("s", 0, 0)]
        if DBG["tiles"] is not None:
            tl = DBG["tiles"]
        for ix, (kd, b, ti) in enumerate(tl):
            tile_body(tg, kd, b, ti, tl[ix + 1] if ix + 1 < len(tl) else None)
            tg += 1

        R.finish("sp")
        R.replay(nc, st)
        build_program.stats = {e: len(R.q[e]) for e in ENGS}
        build_program.sbuf_bytes = sbtot[0]
    return nc


_CACHE = {}


def kernel(x_prompt, x_sample, cache_win_k, cache_win_v, state_hgrn, cache_ffn_conv, p_prompt, p_sample,
           rel_bias_table, lb_logits, g_pre_mix, w_in, attn_sinks, g_hgrn_out, w_br_a, w_br_b, w_out, g_post_mix,
           g_pre_ffn, w_up, w_conv, b_conv, w_down, g_post_ffn, w_ple, w_ple_gate):
    f = lambda a: np.ascontiguousarray(np.asarray(a, dtype=np.float32))
    if "nc" not in _CACHE:
        _CACHE["nc"] = build_program()
    nc = _CACHE["nc"]
    ncores = _CACHE.get("ncores", NCORE)
    cst = _make_cst()
    tabp = np.zeros((128, 128), np.float32)
    tabp[0:32, 0:16] = f(rel_bias_table)
    pv = np.zeros((128, PV_W), np.float32)
    pv[:, P_G1:P_G1 + 8] = f(g_pre_mix)[0].reshape(8, 128).T
    pv[:, P_G2:P_G2 + 8] = f(g_pre_ffn)[0].reshape(8, 128).T
    pv[:, P_L0:P_L0 + 8] = f(lb_logits)[0].reshape(8, 128).T
    pv[:, P_L1:P_L1 + 8] = f(lb_logits)[1].reshape(8, 128).T
    pv[:, P_GH] = f(g_hgrn_out)[0]
    pv[:, P_WC:P_WC + 66] = f(w_conv)[0].reshape(3, NJ, 128).transpose(2, 1, 0).reshape(128, 66)
    pv[:, P_BC:P_BC + NJ] = f(b_conv)[0].reshape(NJ, 128).T
    sk = f(attn_sinks)[0]
    for c in range(8):
        pv[0:64, P_SK + c] = sk[2 * c]
        pv[64:128, P_SK + c] = sk[2 * c + 1]
    gbc = np.ascontiguousarray(np.broadcast_to(np.stack([f(g_post_mix)[0], f(g_post_ffn)[0]])[None], (128, 2, D)))
    shared = {"table_pad": tabp, "cst": cst, "pv": pv, "gbc": gbc, "w_in": f(w_in)[0], "w_br_a": f(w_br_a)[0],
              "w_br_b": f(w_br_b)[0], "w_out": f(w_out)[0], "w_up": f(w_up)[0], "w_down": f(w_down)[0],
              "w_ple": f(w_ple)[0], "w_ple_gate": f(w_ple_gate)[0]}
    xpf, xsf = f(x_prompt), f(x_sample)
    ckf, cvf = f(cache_win_k)[0].reshape(32, 128, 128), f(cache_win_v)[0].reshape(32, 128, 128)
    shf, cff = f(state_hgrn)[0], f(cache_ffn_conv)[0]
    ppf, psf = f(p_prompt)[0], f(p_sample)[0]
    in_maps = []
    for i in range(ncores):
        sl = slice(PB * i, PB * i + PB)
        m = dict(shared)
        m["x_prompt"] = xpf[sl]
        m["x_sample"] = np.ascontiguousarray(xsf[sl].reshape(PB * DSEQ, D))
        m["cache_k"] = ckf[sl]
        m["cache_v"] = cvf[sl]
        m["state_hgrn"] = shf[sl]
        m["cache_conv"] = np.ascontiguousarray(cff[sl].reshape(PB, 2, NJ, 128).transpose(3, 2, 0, 1))
        m["p_prompt"] = ppf[sl]
        m["p_sample"] = np.ascontiguousarray(psf[sl].reshape(PB * DSEQ, PLE))
        in_maps.append(m)
    res = run_bass_kernel_spmd(nc, in_maps, core_ids=list(range(ncores)))
    rs = res.results
    g = lambda k: [np.asarray(r[k], dtype=np.float32) for r in rs]
    y_p = np.concatenate(g("y_prompt"), 0)
    y_s = np.concatenate([a.reshape(PB, DSEQ, D) for a in g("y_sample")], 0)
    pk = np.concatenate(g("o_pk"), 0).reshape(1, -1, 128, 2, 64)
    pvv = np.concatenate(g("o_pv"), 0).reshape(1, -1, 128, 2, 64)
    ps = np.concatenate(g("o_ps"), 0)[None]
    cvt = lambda a: a.transpose(2, 3, 1, 0).reshape(PB, 2, DFF)
    pc = np.concatenate([cvt(a) for a in g("o_pc")], 0)[None]
    skk = np.concatenate([a.reshape(PB, DSEQ, 2, 64) for a in g("o_sk")], 0)[None]
    svv = np.concatenate([a.reshape(PB, DSEQ, 2, 64) for a in g("o_sv")], 0)[None]
    sss = np.concatenate(g("o_ss"), 0)[None]
    sc = np.concatenate([cvt(a) for a in g("o_sc")], 0)[None]
    return (y_p, y_s, pk, pvv, ps, pc, skk, svv, sss, sc)
```

```python
import math
from contextlib import ExitStack

import numpy as np
import concourse.bass as bass
import concourse.mybir as mybir
from concourse.bass_utils import run_bass_kernel_spmd

F32 = mybir.dt.float32
BF16 = mybir.dt.bfloat16
AF = mybir.ActivationFunctionType
ALU = mybir.AluOpType

ENGS = ("pe", "act", "dve", "pool", "sp")


class SemRef:
    def __init__(self, name):
        self.name = name
        self.handle = None
        self.count = 0


class Buf:
    def __init__(self, name, t=None, dsem=None):
        self.name = name
        self.t = t
        self.last_w = None
        self.readers = {}
        self.overlaps = []
        self.dsem = dsem

    def __getitem__(self, idx):
        return self.t[idx]


class Rec:
    def __init__(self):
        self.q = {e: [] for e in ENGS}
        self.esem = {e: SemRef("s_" + e) for e in ENGS}
        self.waited = {e: {} for e in ENGS}
        self.dsems = []
        self.out_marks = {}
        self.n_instr = 0

    def new_dsem(self, name):
        s = SemRef(name)
        self.dsems.append(s)
        return s

    def _deps(self, reads, writes):
        deps = {}

        def add(d):
            if d is None:
                return
            s, v = d
            if deps.get(s, -1) < v:
                deps[s] = v

        for b in reads:
            for bb in [b] + b.overlaps:
                add(bb.last_w)
        for b in writes:
            for bb in [b] + b.overlaps:
                add(bb.last_w)
                for s, v in bb.readers.items():
                    add((s, v))
        return deps

    def _waits(self, eng, deps, skip_self=False):
        ws = []
        wd = self.waited[eng]
        for s, v in deps.items():
            if skip_self and s is self.esem[eng]:
                continue
            if wd.get(s, 0) >= v:
                continue
            wd[s] = v
            ws.append((s, v))
        return ws

    def op(self, eng, fn, reads=(), writes=(), inc=True, skip_self=False):
        reads = list(reads)
        writes = list(writes)
        deps = self._deps(reads, writes)
        ws = self._waits(eng, deps, skip_self)
        sem = self.esem[eng]
        if inc:
            sem.count += 1
        mark = (sem, sem.count if inc else sem.count + 1)
        self.q[eng].append((ws, fn, sem if inc else None, 1))
        for b in reads:
            if b.readers.get(sem, 0) < mark[1]:
                b.readers[sem] = mark[1]
        for b in writes:
            b.last_w = mark
            b.readers = {}
        self.n_instr += 1
        return mark

    def dma(self, eng, out_ap, in_ap, reads=(), writes=(), dsem=None, is_output=False, nodeps=False):
        reads = list(reads)
        writes = list(writes)
        if dsem is None:
            for b in writes + reads:
                if b.dsem is not None:
                    dsem = b.dsem
                    break
        assert dsem is not None
        deps = {} if nodeps else self._deps(reads, writes)
        ws = self._waits(eng, deps)
        dsem.count += 16
        mark = (dsem, dsem.count)

        def fn(e, out_ap=out_ap, in_ap=in_ap):
            return e.dma_start(out=out_ap, in_=in_ap)

        self.q[eng].append((ws, fn, dsem, 16))
        for b in reads:
            b.readers[dsem] = dsem.count
        for b in writes:
            b.last_w = mark
            b.readers = {}
        if is_output:
            self.out_marks[dsem] = dsem.count
        self.n_instr += 1
        return mark

    def barrier(self):
        sems = [s for s in list(self.esem.values()) + self.dsems if s.count > 0]
        for e in ENGS:
            ws = self._waits(e, {s: s.count for s in sems})
            if ws:
                self.q[e].append((ws, None, None, 0))

    def finish(self, eng="sp"):
        ws = [(s, v) for s, v in self.out_marks.items()]
        ws += [(s, s.count) for s in self.esem.values() if s.count > 0]
        self.q[eng].append((ws, None, None, 0))

    def replay(self, nc, stack):
        for s in list(self.esem.values()) + self.dsems:
            s.handle = stack.enter_context(nc.semaphore(s.name))
        block = stack.enter_context(nc.Block())
        decos = {"pe": block.tensor, "act": block.scalar, "dve": block.vector,
                 "pool": block.gpsimd, "sp": block.sync}
        for en in ENGS:
            lst = self.q[en]

            def body(e, lst=lst):
                for ws, fn, sem, inc in lst:
                    for s, v in ws:
                        e.wait_ge(s.handle, v)
                    if fn is None:
                        continue
                    ins = fn(e)
                    if sem is not None:
                        ins.then_inc(sem.handle, inc)

            decos[en](body)


D = 1024
NCORE = 8
PB = 4
SEQ = 2048
T = 256
NTILE = SEQ // T
DSEQ = 32
DFF = 2816
NJ = 22
INC = 7424
PLE = 256
EPS = 1e-6
NEG = -30000.0
C_QA, C_KA, C_VA, C_QB, C_FB, C_IB, C_OB, C_GA, C_GB = 0, 1024, 1152, 1280, 2304, 3328, 4352, 5376, 6400

WSHAPES = {"w_in": (D, INC), "w_br_a": (D, D), "w_br_b": (D, D), "w_out": (D, D), "w_up": (D, 2 * DFF),
           "w_down": (DFF, D), "w_ple": (PLE, D), "w_ple_gate": (D, D)}

O_ID, O_OH, O_MA, O_MAS, O_HMP, O_HMS, O_SCP, O_SCS, O_SEG, O_EPS, O_ONELH, O_ONE = (
    0, 128, 512, 768, 896, 1024, 1152, 1408, 1536, 1542, 1543, 1799)
O_J = 1927
CST_W = 2055
P_G1, P_G2, P_L0, P_L1, P_GH, P_WC, P_BC, P_SK = 0, 8, 16, 24, 32, 33, 99, 121
PV_W = 129


def _t5_bucket(rel):
    nb = 16
    ret = np.where(rel > 0, nb, 0)
    n = np.abs(rel)
    max_exact = nb // 2
    large = max_exact + (np.log(np.maximum(n, max_exact).astype(np.float32) / max_exact)
                         / math.log(128 / max_exact) * (nb - max_exact)).astype(np.int32)
    large = np.minimum(large, nb - 1)
    return ret + np.where(n < max_exact, n, large)


def _make_cst():
    c = np.zeros((128, CST_W), np.float32)
    c[:, O_ID:O_ID + 128] = np.eye(128, dtype=np.float32)
    j = np.arange(384)
    bk = _t5_bucket(127 - j)
    c[bk, O_OH + j] = 1.0
    p = np.arange(128)[:, None]
    col = np.arange(256)[None, :]
    hh = p // 64
    dd = col // 64
    valid = ((dd - hh) >= 0) & ((dd - hh) <= 2)
    c[:, O_MA:O_MA + 256] = np.where(valid, 0.0, NEG)
    col1 = np.arange(128)[None, :]
    c[:, O_MAS:O_MAS + 128] = np.where((p // 32) == (col1 // 32), 0.0, NEG)
    c[:, O_HMP:O_HMP + 128] = ((p // 64) == (col1 // 64)) & (p <= col1)
    c[:, O_HMS:O_HMS + 128] = ((p // 32) == (col1 // 32)) & (p <= col1)
    c[:, O_SCP:O_SCP + 256] = (np.arange(256) % 64 != 0)[None, :]
    c[:, O_SCS:O_SCS + 128] = (np.arange(128) % 32 != 0)[None, :]
    c[:, O_SEG + 0] = (p[:, 0] < 64)
    c[:, O_SEG + 1] = (p[:, 0] >= 64)
    for s in range(4):
        c[:, O_SEG + 2 + s] = (p[:, 0] // 32 == s)
    c[:, O_EPS] = EPS
    c[:, O_ONELH:O_ONELH + 64] = 1.0
    c[:, O_ONELH + 128 + 64:O_ONELH + 256] = 1.0
    c[:, O_ONE:O_ONE + 128] = 1.0
    c[np.arange(128), O_J + 127 - np.arange(128)] = 1.0
    return c


DBG = {"tiles": None, "stage": 99, "setup": 99, "sub": 99}


def build_program():
    nc = bass.Bass("TRN2", target_bir_lowering=False)
    R = Rec()

    def din(name, shape, dt=F32):
        return nc.dram_tensor(name, list(shape), dt, kind="ExternalInput")

    def dout(name, shape, dt=F32):
        return nc.dram_tensor(name, list(shape), dt, kind="ExternalOutput")

    xp = din("x_prompt", [PB, SEQ, D]).ap()
    xs = din("x_sample", [PB * DSEQ, D]).ap()
    ck = din("cache_k", [PB, 128, 128]).ap()
    cv = din("cache_v", [PB, 128, 128]).ap()
    sh = din("state_hgrn", [PB, 8, 128, 128]).ap()
    cfc = din("cache_conv", [128, NJ, PB, 2]).ap()
    pp = din("p_prompt", [PB, SEQ, PLE]).ap()
    psm = din("p_sample", [PB * DSEQ, PLE]).ap()
    tabp = din("table_pad", [128, 128]).ap()
    cst_d = din("cst", [128, CST_W]).ap()
    pv_d = din("pv", [128, PV_W]).ap()
    gbc_d = din("gbc", [128, 2, D]).ap()
    wsrc = {k: din(k, list(v)).ap() for k, v in WSHAPES.items()}

    o_yp = dout("y_prompt", [PB, SEQ, D]).ap()
    o_ys = dout("y_sample", [PB * DSEQ, D]).ap()
    o_pk = dout("o_pk", [PB, 128, 128]).ap()
    o_pvv = dout("o_pv", [PB, 128, 128]).ap()
    o_ps = dout("o_ps", [PB, 8, 128, 128]).ap()
    o_pc = dout("o_pc", [128, NJ, PB, 2]).ap()
    o_sk = dout("o_sk", [PB * DSEQ, 128]).ap()
    o_sv = dout("o_sv", [PB * DSEQ, 128]).ap()
    o_ss = dout("o_ss", [PB, 8, 128, 128]).ap()
    o_sc = dout("o_sc", [128, NJ, PB, 2]).ap()

    wb_t = {k: nc.dram_tensor(k + "_bf", list(v), BF16, kind="Internal") for k, v in WSHAPES.items()}
    trev_t = nc.dram_tensor("trev", [16, 384], F32, kind="Internal")

    with ExitStack() as st:
        sbtot = [0]

        def sb(name, shape, dt):
            n_ = 1
            for d_ in shape[1:]:
                n_ *= d_
            sbtot[0] += n_ * (4 if dt == F32 else 2)
            return st.enter_context(nc.sbuf_tensor("sb_" + name, list(shape), dt))

        def pst(name, shape, dt):
            return st.enter_context(nc.psum_tensor("ps_" + name, list(shape), dt))

        def B(name, shape, dt, dma=False):
            return Buf(name, sb(name, shape, dt), R.new_dsem("d_" + name) if dma else None)

        xtbufs = [B("xt%d" % i, [128, 2, D], F32, dma=True) for i in range(2)]
        xt = xtbufs[0]
        hn = B("hn", [128, 2, D], F32)
        hT = B("hT", [128, 8, T], BF16)
        qT = B("qT", [128, 8, T], BF16)
        kz_t = sb("kz", [128, 4, 128 + T], BF16)
        kz_c = Buf("kz_c", kz_t)
        kz_n = Buf("kz_n", kz_t)
        vz_t = sb("vz", [128, 3, 4, 128], BF16)
        vz_c = Buf("vz_c", vz_t)
        vz_n = Buf("vz_n", vz_t)
        pe_tok = B("pe_tok", [128, 2, PLE], F32, dma=True)
        peT = B("peT", [128, 2, T], BF16)
        ya = B("ya", [128, 8, T], BF16)
        yb = B("yb", [128, 8, T], BF16)
        mixT = B("mixT", [128, 8, T], BF16)
        sga = B("sga", [128, 8, T], BF16)
        sgb = B("sgb", [128, 8, T], BF16)
        tA = B("tA", [128, 8, T], BF16)
        hm = B("hm", [128, NJ, T], BF16)
        vb = B("vb", [128, 2, D], BF16)
        S32_t = sb("S32", [128, 8, 128], F32)
        Sbf_t = sb("Sbf", [128, 8, 128], BF16)
        S32 = [Buf("S32_%d" % i, S32_t, R.new_dsem("d_S32_%d" % i)) for i in range(8)]
        Sbf = [Buf("Sbf_%d" % i, Sbf_t) for i in range(8)]
        carry = B("carry", [128, NJ, 4, 2], F32, dma=True)
        BT = B("BT", [128, 16, 256], BF16)
        BTs = B("BTs", [128, 16, 128], BF16)
        gbc = B("gbc", [128, 2, D], F32, dma=True)
        cst = B("cst", [128, CST_W], F32, dma=True)
        cbf = B("cbf", [128, 512], BF16)
        pv = B("pv", [128, PV_W], F32, dma=True)
        pv2 = B("pv2", [128, 56], F32)
        tab = B("tab", [128, 128], F32, dma=True)
        trs = B("trs", [128, 384], F32, dma=True)
        wkz = B("wkz", [128, 4, 8, 128], BF16, dma=True)
        kvout = [B("kvout%d" % i, [128, 256], F32, dma=True) for i in range(2)]
        wslot = [B("wslot%d" % i, [128, 4096], BF16, dma=True) for i in range(4)]
        ss = B("ss", [128, 8], F32)
        rstd = B("rstd", [128, 8], F32)
        f32t = [B("f32t%d" % i, [128, 256], F32) for i in range(6)]
        hs_f = [[B("hsf%d_%d" % (s_, i), [128, 256], F32) for i in range(8)] for s_ in range(2)]
        hs_b = [[B("hsb%d_%d" % (s_, i), [128, 256], BF16) for i in range(6)] for s_ in range(2)]
        hs_md = [B("hsmd%d" % s_, [128, 8], F32) for s_ in range(2)]
        hs_st = [B("hsst%d" % s_, [128, 3, 128], BF16) for s_ in range(2)]
        f32w = [B("f32w%d" % i, [128, 512], F32, dma=True) for i in range(3)]
        bf16t = [B("bf16t%d" % i, [128, 256], BF16) for i in range(6)]
        aext = [B("aext%d" % i, [128, T + 8], F32) for i in range(3)]
        kstT = [B("kstT%d" % i, [128, 4, 128], BF16) for i in range(2)]
        dec = [B("dec%d" % i, [128, 8], F32) for i in range(2)]
        rot = {"f": 0, "b": 0, "a": 0, "k": 0, "d": 0, "kv": 0, "w": 0}
        junk_ap = hn.t[:, 0, :]
        kcz_v = mixT.t[:, :, :].rearrange("p a (b n) -> p (a b) n", n=128).rearrange("p (i v) n -> p i v n", v=4)
        vcz_v = sga.t[:, :, :].rearrange("p a (b n) -> p (a b) n", n=128).rearrange("p (i v) n -> p i v n", v=4)

        def tw():
            rot["w"] = (rot["w"] + 1) % len(f32w)
            return f32w[rot["w"]]

        def tf():
            rot["f"] = (rot["f"] + 1) % len(f32t)
            return f32t[rot["f"]]

        def tb():
            rot["b"] = (rot["b"] + 1) % len(bf16t)
            return bf16t[rot["b"]]

        pbank = [Buf("pb%d" % i, pst("pb%d" % i, [128, 512], F32)) for i in range(8)]
        PA, PBk, PS0, PS1, PY, PD, PT0, PT1 = pbank
        pT = [PT0, PT1]
        rotp = {"i": 0, "set": list(pbank)}

        def set_rot(banks):
            rotp["set"] = list(banks)

        def pany():
            rotp["i"] = (rotp["i"] + 1) % len(rotp["set"])
            return rotp["set"][rotp["i"]]

        pmm = pany
        pscore = pany

        def ptr():
            bk = pany()
            return bk, bk.t[:, 0:256]

        def run_zip(gens):
            active = list(gens)
            while active:
                for g_ in list(active):
                    try:
                        next(g_)
                    except StopIteration:
                        active.remove(g_)

        ident = cbf.t[:, 0:128]
        ones_lo = cbf.t[:, 128:256]
        ones_hi = cbf.t[:, 256:384]
        ones_bf = cbf.t[:, 384:512]
        ident_f = cst.t[:, O_ID:O_ID + 128]
        eps_col = cst.t[:, O_EPS:O_EPS + 1]

        R.dma("sp", cst[:, :], cst_d, writes=[cst])
        R.dma("sp", pv[:, :], pv_d, writes=[pv])
        R.dma("sp", gbc[:, :, :], gbc_d, writes=[gbc])
        R.dma("sp", tab[:, :], tabp, writes=[tab])
        R.op("dve", lambda e: e.tensor_copy(out=cbf[:, 0:128], in_=cst[:, O_ID:O_ID + 128]), reads=[cst], writes=[cbf])
        R.op("dve", lambda e: e.tensor_copy(out=cbf[:, 128:384], in_=cst[:, O_ONELH:O_ONELH + 256]), reads=[cst], writes=[cbf])
        R.op("dve", lambda e: e.tensor_copy(out=cbf[:, 384:512], in_=cst[:, O_ONE:O_ONE + 128]), reads=[cst], writes=[cbf])
        t0 = tf()
        R.op("dve", lambda e: e.tensor_tensor(out=t0[:, 0:8], in0=pv[:, P_L0:P_L0 + 8], in1=pv[:, P_L1:P_L1 + 8], op=ALU.subtract),
             reads=[pv], writes=[t0])
        R.op("act", lambda e: e.activation(out=pv2[:, 0:8], in_=t0[:, 0:8], func=AF.Sigmoid), reads=[t0], writes=[pv2])
        R.op("act", lambda e: e.activation(out=pv2[:, 8:16], in_=t0[:, 0:8], func=AF.Sigmoid, scale=-1.0), reads=[t0], writes=[pv2])
        R.op("act", lambda e: e.activation(out=pv2[:, 16:24], in_=pv[:, P_SK:P_SK + 8], func=AF.Exp), reads=[pv], writes=[pv2])
        R.op("dve", lambda e: e.tensor_scalar(out=pv2[:, 24:32], in0=pv2[:, 8:16], scalar1=0.5, scalar2=None, op0=ALU.mult), reads=[pv2], writes=[pv2])
        R.op("dve", lambda e: e.tensor_scalar(out=pv2[:, 32:40], in0=pv2[:, 8:16], scalar1=-0.5, scalar2=None, op0=ALU.mult), reads=[pv2], writes=[pv2])
        R.op("dve", lambda e: e.tensor_tensor(out=pv2[:, 40:48], in0=pv2[:, 24:32], in1=pv2[:, 0:8], op=ALU.add), reads=[pv2], writes=[pv2])
        R.op("dve", lambda e: e.tensor_scalar(out=pv2[:, 48:49], in0=pv[:, P_GH:P_GH + 1], scalar1=0.5, scalar2=None, op0=ALU.mult), reads=[pv], writes=[pv2])
        R.op("dve", lambda e: e.tensor_scalar(out=pv2[:, 49:50], in0=cst[:, O_EPS:O_EPS + 1], scalar1=4.0, scalar2=None, op0=ALU.mult), reads=[cst], writes=[pv2])
        R.op("pe", lambda e: e.matmul(PA[:, 0:384], lhsT=tab[:, :], rhs=cst[:, O_OH:O_OH + 384], start=True, stop=True),
             reads=[tab, cst], writes=[PA], skip_self=True)
        R.op("dve", lambda e: e.tensor_copy(out=trs[:, :], in_=PA[:, 0:384]), reads=[PA], writes=[trs])
        d_trev = Buf("trev", trev_t, R.new_dsem("d_trev"))
        R.dma("sp", trev_t.ap()[:, :], trs[0:16, :], reads=[trs], writes=[d_trev])
        btfs = [xb_.t[:, :, :].rearrange("p a (h c) -> p (a h) c", c=256) for xb_ in xtbufs]
        for half in range(2):
            src = bass.AP(trev_t, half * 8 * 384, [[1, 128], [384, 8], [1, 256]])
            R.dma("sp", btfs[half][:, :, :], src, reads=[d_trev], writes=[xtbufs[half]])

        def emit_bias_tiles():
            for half in range(2):
                for h8 in range(8):
                    h = half * 8 + h8
                    pb_ = pany()
                    R.op("pe", lambda e, pb_=pb_, h8=h8, half=half: e.matmul(pb_[:, 0:256], lhsT=cst[:, O_J:O_J + 128], rhs=btfs[half][:, h8, :],
                                                                             start=True, stop=True),
                         reads=[cst, xtbufs[half]], writes=[pb_], skip_self=True)
                    R.op("dve", lambda e, h=h, pb_=pb_: e.scalar_tensor_tensor(out=BT[:, h, :], in0=pb_[:, 0:256], scalar=8.0,
                                                                              in1=cst[:, O_MA:O_MA + 256], op0=ALU.mult, op1=ALU.add),
                         reads=[pb_, cst], writes=[BT])
                    R.op("dve", lambda e, h=h, pb_=pb_: e.scalar_tensor_tensor(out=BTs[:, h, :], in0=pb_[:, 0:128], scalar=8.0,
                                                                              in1=cst[:, O_MAS:O_MAS + 128], op0=ALU.mult, op1=ALU.add),
                         reads=[pb_, cst], writes=[BTs])

        NSTG = 7
        stg = [Buf("stg%d" % i, None, R.new_dsem("d_stg%d" % i)) for i in range(NSTG)]
        cvb = [Buf("cvb%d" % i, None, R.new_dsem("d_cvb%d" % i)) for i in range(NSTG)]
        wbuf = {k: Buf("wb_" + k, wb_t[k]) for k in WSHAPES}
        xt4 = hn.t[:, :, :].rearrange("p a (b c) -> p (a b) c", c=512)
        stg_ap = [xt4[:, i, :] for i in range(4)] + [f32w[i].t[:, 0:512] for i in range(3)]
        cvb_ap = [wslot[i % 4].t[:, (i // 4) * 512:(i // 4) * 512 + 512] for i in range(NSTG)]
        ceng = ("dve", "pool")
        plist = []
        for wn, (K, N) in WSHAPES.items():
            for kc in range(K // 128):
                for c0 in range(0, N, 512):
                    plist.append((wn, kc, c0, min(512, N - c0)))
        pend = []
        for ci, (wn, kc, c0, cw) in enumerate(plist):
            s_ = ci % NSTG
            R.dma("sp", stg_ap[s_][:, 0:cw], wsrc[wn][kc * 128:(kc + 1) * 128, c0:c0 + cw], writes=[stg[s_]])
            dst = cvb_ap[s_][:, 0:cw]
            en = ceng[ci % 2]
            R.op(en, lambda e, s_=s_, cw=cw, dst=dst: e.tensor_copy(out=dst, in_=stg_ap[s_][:, 0:cw]), reads=[stg[s_]], writes=[cvb[s_]])
            pend.append((wn, kc, c0, cw, s_, dst))
            if len(pend) > 4 or ci == len(plist) - 1:
                todo = pend if ci == len(plist) - 1 else [pend.pop(0)]
                for (wn2, kc2, c02, cw2, s2, dst2) in todo:
                    R.dma("act", wb_t[wn2].ap()[kc2 * 128:(kc2 + 1) * 128, c02:c02 + cw2], dst2, reads=[cvb[s2]], dsem=cvb[s2].dsem)
        emit_bias_tiles()
        R.barrier()
        R.op("pool", lambda e: e.memset(wkz[:, :, :, :], 0.0), writes=[wkz])
        wbin = wb_t["w_in"].ap()
        for kv in range(2):
            for hi in range(2):
                v = kv * 2 + hi
                R.dma("sp", wkz[:, v, :, hi * 64:hi * 64 + 64],
                      wbin[:, C_KA + kv * 64:C_KA + kv * 64 + 64].rearrange("(k p) n -> p k n", p=128),
                      reads=[wbuf["w_in"]], writes=[wkz])
        R.op("pool", lambda e: e.memset(vz_t[:, :, :, :], 0.0), writes=[vz_c, vz_n])

        panels = []

        def P_(wn, kc0, nkc, c0, ncols):
            panels.append((wn, kc0, nkc, c0, ncols))
            return len(panels) - 1

        PI = {}
        PI["qa"] = [P_("w_in", 0, 8, C_QA + 512 * i, 512) for i in range(2)]
        PI["kv"] = P_("w_in", 0, 8, C_KA, 256)
        PI["ib"] = [P_("w_in", 0, 8, C_IB + 512 * i, 512) for i in range(2)]
        PI["hg"] = []
        for hg in range(2):
            PI["hg"].append((P_("w_in", 0, 8, C_QB + 512 * hg, 512), P_("w_in", 0, 8, C_FB + 512 * hg, 512),
                             P_("w_in", 0, 8, C_OB + 512 * hg, 512)))
        PI["ga"] = [P_("w_in", 0, 8, C_GA + 512 * i, 512) for i in range(2)]
        PI["gb"] = [P_("w_in", 0, 8, C_GB + 512 * i, 512) for i in range(2)]
        PI["bra"] = [P_("w_br_a", 0, 8, 512 * i, 512) for i in range(2)]
        PI["brb"] = [P_("w_br_b", 0, 8, 512 * i, 512) for i in range(2)]
        PI["out"] = [P_("w_out", 0, 8, 512 * i, 512) for i in range(2)]
        PI["up"] = []
        for jg in range(6):
            ncl = 512 if jg < 5 else 256
            PI["up"].append((P_("w_up", 0, 8, 512 * jg, ncl), P_("w_up", 0, 8, DFF + 512 * jg, ncl), ncl // 128))
        PI["down"] = [[P_("w_down", k0, nk, 512 * i, 512) for (k0, nk) in ((0, 8), (8, 8), (16, 6))] for i in range(2)]
        PI["ple"] = P_("w_ple", 0, 2, 0, 1024)
        PI["pg"] = [P_("w_ple_gate", 0, 8, 512 * i, 512) for i in range(2)]
        NPAN = len(panels)
        wstate = {"issued": 0}
        NT_TILES = (PB * NTILE + 1) if DBG["tiles"] is None else len(DBG["tiles"])
        TOTAL_PAN = NPAN * NT_TILES

        def issue_panel(g):
            wn, kc0, nkc, c0, ncols = panels[g % NPAN]
            s = wslot[g % 4]
            src = wb_t[wn].ap()[kc0 * 128:(kc0 + nkc) * 128, c0:c0 + ncols].rearrange("(k p) n -> p k n", p=128)
            dst = s.t[:, 0:nkc * ncols].rearrange("p (k n) -> p k n", n=ncols)
            R.dma("sp", dst, src, reads=[wbuf[wn]], writes=[s])

        def getp(tile_idx, pidx, lo=None):
            g = tile_idx * NPAN + pidx
            glo = g if lo is None else tile_idx * NPAN + lo
            while wstate["issued"] <= min(glo + 3, TOTAL_PAN - 1):
                issue_panel(wstate["issued"])
                wstate["issued"] += 1
            wn, kc0, nkc, c0, ncols = panels[pidx]
            s = wslot[g % 4]
            return s, s.t[:, 0:nkc * ncols].rearrange("p (k n) -> p k n", n=ncols)

        def evac_copy(out_ap, obuf, pbuf, in_ap, k):
            if k % 2 == 0:
                R.op("act", lambda e: e.copy(out=out_ap, in_=in_ap), reads=[pbuf], writes=[obuf])
            else:
                R.op("dve", lambda e: e.tensor_copy(out=out_ap, in_=in_ap), reads=[pbuf], writes=[obuf])

        def mm_group(pbuf, out_ap, pairs, extra_reads, start=True, stop=True):
            n = len(pairs)
            for i, (l, r) in enumerate(pairs):
                R.op("pe", lambda e, l=l, r=r, i=i: e.matmul(out_ap, lhsT=l, rhs=r, start=(start and i == 0),
                                                              stop=(stop and i == n - 1)),
                     reads=extra_reads, writes=[pbuf], inc=(i == n - 1), skip_self=True)

        def bcast_last(ap2, n):
            prs = [list(x) for x in ap2.ap]
            return bass.AP(ap2.tensor, ap2.offset, prs + [[0, n]])

        def bcast_mid(ap2, n):
            prs = [list(x) for x in ap2.ap]
            return bass.AP(ap2.tensor, ap2.offset, [prs[0], [0, n]] + prs[1:])

        def sq_accum(in_ap, srcbuf, col):
            R.op("act", lambda e: e.activation(out=junk_ap[:, 0:in_ap.shape[-1]], in_=in_ap, func=AF.Square,
                                               accum_out=ss[:, col:col + 1]), reads=[srcbuf], writes=[hn, ss])

        def rstd_from_ss(nb, stride):
            for blk in range(nb):
                R.op("act", lambda e, blk=blk: e.activation(out=rstd[:, blk:blk + 1], in_=ss[:, blk * stride:blk * stride + 1],
                                                            func=AF.Ln, scale=1.0 / D, bias=eps_col), reads=[ss, cst], writes=[rstd])
            R.op("act", lambda e: e.activation(out=rstd[:, 0:nb], in_=rstd[:, 0:nb], func=AF.Exp, scale=-0.5), reads=[rstd], writes=[rstd])

        def to_feature_major(xt, nb, gcol, dstbuf, with_norm):
            NT = nb * 128
            if with_norm:
                for blk in range(nb):
                    sq_accum(xt[:, blk, :], xt, blk)
                rstd_from_ss(nb, 1)
                for blk in range(nb):
                    R.op("dve", lambda e, blk=blk: e.tensor_scalar(out=hn[:, blk, :], in0=xt[:, blk, :], scalar1=rstd[:, blk:blk + 1],
                                                                    scalar2=None, op0=ALU.mult), reads=[xt, rstd], writes=[hn])
            srcb = hn if with_norm else xt
            for c in range(8):
                pb_, pap = ptr()
                for blk in range(nb):
                    R.op("pe", lambda e, c=c, blk=blk, pap=pap: e.transpose(out=pap[:, blk * 128:(blk + 1) * 128],
                                                                           in_=srcb[:, blk, c * 128:(c + 1) * 128], identity=ident_f),
                         reads=[srcb, cst], writes=[pb_], inc=(blk == nb - 1), skip_self=True)
                if with_norm:
                    R.op("dve", lambda e, c=c, pap=pap: e.tensor_scalar(out=dstbuf[:, c, 0:NT], in0=pap[:, 0:NT],
                                                                        scalar1=pv[:, gcol + c:gcol + c + 1], scalar2=None, op0=ALU.mult),
                         reads=[pb_, pv], writes=[dstbuf])
                else:
                    evac_copy(dstbuf[:, c, 0:NT], dstbuf, pb_, pap[:, 0:NT], c)

        def fm_proj(slotbuf, slot3, ncc, rhsbuf, NT, consume):
            for cc in range(ncc):
                pb_ = pmm()
                mm_group(pb_, pb_[:, 0:NT], [(slot3[:, kc, cc * 128:(cc + 1) * 128], rhsbuf[:, kc, 0:NT]) for kc in range(8)],
                         [slotbuf, rhsbuf])
                consume(cc, pb_, pb_[:, 0:NT])

        def load_x(xt, kind, b, ti):
            if kind == "p":
                R.dma("sp", xt[:, 0:2, :], xp[b, ti * T:ti * T + T, :].rearrange("(n p) d -> p n d", p=128), writes=[xt])
            else:
                R.dma("sp", xt[:, 0, :], xs, writes=[xt])

        def tile_body(ti_glob, kind, b, ti, nxt):
            xt = xtbufs[ti_glob % 2]
            if ti_glob == 0:
                load_x(xt, kind, b, ti)
            if nxt is not None:
                load_x(xtbufs[(ti_glob + 1) % 2], *nxt)
            nb = 2 if kind == "p" else 1
            NT = nb * 128
            first_t = (kind == "p" and ti == 0)
            last_t = (kind == "p" and ti == NTILE - 1)
            L = 64 if kind == "p" else 32
            nch = NT // L
            nseg = 1 if kind == "p" else 4
            Ls = NT // nseg

            if kind == "p":
                t0_ = ti * T
                R.dma("sp", pe_tok[:, 0:nb, :], pp[b, t0_:t0_ + NT, :].rearrange("(n p) d -> p n d", p=128), writes=[pe_tok])
            else:
                R.dma("sp", pe_tok[:, 0, :], psm, writes=[pe_tok])
                kctok, vctok = tw(), tw()
                kc3 = kctok.t[:, 0:512].rearrange("p (i f) -> p i f", f=128)
                vc3 = vctok.t[:, 0:512].rearrange("p (i f) -> p i f", f=128)
                R.dma("sp", kc3, ck.rearrange("b s f -> s b f"), writes=[kctok])
                R.dma("sp", vc3, cv.rearrange("b s f -> s b f"), writes=[vctok])
                R.dma("sp", carry[:, :, :, :], cfc, writes=[carry])
                R.op("pool", lambda e: e.memset(kcz_v, 0.0), writes=[mixT])
                R.op("pool", lambda e: e.memset(vcz_v, 0.0), writes=[sga])
                kcsw = tw()
                ks3 = kcsw.t[:, 0:512].rearrange("p (i f) -> p i f", f=128)
                R.op("dve", lambda e: e.tensor_copy(out=ks3[:, :, 0:64], in_=kc3[:, :, 64:128]), reads=[kctok], writes=[kcsw])
                R.op("dve", lambda e: e.tensor_copy(out=ks3[:, :, 64:128], in_=kc3[:, :, 0:64]), reads=[kctok], writes=[kcsw])
                for i in range(4):
                    for kv in range(2):
                        for hi in range(2):
                            v = kv * 2 + hi
                            R.op("pool", lambda e, i=i, kv=kv, hi=hi, v=v: e.tensor_copy(
                                out=vcz_v[:, i, v, hi * 64:hi * 64 + 64], in_=vc3[:, i, kv * 64:kv * 64 + 64]),
                                reads=[vctok], writes=[sga])
                for i in range(4):
                    pb_, pap = ptr()
                    R.op("pe", lambda e, i=i, pap=pap: e.transpose(out=pap[:, 0:128], in_=kc3[:, i, :], identity=ident_f),
                         reads=[kctok, cst], writes=[pb_], inc=False, skip_self=True)
                    R.op("pe", lambda e, i=i, pap=pap: e.transpose(out=pap[:, 128:256], in_=ks3[:, i, :], identity=ident_f),
                         reads=[kcsw, cst], writes=[pb_], inc=True, skip_self=True)
                    R.op("dve", lambda e, i=i, pap=pap: e.tensor_copy(out=kcz_v[0:64, i, 0, :], in_=pap[0:64, 0:128]), reads=[pb_], writes=[mixT])
                    R.op("dve", lambda e, i=i, pap=pap: e.tensor_copy(out=kcz_v[64:128, i, 3, :], in_=pap[64:128, 0:128]), reads=[pb_], writes=[mixT])
                    R.op("act", lambda e, i=i, pap=pap: e.copy(out=kcz_v[0:64, i, 2, :], in_=pap[0:64, 128:256]), reads=[pb_], writes=[mixT])
                    R.op("act", lambda e, i=i, pap=pap: e.copy(out=kcz_v[64:128, i, 1, :], in_=pap[64:128, 128:256]), reads=[pb_], writes=[mixT])
            if first_t:
                R.op("pool", lambda e: e.memset(S32_t[:, :, :], 0.0), writes=S32)
                R.op("pool", lambda e: e.memset(Sbf_t[:, :, :], 0.0), writes=Sbf)
                R.op("pool", lambda e: e.memset(carry[:, :, :, :], 0.0), writes=[carry])

            if DBG["stage"] <= 1:
                return
            to_feature_major(xt, nb, P_G1, hT, True)

            if DBG["stage"] <= 2:
                return
            for p in range(2):
                sbuf_, s3 = getp(ti_glob, PI["qa"][p])
                fm_proj(sbuf_, s3, 4, hT, NT,
                        lambda cc, pb_, pap, p=p: evac_copy(qT[:, 4 * p + cc, 0:NT], qT, pb_, pap, cc))
            sbuf_, s3 = getp(ti_glob, PI["kv"])
            for blk in range(nb):
                PX = pscore()
                mm_group(PX, PX[:, 0:256], [(hT[:, kc, blk * 128:(blk + 1) * 128], s3[:, kc, 0:256]) for kc in range(8)], [sbuf_, hT])
                for kv in range(2):
                    for hi in range(2):
                        v = kv * 2 + hi
                        eng = "dve"
                        if eng == "dve":
                            R.op("dve", lambda e, blk=blk, kv=kv, hi=hi, v=v, PX=PX: e.tensor_copy(
                                out=vz_t[:, 1 + blk, v, hi * 64:hi * 64 + 64], in_=PX[:, 128 + kv * 64:128 + kv * 64 + 64]),
                                reads=[PX], writes=[vz_n])
                        else:
                            R.op("act", lambda e, blk=blk, kv=kv, hi=hi, v=v: e.copy(
                                out=vz_t[:, 1 + blk, v, hi * 64:hi * 64 + 64], in_=PX[:, 128 + kv * 64:128 + kv * 64 + 64]),
                                reads=[PX], writes=[vz_n])
                want = (kind == "s") or (last_t and blk == nb - 1)
                if want:
                    rot["kv"] ^= 1
                    ko = kvout[rot["kv"]]
                    R.op("dve", lambda e, ko=ko, PX=PX: e.tensor_copy(out=ko[:, :], in_=PX[:, 0:256]), reads=[PX], writes=[ko])
                    if kind == "s":
                        R.dma("sp", o_sk, ko[:, 0:128], reads=[ko], is_output=True)
                        R.dma("sp", o_sv, ko[:, 128:256], reads=[ko], is_output=True)
                    else:
                        R.dma("sp", o_pk[b], ko[:, 0:128], reads=[ko], is_output=True)
                        R.dma("sp", o_pvv[b], ko[:, 128:256], reads=[ko], is_output=True)
            for v in range(4):
                pb_ = pmm()
                mm_group(pb_, pb_[:, 0:NT], [(wkz[:, v, kc, :], hT[:, kc, 0:NT]) for kc in range(8)], [wkz, hT])
                evac_copy(kz_t[:, v, 128:128 + NT], kz_n, pb_, pb_[:, 0:NT], v)

            if DBG["stage"] <= 3:
                return
            set_rot([PA, PBk, PS0, PS1])

            def attn_stream(pairs, YB, DB, sbanks=None):
                arot = [0]
                items = []
                for c in pairs:
                    kvh = c // 4
                    for hi in range(2):
                        h = 2 * c + hi
                        v = kvh * 2 + hi
                        if kind == "p":
                            blks = []
                            if not first_t:
                                blks.append((kz_c, kz_t[:, v, 0:128], 0, 128, BT, BT[:, h, 128:256], vz_c, vz_t[:, 0, v, :]))
                            blks.append((kz_n, kz_t[:, v, 128:256], 0, 256, BT, BT[:, h, 0:256], vz_n, vz_t[:, 1, v, :]))
                            blks.append((kz_n, kz_t[:, v, 256:384], 128, 256, BT, BT[:, h, 0:128], vz_n, vz_t[:, 2, v, :]))
                        else:
                            blks = [(kz_n, kz_t[:, v, 128:256], 0, 128, BTs, BTs[:, h, :], vz_n, vz_t[:, 1, v, :])]
                            for i in range(4):
                                blks.append((mixT, kcz_v[:, i, v, :], 32 * i, 32 * i + 32, BT, BT[:, h, 128:160], sga, vcz_v[:, i, v, :]))
                        for bi, blk_ in enumerate(blks):
                            items.append((c, hi, bi == 0 and hi == 0, (hi == 1 and bi == len(blks) - 1), blk_))

                def emit_scores(it):
                    c, hi, _, _, (kb, kap, q0, q1, bb, bap, _, _) = it
                    arot[0] ^= 1
                    ps_ = (PS0, PS1)[arot[0]] if sbanks is None else sbanks[arot[0]]
                    nq = q1 - q0
                    mm_group(ps_, ps_[:, 0:nq], [(kap, qT[:, c, q0:q1]), (ident, bap)], [kb, qT, bb, cbf])
                    return ps_

                written = {}

                def emit_rest(it, ps_):
                    c, hi, firstpair, lastpair, (kb, kap, q0, q1, bb, bap, vbuf, vap) = it
                    nq = q1 - q0
                    ptb = tb()
                    R.op("act", lambda e: e.activation(out=ptb[:, 0:nq], in_=ps_[:, 0:nq], func=AF.Exp, scale=0.125),
                         reads=[ps_], writes=[ptb])
                    if firstpair:
                        written.clear()
                    ol = ones_lo if hi == 0 else ones_hi
                    for s0 in range(q0, q1, 128):
                        s1 = min(s0 + 128, q1)
                        st_ = (len(written) == 0)
                        written[s0] = True
                        R.op("pe", lambda e, s0=s0, s1=s1, st_=st_: e.matmul(YB[:, s0:s1], lhsT=vap, rhs=ptb[:, s0 - q0:s1 - q0],
                                                                             start=st_, stop=False, skip_group_check=True),
                             reads=[vbuf, ptb], writes=[YB], inc=False, skip_self=True)
                        R.op("pe", lambda e, s0=s0, s1=s1, st_=st_: e.matmul(DB[:, s0:s1], lhsT=ol, rhs=ptb[:, s0 - q0:s1 - q0],
                                                                             start=st_, stop=False, skip_group_check=True),
                             reads=[cbf, ptb], writes=[DB], inc=True, skip_self=True)
                    if lastpair:
                        t1, t1y = tf(), tf()
                        R.op("act", lambda e: e.activation(out=t1[:, 0:NT], in_=DB[:, 0:NT], func=AF.Identity, bias=pv2[:, 16 + c:17 + c]),
                             reads=[DB, pv2], writes=[t1])
                        R.op("act", lambda e: e.copy(out=t1y[:, 0:NT], in_=YB[:, 0:NT]), reads=[YB], writes=[t1y])
                        R.op("dve", lambda e: e.reciprocal(out=t1[:, 0:NT], in_=t1[:, 0:NT]), reads=[t1], writes=[t1])
                        R.op("pool", lambda e: e.tensor_tensor(out=ya[:, c, 0:NT], in0=t1y[:, 0:NT], in1=t1[:, 0:NT], op=ALU.mult),
                             reads=[t1y, t1], writes=[ya])

                ps_cur = emit_scores(items[0])
                for i, it in enumerate(items):
                    ps_next = emit_scores(items[i + 1]) if i + 1 < len(items) else None
                    emit_rest(it, ps_cur)
                    ps_cur = ps_next
                    yield

            if DBG["stage"] <= 4:
                run_zip([attn_stream(range(0, 8, 2), PY, PD, (PA, PBk)), attn_stream(range(1, 8, 2), PT0, PT1, (PS0, PS1))])
                set_rot(pbank)
                return
            set_rot(pbank)
            for p in range(2):
                sbuf_, s3 = getp(ti_glob, PI["ib"][p])
                for blk in range(nb):
                    pb_ = pany()
                    mm_group(pb_, pb_[:, 0:512], [(hT[:, kc, blk * 128:(blk + 1) * 128], s3[:, kc, 0:512]) for kc in range(8)], [sbuf_, hT])
                    evac_copy(vb[:, blk, p * 512:(p + 1) * 512], vb, pb_, pb_[:, 0:512], blk)
            set_rot([PA, PBk])
            scm = cst.t[:, O_SCP:O_SCP + NT] if kind == "p" else cst.t[:, O_SCS:O_SCS + NT]
            hmk = cst.t[:, O_HMP:O_HMP + 128] if kind == "p" else cst.t[:, O_HMS:O_HMS + 128]
            mid = L // 2
            nsb = 128 // L

            def hgrn_head(h, hh, slot, YB, sq_b, sq3, sf_b, sf3, so_b, so3):
                fA, fB, fC, fD, fE, fF, fG, fH = hs_f[slot]
                qe, qst, ke, atm0, atm1, o2 = hs_b[slot]
                dc, kT = dec[slot], kstT[slot]
                wcols = slice(hh * 128, hh * 128 + 128)
                c3 = lambda ap_: ap_.rearrange("p (c l) -> p c l", l=L)
                pf = pany()
                mm_group(pf, pf[:, 0:NT], [(sf3[:, kc, wcols], hT[:, kc, 0:NT]) for kc in range(8)], [sf_b, hT])
                R.op("act", lambda e: e.activation(out=fB[:, 0:NT], in_=pf[:, 0:NT], func=AF.Tanh, scale=0.5), reads=[pf], writes=[fB])
                yield
                pq = pany()
                mm_group(pq, pq[:, 0:NT], [(sq3[:, kc, wcols], hT[:, kc, 0:NT]) for kc in range(8)], [sq_b, hT])
                R.op("act", lambda e: e.activation(out=fF[:, 0:NT], in_=pq[:, 0:NT], func=AF.Tanh, scale=0.5), reads=[pq], writes=[fF])
                R.op("dve", lambda e: e.scalar_tensor_tensor(out=fF[:, 0:NT], in0=fF[:, 0:NT], scalar=1.0, in1=pq[:, 0:NT],
                                                             op0=ALU.add, op1=ALU.mult), reads=[fF, pq], writes=[fF])
                yield
                po = pany()
                mm_group(po, po[:, 0:NT], [(so3[:, kc, wcols], hT[:, kc, 0:NT]) for kc in range(8)], [so_b, hT])
                R.op("act", lambda e: e.activation(out=fH[:, 0:NT], in_=po[:, 0:NT], func=AF.Tanh, scale=0.5), reads=[po], writes=[fH])
                R.op("dve", lambda e: e.scalar_tensor_tensor(out=fH[:, 0:NT], in0=fH[:, 0:NT], scalar=1.0, in1=po[:, 0:NT],
                                                             op0=ALU.add, op1=ALU.mult), reads=[fH, po], writes=[fH])
                yield
                R.op("act", lambda e: e.activation(out=fA[:, 0:NT], in_=fB[:, 0:NT], func=AF.Ln, scale=pv2[:, 24 + h:25 + h],
                                                   bias=pv2[:, 40 + h:41 + h]), reads=[fB, pv2], writes=[fA])
                yield
                R.op("dve", lambda e: e.tensor_tensor_scan(out=fC[:, 0:NT], data0=scm, data1=fA[:, 0:NT], initial=0.0,
                                                           op0=ALU.mult, op1=ALU.add), reads=[cst, fA], writes=[fC])
                yield
                R.op("act", lambda e: e.activation(out=fD[:, 0:NT], in_=fC[:, 0:NT], func=AF.Exp), reads=[fC], writes=[fD])
                R.op("dve", lambda e: e.tensor_tensor(out=c3(fG[:, 0:NT]), in0=c3(fC[:, 0:NT]), in1=bcast_last(c3(fC[:, 0:NT])[:, :, mid], L),
                                                      op=ALU.subtract), reads=[fC], writes=[fG])
                yield
                R.op("act", lambda e: e.activation(out=fE[:, 0:NT], in_=fG[:, 0:NT], func=AF.Exp), reads=[fG], writes=[fE])
                R.op("act", lambda e: e.activation(out=fC[:, 0:NT], in_=fG[:, 0:NT], func=AF.Exp, scale=-1.0), reads=[fG], writes=[fC])
                R.op("act", lambda e: e.activation(out=fA[:, 0:NT], in_=fB[:, 0:NT], func=AF.Identity, scale=pv2[:, 32 + h:33 + h],
                                                   bias=pv2[:, 24 + h:25 + h]), reads=[fB, pv2], writes=[fA])
                yield
                R.op("dve", lambda e: e.tensor_tensor(out=qe[:, 0:NT], in0=fF[:, 0:NT], in1=fE[:, 0:NT], op=ALU.mult), reads=[fF, fE], writes=[qe])
                R.op("pool", lambda e: e.tensor_tensor(out=qst[:, 0:NT], in0=fF[:, 0:NT], in1=fD[:, 0:NT], op=ALU.mult), reads=[fF, fD], writes=[qst])
                R.op("dve", lambda e: e.tensor_tensor(out=ke[:, 0:NT], in0=fA[:, 0:NT], in1=fC[:, 0:NT], op=ALU.mult), reads=[fA, fC], writes=[ke])
                R.op("dve", lambda e: e.tensor_tensor(out=c3(fG[:, 0:NT]), in0=c3(ke[:, 0:NT]), in1=bcast_last(c3(fE[:, 0:NT])[:, :, L - 1], L),
                                                      op=ALU.mult), reads=[ke, fE], writes=[fG])
                yield
                pb_, pap = ptr()
                for blk in range(nb):
                    R.op("pe", lambda e, blk=blk: e.transpose(out=pap[:, blk * 128:(blk + 1) * 128], in_=fG[:, blk * 128:(blk + 1) * 128],
                                                              identity=ident_f), reads=[fG, cst], writes=[pb_], inc=(blk == nb - 1), skip_self=True)
                sg0 = O_SEG + (0 if nsb == 2 else 2)
                for blk in range(nb):
                    R.op("dve", lambda e, blk=blk: e.tensor_tensor(
                        out=kT[:, blk * nsb:(blk + 1) * nsb, :], in0=bcast_mid(pap[:, blk * 128:(blk + 1) * 128], nsb),
                        in1=bcast_last(cst[:, sg0:sg0 + nsb], 128), op=ALU.mult), reads=[pb_, cst], writes=[kT])
                yield
                atms = [atm0, atm1]
                for blk in range(nb):
                    ps_ = pany()
                    mm_group(ps_, ps_[:, 0:128], [(ke[:, blk * 128:(blk + 1) * 128], qe[:, blk * 128:(blk + 1) * 128])], [ke, qe])
                    atm = atms[blk]
                    R.op("dve", lambda e, ps_=ps_, atm=atm: e.tensor_tensor(out=atm[:, 0:128], in0=ps_[:, 0:128], in1=hmk, op=ALU.mult),
                         reads=[ps_, cst], writes=[atm])
                    yield
                for blk in range(nb):
                    atm = atms[blk]
                    R.op("pe", lambda e, blk=blk, atm=atm: e.matmul(YB[:, blk * 128:(blk + 1) * 128], lhsT=vb[:, blk, h * 128:(h + 1) * 128],
                                                                    rhs=atm[:, 0:128], start=(blk == 0), stop=False, skip_group_check=True),
                         reads=[vb, atm], writes=[YB], inc=True, skip_self=True)
                yield
                if kind == "p":
                    stt = hs_st[slot]
                    R.op("pe", lambda e: e.matmul(YB[:, 0:L], lhsT=Sbf_t[:, h, :], rhs=qst[:, 0:L],
                                                  start=False, stop=True, skip_group_check=True),
                         reads=[Sbf[h], qst], writes=[YB], inc=False, skip_self=True)
                    PU = pany()
                    for cix in range(nch):
                        blk = (cix * L) // 128
                        s_ = (cix * L % 128) // L
                        R.op("pe", lambda e, cix=cix, blk=blk, s_=s_: e.matmul(PU[:, cix * 128:(cix + 1) * 128], lhsT=kT[:, blk * nsb + s_, :],
                                                                               rhs=vb[:, blk, h * 128:(h + 1) * 128], start=True, stop=True),
                             reads=[kT, vb], writes=[PU], inc=(cix == nch - 1), skip_self=True)
                    scr = [(fB, fB[:, 0:128]), (fC, fC[:, 0:128]), (fE, fE[:, 0:128]), (S32[h], S32_t[:, h, :])]
                    prev_b, prev_ap = S32[h], S32_t[:, h, :]
                    for cix in range(nch):
                        last = (cix == nch - 1)
                        ob, oap = scr[3] if last else scr[cix]
                        R.op("dve", lambda e, cix=cix, oap=oap, prev_ap=prev_ap: e.scalar_tensor_tensor(
                            out=oap, in0=prev_ap, scalar=fD[:, cix * L + L - 1:cix * L + L], in1=PU[:, cix * 128:(cix + 1) * 128],
                            op0=ALU.mult, op1=ALU.add), reads=[prev_b, fD, PU], writes=[ob])
                        dstb = Sbf[h] if last else stt
                        dst_ap = Sbf_t[:, h, :] if last else stt[:, cix, :]
                        if last:
                            R.op("pool", lambda e, oap=oap, dst_ap=dst_ap: e.tensor_copy(out=dst_ap, in_=oap), reads=[ob], writes=[dstb])
                        else:
                            R.op("act", lambda e, oap=oap, dst_ap=dst_ap: e.copy(out=dst_ap, in_=oap), reads=[ob], writes=[dstb])
                        prev_b, prev_ap = ob, oap
                    yield
                    for cix in range(1, nch):
                        cols = slice(cix * L, (cix + 1) * L)
                        R.op("pe", lambda e, cix=cix, cols=cols: e.matmul(YB[:, cols], lhsT=stt[:, cix - 1, :], rhs=qst[:, cols],
                                                                          start=False, stop=True, skip_group_check=True),
                             reads=[stt, qst], writes=[YB], inc=(cix == nch - 1), skip_self=True)
                    yield
                else:
                    for cix in range(nch):
                        blk = (cix * L) // 128
                        s_ = (cix * L % 128) // L
                        cols = slice(cix * L, (cix + 1) * L)
                        R.dma("sp", S32_t[:, h, :], sh[cix, h], writes=[S32[h]])
                        R.op("act", lambda e: e.copy(out=Sbf_t[:, h, :], in_=S32_t[:, h, :]), reads=[S32[h]], writes=[Sbf[h]])
                        R.op("pe", lambda e, cols=cols: e.matmul(YB[:, cols], lhsT=Sbf_t[:, h, :], rhs=qst[:, cols],
                                                                 start=False, stop=True, skip_group_check=True),
                             reads=[Sbf[h], qst], writes=[YB], inc=True, skip_self=True)
                        PX = pany()
                        mm_group(PX, PX[:, 0:128], [(kT[:, blk * nsb + s_, :], vb[:, blk, h * 128:(h + 1) * 128])], [kT, vb])
                        R.op("dve", lambda e, cix=cix, PX=PX: e.scalar_tensor_tensor(out=S32_t[:, h, :], in0=S32_t[:, h, :],
                                                                                    scalar=fD[:, cix * L + L - 1:cix * L + L], in1=PX[:, 0:128],
                                                                                    op0=ALU.mult, op1=ALU.add),
                             reads=[S32[h], fD, PX], writes=[S32[h]])
                        R.dma("sp", o_ss[cix, h], S32_t[:, h, :], reads=[S32[h]], is_output=True)
                        yield
                R.op("act", lambda e: e.activation(out=o2[:, 0:NT], in_=YB[:, 0:NT], func=AF.Square), reads=[YB], writes=[o2])
                pd_ = pany()
                mm_group(pd_, pd_[:, 0:NT], [(ones_bf, o2[:, 0:NT])], [cbf, o2])
                R.op("act", lambda e: e.activation(out=fA[:, 0:NT], in_=pd_[:, 0:NT], func=AF.Ln, scale=1.0 / 128, bias=pv2[:, 49:50]),
                     reads=[pd_, pv2], writes=[fA])
                R.op("act", lambda e: e.activation(out=fA[:, 0:NT], in_=fA[:, 0:NT], func=AF.Exp, scale=-0.5), reads=[fA], writes=[fA])
                yield
                R.op("dve", lambda e: e.scalar_tensor_tensor(out=fE[:, 0:NT], in0=YB[:, 0:NT], scalar=pv2[:, 48:49], in1=fA[:, 0:NT],
                                                             op0=ALU.mult, op1=ALU.mult), reads=[YB, pv2, fA], writes=[fE])
                R.op("dve", lambda e: e.tensor_tensor(out=yb[:, h, 0:NT], in0=fE[:, 0:NT], in1=fH[:, 0:NT], op=ALU.mult),
                     reads=[fE, fH], writes=[yb])
                yield

            def zip_gen(gens):
                active = list(gens)
                while active:
                    for g_ in list(active):
                        try:
                            next(g_)
                        except StopIteration:
                            active.remove(g_)
                    yield

            def hgrn_all():
                for hg in range(2):
                    q_i, f_i, o_i = PI["hg"][hg]
                    sq_b, sq3 = getp(ti_glob, q_i, lo=q_i)
                    sf_b, sf3 = getp(ti_glob, f_i, lo=q_i)
                    so_b, so3 = getp(ti_glob, o_i, lo=q_i)
                    for pi_ in range(2):
                        yield from zip_gen([hgrn_head(4 * hg + 2 * pi_ + s_, 2 * pi_ + s_, s_, (PY, PD)[s_], sq_b, sq3, sf_b, sf3, so_b, so3)
                                            for s_ in range(2)])

            run_zip([attn_stream(range(8), PT0, PT1), hgrn_all()])
            if kind == "p" and not last_t:
                R.op("pool", lambda e: e.tensor_copy(out=kz_t[:, :, 0:128], in_=kz_t[:, :, T:T + 128]), reads=[kz_n], writes=[kz_c])
                R.op("pool", lambda e: e.tensor_copy(out=vz_t[:, 0, :, :], in_=vz_t[:, 2, :, :]), reads=[vz_n], writes=[vz_c])
            if last_t:
                R.dma("sp", o_ps[b].rearrange("h d v -> d h v"), S32_t[:, :, :], reads=S32, dsem=S32[0].dsem, is_output=True)
            set_rot(pbank)


            if DBG["stage"] <= 5:
                return
            for nm, dst in (("ga", sga), ("gb", sgb)):
                for p in range(2):
                    sbuf_, s3 = getp(ti_glob, PI[nm][p])
                    fm_proj(sbuf_, s3, 4, hT, NT, lambda cc, pb_, pap, p=p, dst=dst: R.op(
                        "act", lambda e: e.activation(out=dst[:, 4 * p + cc, 0:NT], in_=pap, func=AF.Sigmoid), reads=[pb_], writes=[dst]))
            for p in range(2):
                sbuf_, s3 = getp(ti_glob, PI["bra"][p])
                fm_proj(sbuf_, s3, 4, ya, NT, lambda cc, pb_, pap, p=p: R.op(
                    "dve", lambda e: e.tensor_tensor(out=tA[:, 4 * p + cc, 0:NT], in0=pap, in1=sga[:, 4 * p + cc, 0:NT], op=ALU.mult),
                    reads=[pb_, sga], writes=[tA]))
            for p in range(2):
                sbuf_, s3 = getp(ti_glob, PI["brb"][p])

                def cons(cc, pb_, pap, p=p):
                    t3 = tf()
                    R.op("dve", lambda e: e.tensor_tensor(out=t3[:, 0:NT], in0=pap, in1=sgb[:, 4 * p + cc, 0:NT], op=ALU.mult),
                         reads=[pb_, sgb], writes=[t3])
                    R.op("pool", lambda e: e.tensor_tensor(out=mixT[:, 4 * p + cc, 0:NT], in0=t3[:, 0:NT], in1=tA[:, 4 * p + cc, 0:NT], op=ALU.add),
                         reads=[t3, tA], writes=[mixT])
                fm_proj(sbuf_, s3, 4, yb, NT, cons)

            def tok_proj_norm(pkey, lhs_buf, kgroups, gidx):
                banks = {(0, 0): PA, (0, 1): PBk, (1, 0): PS0, (1, 1): PS1}
                for p in range(2):
                    for gi, (k0, nk) in enumerate(kgroups):
                        sbuf_, s3 = getp(ti_glob, PI[pkey][p][gi] if isinstance(PI[pkey][p], list) else PI[pkey][p])
                        for blk in range(nb):
                            bk = banks[(blk, p)]
                            mm_group(bk, bk[:, 0:512], [(lhs_buf[:, k0 + kc, blk * 128:(blk + 1) * 128], s3[:, kc, 0:512]) for kc in range(nk)],
                                     [sbuf_, lhs_buf], start=(gi == 0), stop=(gi == len(kgroups) - 1))
                    for blk in range(nb):
                        sq_accum(banks[(blk, p)][:, 0:512], banks[(blk, p)], blk * 2 + p)
                for blk in range(nb):
                    R.op("dve", lambda e, blk=blk: e.tensor_tensor(out=ss[:, blk * 2:blk * 2 + 1], in0=ss[:, blk * 2:blk * 2 + 1],
                                                                    in1=ss[:, blk * 2 + 1:blk * 2 + 2], op=ALU.add),
                         reads=[ss], writes=[ss])
                rstd_from_ss(nb, 2)
                for blk in range(nb):
                    for p in range(2):
                        bk = banks[(blk, p)]
                        t4 = tw()
                        R.op("dve", lambda e, bk=bk, blk=blk, p=p, t4=t4: e.scalar_tensor_tensor(
                            out=t4[:, 0:512], in0=bk[:, 0:512], scalar=rstd[:, blk:blk + 1], in1=gbc[:, gidx, p * 512:(p + 1) * 512],
                            op0=ALU.mult, op1=ALU.mult), reads=[bk, rstd, gbc], writes=[t4])
                        eng = "pool" if p == 0 else "dve"
                        R.op(eng, lambda e, blk=blk, p=p, t4=t4: e.tensor_tensor(out=xt[:, blk, p * 512:(p + 1) * 512],
                                                                                in0=xt[:, blk, p * 512:(p + 1) * 512], in1=t4[:, 0:512], op=ALU.add),
                             reads=[xt, t4], writes=[xt])

            tok_proj_norm("out", mixT, [(0, 8)], 0)

            if DBG["stage"] <= 6:
                return
            to_feature_major(xt, nb, P_G2, hT, True)
            ffn_pend = []
            for jg in range(6):
                pa_i, pu_i, ncc = PI["up"][jg]
                sa_b, sa3 = getp(ti_glob, pa_i, lo=pa_i)
                su_b, su3 = getp(ti_glob, pu_i, lo=pa_i)
                for cc in range(ncc):
                    j = 4 * jg + cc
                    wc = slice(cc * 128, cc * 128 + 128)
                    pa_ = (PA, PS0, PY, PT0)[j % 4]
                    pu_ = (PBk, PS1, PD, PT1)[j % 4]
                    mm_group(pa_, pa_[:, 0:NT], [(sa3[:, kc, wc], hT[:, kc, 0:NT]) for kc in range(8)], [sa_b, hT])
                    mm_group(pu_, pu_[:, 0:NT], [(su3[:, kc, wc], hT[:, kc, 0:NT]) for kc in range(8)], [su_b, hT])
                    rot["a"] = (rot["a"] + 1) % len(aext)
                    ax = aext[rot["a"]]
                    ax3 = ax.t[:, 0:nseg * (Ls + 2)].rearrange("p (s l) -> p s l", l=Ls + 2)
                    pa3 = pa_.t[:, 0:NT].rearrange("p (s l) -> p s l", l=Ls)
                    R.op("pool", lambda e, ax3=ax3, j=j: e.tensor_copy(out=ax3[:, :, 0:2], in_=carry[:, j, 0:nseg, :]), reads=[carry], writes=[ax])
                    R.op("act", lambda e, ax3=ax3, pa3=pa3: e.copy(out=ax3[:, :, 2:Ls + 2], in_=pa3), reads=[pa_], writes=[ax])
                    R.op("pool", lambda e, ax3=ax3, j=j: e.tensor_copy(out=carry[:, j, 0:nseg, :], in_=ax3[:, :, Ls:Ls + 2]), reads=[ax], writes=[carry])
                    t5 = tf()
                    wcol = P_WC + 3 * j
                    t53 = t5.t[:, 0:NT].rearrange("p (s l) -> p s l", l=Ls)
                    R.op("pool", lambda e, ax3=ax3, t53=t53, wcol=wcol, j=j: e.tensor_scalar(
                        out=t53, in0=ax3[:, :, 2:Ls + 2], scalar1=pv[:, wcol + 2:wcol + 3], scalar2=pv[:, P_BC + j:P_BC + j + 1],
                        op0=ALU.mult, op1=ALU.add), reads=[ax, pv], writes=[t5])
                    R.op("dve", lambda e, ax3=ax3, t53=t53, wcol=wcol: e.scalar_tensor_tensor(
                        out=t53, in0=ax3[:, :, 1:Ls + 1], scalar=pv[:, wcol + 1:wcol + 2], in1=t53, op0=ALU.mult, op1=ALU.add),
                        reads=[ax, pv, t5], writes=[t5])
                    R.op("dve", lambda e, ax3=ax3, t53=t53, wcol=wcol: e.scalar_tensor_tensor(
                        out=t53, in0=ax3[:, :, 0:Ls], scalar=pv[:, wcol:wcol + 1], in1=t53, op0=ALU.mult, op1=ALU.add),
                        reads=[ax, pv, t5], writes=[t5])
                    if ffn_pend:
                        ffn_pend.pop(0)()

                    def tail(t5=t5, pu_=pu_, j=j):
                        R.op("act", lambda e: e.activation(out=t5[:, 0:NT], in_=t5[:, 0:NT], func=AF.Gelu_apprx_tanh), reads=[t5], writes=[t5])
                        R.op("dve", lambda e: e.tensor_tensor(out=hm[:, j, 0:NT], in0=pu_[:, 0:NT], in1=t5[:, 0:NT], op=ALU.mult),
                             reads=[pu_, t5], writes=[hm])
                    ffn_pend.append(tail)
            while ffn_pend:
                ffn_pend.pop(0)()
            if kind == "s":
                R.dma("sp", o_sc, carry[:, :, :, :], reads=[carry], is_output=True)
            elif last_t:
                R.dma("sp", o_pc[:, :, b, :], carry[:, :, 0, :], reads=[carry], is_output=True)
            tok_proj_norm("down", hm, [(0, 8), (8, 8), (16, 6)], 1)

            if DBG["stage"] <= 7:
                return
            to_feature_major(xt, nb, 0, hT, False)
            for c2 in range(2):
                pb_, pap = ptr()
                for blk in range(nb):
                    R.op("pe", lambda e, c2=c2, blk=blk, pap=pap: e.transpose(out=pap[:, blk * 128:(blk + 1) * 128],
                                                                             in_=pe_tok[:, blk, c2 * 128:(c2 + 1) * 128], identity=ident_f),
                         reads=[pe_tok, cst], writes=[pb_], inc=(blk == nb - 1), skip_self=True)
                evac_copy(peT[:, c2, 0:NT], peT, pb_, pap[:, 0:NT], c2)
            sple_b, sple3 = getp(ti_glob, PI["ple"])
            for p in range(2):
                spg_b, spg3 = getp(ti_glob, PI["pg"][p], lo=PI["ple"])
                for blk in range(nb):
                    bg = PA if blk == 0 else PS0
                    bp = PBk if blk == 0 else PS1
                    mm_group(bg, bg[:, 0:512], [(hT[:, kc, blk * 128:(blk + 1) * 128], spg3[:, kc, 0:512]) for kc in range(8)], [spg_b, hT])
                    mm_group(bp, bp[:, 0:512], [(peT[:, k2, blk * 128:(blk + 1) * 128], sple3[:, k2, p * 512:(p + 1) * 512]) for k2 in range(2)],
                             [sple_b, peT])
                    t8, t9 = tw(), tw()
                    R.op("act", lambda e, bg=bg, t8=t8: e.activation(out=t8[:, 0:512], in_=bg[:, 0:512], func=AF.Sigmoid), reads=[bg], writes=[t8])
                    R.op("dve", lambda e, bp=bp, t8=t8, t9=t9: e.tensor_tensor(out=t9[:, 0:512], in0=bp[:, 0:512], in1=t8[:, 0:512], op=ALU.mult),
                         reads=[bp, t8], writes=[t9])
                    R.op("pool", lambda e, blk=blk, p=p, t9=t9: e.tensor_tensor(out=xt[:, blk, p * 512:(p + 1) * 512],
                                                                               in0=xt[:, blk, p * 512:(p + 1) * 512], in1=t9[:, 0:512], op=ALU.add),
                         reads=[xt, t9], writes=[xt])
            if kind == "p":
                t0_ = ti * T
                R.dma("sp", o_yp[b, t0_:t0_ + NT, :].rearrange("(n p) d -> p n d", p=128), xt[:, 0:nb, :], reads=[xt], is_output=True)
            else:
                R.dma("sp", o_ys, xt[:, 0, :], reads=[xt], is_output=True)

        tg = 0
        tl = [("p", b, ti) for b in range(PB) for ti in range(NTILE)] + [("s", 0, 0)]
        if DBG["tiles"] is not None:
            tl = DBG["tiles"]
        for ix, (kd, b, ti) in enumerate(tl):
            tile_body(tg, kd, b, ti, tl[ix + 1] if ix + 1 < len(tl) else None)
            tg += 1

        R.finish("sp")
        R.replay(nc, st)
        build_program.stats = {e: len(R.q[e]) for e in ENGS}
        build_program.sbuf_bytes = sbtot[0]
    return nc


_CACHE = {}


def kernel(x_prompt, x_sample, cache_win_k, cache_win_v, state_hgrn, cache_ffn_conv, p_prompt, p_sample,
           rel_bias_table, lb_logits, g_pre_mix, w_in, attn_sinks, g_hgrn_out, w_br_a, w_br_b, w_out, g_post_mix,
           g_pre_ffn, w_up, w_conv, b_conv, w_down, g_post_ffn, w_ple, w_ple_gate):
    f = lambda a: np.ascontiguousarray(np.asarray(a, dtype=np.float32))
    if "nc" not in _CACHE:
        _CACHE["nc"] = build_program()
    nc = _CACHE["nc"]
    ncores = _CACHE.get("ncores", NCORE)
    cst = _make_cst()
    tabp = np.zeros((128, 128), np.float32)
    tabp[0:32, 0:16] = f(rel_bias_table)
    pv = np.zeros((128, PV_W), np.float32)
    pv[:, P_G1:P_G1 + 8] = f(g_pre_mix)[0].reshape(8, 128).T
    pv[:, P_G2:P_G2 + 8] = f(g_pre_ffn)[0].reshape(8, 128).T
    pv[:, P_L0:P_L0 + 8] = f(lb_logits)[0].reshape(8, 128).T
    pv[:, P_L1:P_L1 + 8] = f(lb_logits)[1].reshape(8, 128).T
    pv[:, P_GH] = f(g_hgrn_out)[0]
    pv[:, P_WC:P_WC + 66] = f(w_conv)[0].reshape(3, NJ, 128).transpose(2, 1, 0).reshape(128, 66)
    pv[:, P_BC:P_BC + NJ] = f(b_conv)[0].reshape(NJ, 128).T
    sk = f(attn_sinks)[0]
    for c in range(8):
        pv[0:64, P_SK + c] = sk[2 * c]
        pv[64:128, P_SK + c] = sk[2 * c + 1]
    gbc = np.ascontiguousarray(np.broadcast_to(np.stack([f(g_post_mix)[0], f(g_post_ffn)[0]])[None], (128, 2, D)))
    shared = {"table_pad": tabp, "cst": cst, "pv": pv, "gbc": gbc, "w_in": f(w_in)[0], "w_br_a": f(w_br_a)[0],
              "w_br_b": f(w_br_b)[0], "w_out": f(w_out)[0], "w_up": f(w_up)[0], "w_down": f(w_down)[0],
              "w_ple": f(w_ple)[0], "w_ple_gate": f(w_ple_gate)[0]}
    xpf, xsf = f(x_prompt), f(x_sample)
    ckf, cvf = f(cache_win_k)[0].reshape(32, 128, 128), f(cache_win_v)[0].reshape(32, 128, 128)
    shf, cff = f(state_hgrn)[0], f(cache_ffn_conv)[0]
    ppf, psf = f(p_prompt)[0], f(p_sample)[0]
    in_maps = []
    for i in range(ncores):
        sl = slice(PB * i, PB * i + PB)
        m = dict(shared)
        m["x_prompt"] = xpf[sl]
        m["x_sample"] = np.ascontiguousarray(xsf[sl].reshape(PB * DSEQ, D))
        m["cache_k"] = ckf[sl]
        m["cache_v"] = cvf[sl]
        m["state_hgrn"] = shf[sl]
        m["cache_conv"] = np.ascontiguousarray(cff[sl].reshape(PB, 2, NJ, 128).transpose(3, 2, 0, 1))
        m["p_prompt"] = ppf[sl]
        m["p_sample"] = np.ascontiguousarray(psf[sl].reshape(PB * DSEQ, PLE))
        in_maps.append(m)
    res = run_bass_kernel_spmd(nc, in_maps, core_ids=list(range(ncores)))
    rs = res.results
    g = lambda k: [np.asarray(r[k], dtype=np.float32) for r in rs]
    y_p = np.concatenate(g("y_prompt"), 0)
    y_s = np.concatenate([a.reshape(PB, DSEQ, D) for a in g("y_sample")], 0)
    pk = np.concatenate(g("o_pk"), 0).reshape(1, -1, 128, 2, 64)
    pvv = np.concatenate(g("o_pv"), 0).reshape(1, -1, 128, 2, 64)
    ps = np.concatenate(g("o_ps"), 0)[None]
    cvt = lambda a: a.transpose(2, 3, 1, 0).reshape(PB, 2, DFF)
    pc = np.concatenate([cvt(a) for a in g("o_pc")], 0)[None]
    skk = np.concatenate([a.reshape(PB, DSEQ, 2, 64) for a in g("o_sk")], 0)[None]
    svv = np.concatenate([a.reshape(PB, DSEQ, 2, 64) for a in g("o_sv")], 0)[None]
    sss = np.concatenate(g("o_ss"), 0)[None]
    sc = np.concatenate([cvt(a) for a in g("o_sc")], 0)[None]
    return (y_p, y_s, pk, pvv, ps, pc, skk, svv, sss, sc)
```
